# Optimizing a Trainium2 kernel written in Bass

```python
import jax, jax.numpy as jnp
from jax import lax
import numpy as np

D_MODEL = 1024
BATCH = 8
SEQ = 2048
DEPTH = 2
DEC_BATCH = 32
DEC_SEQ = 8
PAST_LEN = 8192
PAGE_SIZE = 128

HEAD_DIM = 64
POOL_WIDTH = D_MODEL // 4
POOL_WINDOWS = (2, 4, 8, 16)
POOL_GROUP = POOL_WIDTH // len(POOL_WINDOWS)
POOL_HIST = max(POOL_WINDOWS) - 1
ATTN_WIDTH = 3 * D_MODEL // 8
DIL_PAIRS = ((128, 1), (512, 4), (2048, 16))
N_DIL = len(DIL_PAIRS)
ATTN_HEADS = ATTN_WIDTH // HEAD_DIM
HEADS_PER_DIL = ATTN_HEADS // N_DIL
ATTN_OUT = HEADS_PER_DIL * HEAD_DIM
CONV_WIDTH = D_MODEL - POOL_WIDTH - ATTN_WIDTH
CONV_K = 3
ROPE_DIM = HEAD_DIM // 4
ROPE_THETA = 500000.0
D_FF = 4 * D_MODEL
IN_COLS = POOL_WIDTH + 3 * ATTN_WIDTH + 3 * CONV_WIDTH
MIX_OUT = POOL_WIDTH + ATTN_OUT + CONV_WIDTH
EPS = 1e-6
NEG_INF = -1e30

kernel_name = "hybrid_pool_dilattn_shortconv_decode_step"


def rmsnorm(x, g):
    xf = x.astype(jnp.float32)
    y = xf * lax.rsqrt(jnp.mean(xf * xf, axis=-1, keepdims=True) + EPS)
    return (y * g.astype(jnp.float32)).astype(x.dtype)


def partial_rope(x, pos):
    half = ROPE_DIM // 2
    inv = jnp.power(jnp.float32(ROPE_THETA), -jnp.arange(half, dtype=jnp.float32) / half)
    ang = pos.astype(jnp.float32)[:, None] * inv[None, :]
    cos = jnp.cos(ang)[:, None, :]
    sin = jnp.sin(ang)[:, None, :]
    xf = x.astype(jnp.float32)
    x1 = xf[..., :half]
    x2 = xf[..., half:ROPE_DIM]
    out = jnp.concatenate([x1 * cos - x2 * sin, x2 * cos + x1 * sin, xf[..., ROPE_DIM:]], axis=-1)
    return out.astype(x.dtype)


def softmax_stats(s):
    m = jnp.max(s, axis=-1, keepdims=True)
    p = jnp.exp(s - m)
    den = jnp.sum(p, axis=-1, keepdims=True)
    return p / den, (m + jnp.log(den))[..., 0]


def pool_mix(u_hist, u_new, pos, pool_w, pool_scale):
    n, t, c = u_new.shape
    u_ext = jnp.concatenate([u_hist, u_new], axis=1).astype(jnp.float32)
    cs = jnp.concatenate([jnp.zeros((n, 1, c), jnp.float32), jnp.cumsum(u_ext, axis=1)], axis=1)
    end = cs[:, POOL_HIST + 1:]
    uf = u_new.astype(jnp.float32)
    outs = []
    for j, w in enumerate(POOL_WINDOWS):
        sl = slice(j * POOL_GROUP, (j + 1) * POOL_GROUP)
        win_sum = end[..., sl] - cs[:, POOL_HIST + 1 - w:POOL_HIST + 1 - w + t, sl]
        cnt = jnp.minimum(pos + 1, w).astype(jnp.float32)[None, :, None]
        outs.append(win_sum / cnt - uf[..., sl])
    d = jnp.stack(outs, axis=2).astype(u_new.dtype)
    y = jnp.einsum("ntgc,gcd->ntgd", d, pool_w).reshape(n, t, POOL_WIDTH)
    return y * pool_scale


def conv_mix(z_hist, z_new, conv_w):
    t = z_new.shape[1]
    z_ext = jnp.concatenate([z_hist, z_new], axis=1)
    return sum(conv_w[i] * z_ext[:, i:i + t] for i in range(CONV_K))


def band_attn(q, k, v, n_back):
    n, l, h, dh = q.shape
    qb_len = n_back
    nb = -(-l // qb_len)
    lp = nb * qb_len
    pad = ((0, 0), (0, lp - l), (0, 0), (0, 0))

    def blocks(a):
        return jnp.pad(a, pad).reshape(n, nb, qb_len, h, dh)

    def with_prev(a):
        prev = jnp.concatenate([jnp.zeros_like(a[:, :1]), a[:, :-1]], axis=1)
        return jnp.concatenate([prev, a], axis=2)

    qb = blocks(q)
    kk = with_prev(blocks(k))
    vv = with_prev(blocks(v))
    s = jnp.einsum("nbqhd,nbkhd->nbhqk", qb, kk, preferred_element_type=jnp.float32) * (HEAD_DIM ** -0.5)
    qi = jnp.arange(qb_len)[:, None]
    ki = jnp.arange(2 * qb_len)[None, :]
    dist = qi + qb_len - ki
    blk = jnp.arange(nb)[:, None, None]
    valid = (dist >= 0) & (dist <= n_back) & (blk * qb_len - qb_len + ki >= 0)
    s = jnp.where(valid[None, :, None], s, NEG_INF)
    p, lse = softmax_stats(s)
    o = jnp.einsum("nbhqk,nbkhd->nbqhd", p, vv.astype(jnp.float32)).reshape(n, lp, h, dh)[:, :l]
    lse = lse.transpose(0, 1, 3, 2).reshape(n, lp, h)[:, :l]
    return o, lse


def dilated_prompt(q, k, v, window, dil):
    b, s, h, dh = q.shape
    m = s // dil

    def to_res(a):
        return a.reshape(b, m, dil, h, dh).transpose(0, 2, 1, 3, 4).reshape(b * dil, m, h, dh)

    o, lse = band_attn(to_res(q), to_res(k), to_res(v), window // dil)
    o = o.reshape(b, dil, m, h, dh).transpose(0, 2, 1, 3, 4).reshape(b, s, h, dh)
    lse = lse.reshape(b, dil, m, h).transpose(0, 2, 1, 3).reshape(b, s, h)
    return o, lse


def dilated_sample(q, k_ext, v_ext, start_pos, window, dil):
    t = q.shape[1]
    n_keys = window // dil + 1
    idx = window + jnp.arange(t)[:, None] - dil * jnp.arange(n_keys)[None, :]
    kg = k_ext[:, idx]
    vg = v_ext[:, idx]
    s = jnp.einsum("nthd,ntjhd->nthj", q, kg, preferred_element_type=jnp.float32) * (HEAD_DIM ** -0.5)
    valid = (start_pos - window + idx) >= 0
    s = jnp.where(valid[None, :, None, :], s, NEG_INF)
    p, lse = softmax_stats(s)
    o = jnp.einsum("nthj,ntjhd->nthd", p, vg.astype(jnp.float32))
    return o, lse


def trunk_layer(x, pos, is_prompt, pool_hist, conv_hist, kv_bufs,
                norm1_g, w_in, q_norm_g, k_norm_g, pool_w, pool_scale, conv_w,
                w_out, norm2_g, w_up, w_down):
    n, t, _ = x.shape
    h = rmsnorm(x, norm1_g)
    proj = jnp.einsum("ntd,dc->ntc", h, w_in)
    cuts = [POOL_WIDTH, POOL_WIDTH + ATTN_WIDTH, POOL_WIDTH + 2 * ATTN_WIDTH, POOL_WIDTH + 3 * ATTN_WIDTH,
            POOL_WIDTH + 3 * ATTN_WIDTH + CONV_WIDTH, POOL_WIDTH + 3 * ATTN_WIDTH + 2 * CONV_WIDTH]
    u, q, k, v, gb, gc, gh = jnp.split(proj, cuts, axis=-1)

    y_pool = pool_mix(pool_hist, u, pos, pool_w, pool_scale)
    new_pool = jnp.concatenate([pool_hist, u], axis=1)[:, -POOL_HIST:]

    q = partial_rope(rmsnorm(q.reshape(n, t, ATTN_HEADS, HEAD_DIM), q_norm_g), pos)
    k = partial_rope(rmsnorm(k.reshape(n, t, ATTN_HEADS, HEAD_DIM), k_norm_g), pos)
    v = v.reshape(n, t, ATTN_HEADS, HEAD_DIM)
    outs, lses, new_kv = [], [], []
    for g, (window, dil) in enumerate(DIL_PAIRS):
        hs = slice(g * HEADS_PER_DIL, (g + 1) * HEADS_PER_DIL)
        qg, kg, vg = q[:, :, hs], k[:, :, hs], v[:, :, hs]
        kv_ext = jnp.concatenate([kv_bufs[g], jnp.stack([kg, vg], axis=2)], axis=1)
        if is_prompt:
            o, lse = dilated_prompt(qg, kg, vg, window, dil)
        else:
            o, lse = dilated_sample(qg, kv_ext[:, :, 0], kv_ext[:, :, 1], pos[0], window, dil)
        outs.append(o)
        lses.append(lse)
        new_kv.append(kv_ext[:, -window:])
    wts = jax.nn.softmax(jnp.stack(lses, axis=0), axis=0)
    y_attn = jnp.sum(wts[..., None] * jnp.stack(outs, axis=0), axis=0)
    y_attn = y_attn.reshape(n, t, ATTN_OUT).astype(x.dtype)

    z = gc * gh
    y_conv = gb * conv_mix(conv_hist, z, conv_w)
    new_conv = jnp.concatenate([conv_hist, z], axis=1)[:, -(CONV_K - 1):]

    mixed = jnp.concatenate([y_pool, y_attn, y_conv], axis=-1)
    x = x + jnp.einsum("ntc,cd->ntd", mixed, w_out)

    hf = jnp.einsum("ntd,df->ntf", rmsnorm(x, norm2_g), w_up)
    x = x + jnp.einsum("ntf,fd->ntd", jnp.square(jax.nn.relu(hf)), w_down)
    return x, new_pool, new_conv, new_kv


def setup_inputs(seed: int = 0) -> dict:
    key = jax.random.key(seed)
    ks = jax.random.split(key, 24)
    f32 = jnp.float32

    def nrm(k, shape, scale):
        return jax.random.normal(k, shape, f32) * scale

    kv_shape = lambda w: (DEPTH, DEC_BATCH, w, 2, HEADS_PER_DIL, HEAD_DIM)
    return {
        "x_prompt": nrm(ks[0], (BATCH, SEQ, D_MODEL), 1.0),
        "x_sample": nrm(ks[1], (DEC_BATCH, DEC_SEQ, D_MODEL), 1.0),
        "state_pool": nrm(ks[2], (DEPTH, DEC_BATCH, POOL_HIST, POOL_WIDTH), 1.0),
        "state_conv": nrm(ks[3], (DEPTH, DEC_BATCH, CONV_K - 1, CONV_WIDTH), 1.0),
        "cache_kv_w128": nrm(ks[4], kv_shape(DIL_PAIRS[0][0]), 1.0),
        "cache_kv_w512": nrm(ks[5], kv_shape(DIL_PAIRS[1][0]), 1.0),
        "cache_kv_w2048": nrm(ks[6], kv_shape(DIL_PAIRS[2][0]), 1.0),
        "norm1_g": 1.0 + nrm(ks[7], (DEPTH, D_MODEL), 0.02),
        "w_in": nrm(ks[8], (DEPTH, D_MODEL, IN_COLS), D_MODEL ** -0.5),
        "q_norm_g": 1.0 + nrm(ks[9], (DEPTH, HEAD_DIM), 0.02),
        "k_norm_g": 1.0 + nrm(ks[10], (DEPTH, HEAD_DIM), 0.02),
        "pool_w": nrm(ks[11], (DEPTH, len(POOL_WINDOWS), POOL_GROUP, POOL_GROUP), POOL_GROUP ** -0.5),
        "pool_scale": 1.0 + nrm(ks[12], (DEPTH, POOL_WIDTH), 0.02),
        "conv_w": nrm(ks[13], (DEPTH, CONV_K, CONV_WIDTH), CONV_K ** -0.5),
        "w_out": nrm(ks[14], (DEPTH, MIX_OUT, D_MODEL), MIX_OUT ** -0.5),
        "norm2_g": 1.0 + nrm(ks[15], (DEPTH, D_MODEL), 0.02),
        "w_up": nrm(ks[16], (DEPTH, D_MODEL, D_FF), D_MODEL ** -0.5),
        "w_down": nrm(ks[17], (DEPTH, D_FF, D_MODEL), D_FF ** -0.5),
    }


def reference(x_prompt, x_sample, state_pool, state_conv, cache_kv_w128, cache_kv_w512, cache_kv_w2048,
              norm1_g, w_in, q_norm_g, k_norm_g, pool_w, pool_scale, conv_w, w_out, norm2_g, w_up, w_down):
    b, s, _ = x_prompt.shape
    pos_p = jnp.arange(s, dtype=jnp.int32)
    pos_s = PAST_LEN + jnp.arange(x_sample.shape[1], dtype=jnp.int32)
    caches = (cache_kv_w128, cache_kv_w512, cache_kv_w2048)
    xp, xs = x_prompt, x_sample
    pool_p, conv_p, kv_p = [], [], [[] for _ in DIL_PAIRS]
    pool_s, conv_s, kv_s = [], [], [[] for _ in DIL_PAIRS]
    for layer in range(DEPTH):
        wts = (norm1_g[layer], w_in[layer], q_norm_g[layer], k_norm_g[layer], pool_w[layer], pool_scale[layer],
               conv_w[layer], w_out[layer], norm2_g[layer], w_up[layer], w_down[layer])
        zero_kv = [jnp.zeros((b, w, 2, HEADS_PER_DIL, HEAD_DIM), xp.dtype) for (w, _) in DIL_PAIRS]
        xp, np_pool, np_conv, np_kv = trunk_layer(
            xp, pos_p, True,
            jnp.zeros((b, POOL_HIST, POOL_WIDTH), xp.dtype),
            jnp.zeros((b, CONV_K - 1, CONV_WIDTH), xp.dtype),
            zero_kv, *wts)
        xs, ns_pool, ns_conv, ns_kv = trunk_layer(
            xs, pos_s, False, state_pool[layer], state_conv[layer],
            [c[layer] for c in caches], *wts)
        pool_p.append(np_pool)
        conv_p.append(np_conv)
        pool_s.append(ns_pool)
        conv_s.append(ns_conv)
        for g in range(N_DIL):
            kv_p[g].append(np_kv[g])
            kv_s[g].append(ns_kv[g])
    return (xp, xs,
            jnp.stack(pool_p), jnp.stack(conv_p),
            jnp.stack(kv_p[0]), jnp.stack(kv_p[1]), jnp.stack(kv_p[2]),
            jnp.stack(pool_s), jnp.stack(conv_s),
            jnp.stack(kv_s[0]), jnp.stack(kv_s[1]), jnp.stack(kv_s[2]))
```

```python
import types
from contextlib import ExitStack

import ml_dtypes
import numpy as np

import concourse.bass as bass
import concourse.mybir as mybir
from concourse.bass_utils import run_bass_kernel_spmd

F32 = mybir.dt.float32
BF16 = mybir.dt.bfloat16
U8 = mybir.dt.uint8
ALU = mybir.AluOpType
AF = mybir.ActivationFunctionType
AX = mybir.AxisListType

NCORES = 8
D = 1024
T = 2048
TS = 32
NTOK = T + TS
NTILE = 17
DEPTH = 2
DFF = 4096
EPS = 1e-6
WINS = (128, 512, 2048)
DILS = (1, 4, 16)
COMPUTE = ("pe", "act", "dve", "pool")


def _snap(fn):
    if fn.__closure__ is None:
        return fn
    cells = tuple(types.CellType(c.cell_contents) for c in fn.__closure__)
    return types.FunctionType(fn.__code__, fn.__globals__, fn.__name__, fn.__defaults__, cells)


class Op:
    __slots__ = ("eng", "fn", "waits", "signal", "sigval", "idx", "dma_slot", "dma_val", "queue", "wait_vals")

    def __init__(self, eng, fn):
        self.eng = eng
        self.fn = _snap(fn)
        self.waits = []
        self.wait_vals = {}
        self.signal = False
        self.sigval = None
        self.idx = None
        self.dma_slot = None
        self.dma_val = None
        self.queue = None


class Region:
    __slots__ = ("name", "lo", "hi", "writer", "readers", "overl")

    def __init__(self, name, lo, hi):
        self.name, self.lo, self.hi = name, lo, hi
        self.writer = None
        self.readers = {}
        self.overl = None


class Sched:
    def __init__(self, nc):
        self.nc = nc
        self.ops = {e: [] for e in COMPUTE + ("sp",)}
        self.regions = {}
        self.phys = []
        self.dma_slots = {}
        self.waited = {}

    def region(self, name, lo=None, hi=None):
        r = self.regions.get(name)
        if r is None:
            r = Region(name, lo, hi)
            self.regions[name] = r
            if lo is not None:
                for o in self.phys:
                    if o.lo < hi and lo < o.hi:
                        if o.overl is None:
                            o.overl = []
                        o.overl.append(r)
                        if r.overl is None:
                            r.overl = []
                        r.overl.append(o)
                self.phys.append(r)
        return r

    def _regs(self, keys):
        out = []
        for k in keys:
            r = self.regions[k]
            out.append(r)
            if r.overl:
                out.extend(r.overl)
        return out

    def _deps(self, op, reads, writes):
        deps = []
        for r in self._regs(reads):
            if r.writer is not None:
                deps.append(r.writer)
        for r in self._regs(writes):
            if r.writer is not None:
                deps.append(r.writer)
            deps.extend(r.readers.values())
        best = {}
        for d in deps:
            if d is op:
                continue
            if d.dma_slot is not None:
                key = ("dma", d.dma_slot)
                v = self.dma_slots[d.dma_slot] - (16 if (op.dma_slot == d.dma_slot) else 0)
            else:
                if d.eng == op.eng and op.dma_slot is None and d.eng == "pe":
                    continue
                key = ("eng", d.eng)
                v = d.idx
            if key not in best or v > best[key][0]:
                best[key] = (v, d)
        wq = op.eng
        for key, (v, d) in best.items():
            wk = (wq, key)
            if self.waited.get(wk, -1) >= v:
                continue
            self.waited[wk] = v
            if d.dma_slot is None:
                d.signal = True
            else:
                op.wait_vals[id(d)] = v
            op.waits.append(d)

    def _commit(self, op, reads, writes):
        tag = ("dma", op.dma_slot) if op.dma_slot is not None else op.eng
        for k in reads:
            self.regions[k].readers[tag] = op
        for k in writes:
            r = self.regions[k]
            r.writer = op
            r.readers = {}

    def op(self, eng, fn, reads=(), writes=()):
        o = Op(eng, fn)
        o.idx = len(self.ops[eng])
        self._deps(o, reads, writes)
        self.ops[eng].append(o)
        self._commit(o, reads, writes)
        return o

    def dma(self, fn, reads=(), writes=(), slot=None, queue="sp"):
        o = Op(queue, fn)
        o.dma_slot = slot
        self.dma_slots[slot] = self.dma_slots.get(slot, 0) + 16
        o.dma_val = self.dma_slots[slot]
        o.idx = len(self.ops[queue])
        self._deps(o, reads, writes)
        self.ops[queue].append(o)
        self._commit(o, reads, writes)
        return o

    def emit(self, es):
        nc = self.nc
        sems = {}
        for e in COMPUTE:
            sems[("eng", e)] = es.enter_context(nc.semaphore("s_" + e))
        for i, s in enumerate(self.dma_slots):
            sems[("dma", s)] = es.enter_context(nc.semaphore("d%d" % i))
        for e in COMPUTE:
            c = 0
            for o in self.ops[e]:
                if o.signal:
                    c += 1
                    o.sigval = c
        final_waits = dict(self.dma_slots)

        def run(engname, eng):
            for o in self.ops[engname]:
                for d in o.waits:
                    if d.dma_slot is not None:
                        eng.wait_ge(sems[("dma", d.dma_slot)], o.wait_vals[id(d)])
                    else:
                        eng.wait_ge(sems[("eng", d.eng)], d.sigval)
                ins = o.fn(eng)
                if o.dma_slot is not None:
                    ins.then_inc(sems[("dma", o.dma_slot)], 16)
                elif o.signal:
                    ins.then_inc(sems[("eng", o.eng)], 1)
            if engname == "sp":
                for s, v in final_waits.items():
                    eng.wait_ge(sems[("dma", s)], v)

        block = es.enter_context(nc.Block())

        @block.tensor
        def _(e):
            run("pe", e)

        @block.scalar
        def _(e):
            run("act", e)

        @block.vector
        def _(e):
            run("dve", e)

        @block.gpsimd
        def _(e):
            run("pool", e)

        @block.sync
        def _(e):
            run("sp", e)


def _tile_positions(g, ti):
    if ti == 16:
        return None
    if g == 0:
        return 128 * ti + np.arange(128)
    if g == 1:
        r, kb = ti // 4, ti % 4
        return 512 * kb + 4 * np.arange(128) + r
    return 16 * np.arange(128) + ti


def make_consts():
    bf = ml_dtypes.bfloat16
    c = {}
    c["ident"] = np.eye(128, dtype=np.float32)
    k = np.arange(128)[:, None]
    q = np.arange(128)[None, :]
    prev = np.where(k >= q, 0.0, -1000.0).astype(np.float32)
    cur = np.where(k <= q, 0.0, -1000.0).astype(np.float32)
    c["maskp"] = np.concatenate([prev, prev, cur, cur], axis=1).astype(bf)
    ms = np.zeros((3, 128, 4, 2, 2, 4, 8), np.float32)
    mn = np.zeros((3, 32, 2, 4, 8), np.float32)
    for g in range(3):
        dil = DILS[g]
        p = np.arange(128)[:, None]
        t = np.arange(8)[None, :]
        base = ((t - p) % dil == 0).astype(np.float32)
        v0 = base * (p >= t)
        for n in range(4):
            for h in range(2):
                ms[g, :, n, 0, h, n, :] = v0
                ms[g, :, n, 1, h, n, :] = base
        for n in range(4):
            for tp in range(8):
                for tq in range(8):
                    if tp <= tq and (tq - tp) % dil == 0:
                        mn[g, n * 8 + tp, :, n, tq] = 1.0
    c["masks"] = ms.reshape(3, 128, 4, 2, 64)[0:2].astype(bf)
    m2 = np.zeros((128, 4, 8, 2, 4, 8), np.float32)
    for n in range(4):
        for j in range(8):
            m2[:, n, j, :, n, j] = 1.0
    c["mask2"] = m2.reshape(128, 4, 8, 64).astype(bf)
    c["maskn"] = mn.reshape(3, 32, 64).astype(bf)
    half = 8
    inv = np.power(np.float32(500000.0), -np.arange(half, dtype=np.float32) / half).astype(np.float32)
    rope = np.zeros((3, 128, 17, 2, 4, 8), np.float32)
    for g in range(3):
        for ti in range(17):
            if ti < 16:
                pos = _tile_positions(g, ti).astype(np.float32)
            else:
                pos = np.zeros(128, np.float32)
                pos[:32] = (8192 + (np.arange(32) % 8)).astype(np.float32)
            ang = (pos[:, None] * inv[None, :]).astype(np.float32)
            rope[g, :, ti, 0] = np.cos(ang)[:, None, :]
            rope[g, :, ti, 1] = np.sin(ang)[:, None, :]
    c["rope"] = rope
    wins = np.array([[2, 4], [8, 16]], np.float32)
    invw = np.zeros((128, 2), np.float32)
    invtab = np.zeros((128, 2, 15), np.float32)
    for pc in range(2):
        for hf in range(2):
            w = wins[pc, hf]
            invw[hf * 64:(hf + 1) * 64, pc] = 1.0 / w
            invtab[hf * 64:(hf + 1) * 64, pc, :] = 1.0 / np.minimum(np.arange(15) + 1, w)
    c["invw"] = invw
    c["invtab"] = invtab
    return c


def build_program(stop=None):
    nc = bass.Bass("TRN2", target_bir_lowering=False)

    def din(name, shape, dt=F32):
        return nc.dram_tensor(name, list(shape), dt, kind="ExternalInput").ap()

    def dout(name, shape):
        return nc.dram_tensor(name, list(shape), F32, kind="ExternalOutput").ap()

    xp = din("xp", [T, D])
    xs = din("xs", [TS, D])
    spool = din("spool", [DEPTH, 4, 15, 256])
    sconv = din("sconv", [DEPTH, 4, 2, 384])
    caches = [din("c128", [DEPTH, 4, 128, 256]), din("c512", [DEPTH, 4, 512, 256]), din("c2048", [DEPTH, 4, 2048, 256])]
    norm1_g = din("norm1_g", [DEPTH, D])
    w_in = din("w_in", [DEPTH, D, 2560])
    q_norm_g = din("q_norm_g", [DEPTH, 64])
    k_norm_g = din("k_norm_g", [DEPTH, 64])
    pool_w = din("pool_w", [DEPTH, 4, 64, 64])
    pool_scale = din("pool_scale", [DEPTH, 256])
    conv_w = din("conv_w", [DEPTH, 3, 384])
    w_out = din("w_out", [DEPTH, 768, D])
    norm2_g = din("norm2_g", [DEPTH, D])
    w_up = din("w_up", [DEPTH, D, DFF])
    w_down = din("w_down", [DEPTH, DFF, D])
    c_ident = din("ident", [128, 128])
    c_maskp = din("maskp", [128, 512], BF16)
    c_masks = din("masks", [2, 128, 4, 2, 64], BF16)
    c_mask2 = din("mask2", [128, 4, 8, 64], BF16)
    c_maskn = din("maskn", [3, 32, 64], BF16)
    c_rope = din("rope", [3, 128, 17, 2, 4, 8])
    c_invw = din("invw", [128, 2])
    c_invtab = din("invtab", [128, 2, 15])

    yp = dout("yp", [T, D])
    ys = dout("ys", [TS, D])
    o_pool_p = dout("pool_p", [DEPTH, 15, 256])
    o_conv_p = dout("conv_p", [DEPTH, 2, 384])
    o_kv_p = [dout("kv128_p", [DEPTH, 128, 256]), dout("kv512_p", [DEPTH, 512, 256]), dout("kv2048_p", [DEPTH, 2048, 256])]
    o_pool_s = dout("pool_s", [DEPTH, 4, 15, 256])
    o_conv_s = dout("conv_s", [DEPTH, 4, 2, 384])
    o_kv_s = [dout("kv128_s", [DEPTH, 4, 128, 256]), dout("kv512_s", [DEPTH, 4, 512, 256]), dout("kv2048_s", [DEPTH, 4, 2048, 256])]
    xpark = nc.dram_tensor("xpark", [NTILE, 128, D], F32, kind="Internal").ap()

    es = ExitStack()
    S = Sched(nc)
    NB = 212800
    big = es.enter_context(nc.sbuf_tensor("big", [128, NB], U8))
    psum = es.enter_context(nc.psum_tensor("psum", [128, 8 * 512], F32))

    def bank(i):
        return psum[:, i * 512:(i + 1) * 512]

    for i in range(8):
        S.region(("ps", i))
    ps_rr = [0]

    ps_held = set()

    def next_ps():
        for _ in range(8):
            i = ps_rr[0]
            ps_rr[0] = (i + 1) % 6
            if i not in ps_held:
                return i
        raise AssertionError("no free PSUM bank")

    def hold_ps():
        i = next_ps()
        ps_held.add(i)
        return i

    def rel_ps(i):
        ps_held.discard(i)

    def hold_ps_pref(order):
        for i in order:
            if i not in ps_held:
                ps_held.add(i)
                return i
        raise AssertionError("no free PSUM bank")

    def hold_ps_pair():
        for p0 in (0, 2, 4):
            if p0 not in ps_held and p0 + 1 not in ps_held:
                ps_held.add(p0)
                ps_held.add(p0 + 1)
                return p0
        raise AssertionError("no free PSUM bank pair")

    def carve(name, off, shape, dt):
        esz = 4 if dt == F32 else 2
        n = int(np.prod(shape[1:])) * esz
        assert off + n <= NB, (name, off, n)
        a = big[:, off:off + n].bitcast(dt)
        if len(shape) == 3:
            a = a.rearrange("p (a b) -> p a b", a=shape[1])
        elif len(shape) == 4:
            a = a.rearrange("p (a b c) -> p a b c", a=shape[1], b=shape[2])
        elif len(shape) == 5:
            a = a.rearrange("p (a b c d) -> p a b c d", a=shape[1], b=shape[2], c=shape[3])
        return a, off + n

    def reg(name, off, nbytes):
        S.region(name, off, off + nbytes)

    off = 0
    IDB, off = carve("idb", off, [128, 128], BF16); reg("idb", off - 256, 256)
    IDF, off = carve("idf", off, [128, 128], F32); reg("idf", off - 512, 512)
    ONESB, off = carve("onesb", off, [128, 128], BF16); reg("onesb", off - 256, 256)
    EPSC, off = carve("epsc", off, [128, 1], F32); reg("epsc", off - 4, 4)
    off = (off + 63) // 64 * 64
    HSEL, off = carve("hsel", off, [128, 2, 128], BF16); reg("hsel", off - 512, 512)
    off = (off + 63) // 64 * 64
    MASKP, off = carve("maskp", off, [128, 512], BF16); reg("maskp", off - 1024, 1024)
    MASKS, off = carve("masks", off, [128, 2, 4, 2, 64], BF16); reg("masks", off - 2048, 2048)
    MASK2, off = carve("mask2", off, [128, 4, 8, 64], BF16); reg("mask2", off - 4096, 4096)
    MASKN, off = carve("maskn", off, [128, 3, 64], BF16); reg("maskn", off - 384, 384)
    INVW, off = carve("invw", off, [128, 2], F32); reg("invw", off - 8, 8)
    INVTAB, off = carve("invtab", off, [128, 2, 15], F32); reg("invtab", off - 120, 120)
    PWBD, off = carve("pwbd", off, [128, 2, 128], BF16); reg("pwbd", off - 512, 512)
    PSCALE, off = carve("pscale", off, [128, 2], F32)
    CONVW, off = carve("convw", off, [128, 3, 3], F32)
    GQK, off = carve("gqk", off, [128, 256], F32)
    K_PS = [("pscale", i) for i in range(2)]
    K_CW = [("convw", i) for i in range(9)]
    K_GQ = [("gqk", i) for i in range(4)]
    for k_ in K_PS + K_CW + K_GQ:
        S.region(k_)
    GBC, off = carve("gbc", off, [128, 1024], F32); reg("gbc", off - 4096, 4096)
    SS, off = carve("ss", off, [128, 17], F32); reg("ss", off - 68, 68)
    RSTD, off = carve("rstd", off, [128, 17], F32); reg("rstd", off - 68, 68)
    off = (off + 63) // 64 * 64
    HB_OFF = off
    HB = []
    for i in range(2):
        a, off = carve("hb", off, [128, 1024], BF16); reg(("hb", i), off - 2048, 2048)
        HB.append(a)
    ARENA_OFF = off
    ARENA, off = carve("arena", off, [128, 32768], BF16)
    HT, off = carve("ht", off, [128, 8, NTOK], BF16)
    for t in range(NTILE):
        S.region(("ht", t))
    X_OFF = off
    X, off = carve("x", off, [128, NTILE, D], F32)
    for t in range(NTILE):
        reg(("x", t), X_OFF + t * 4096, 4096)
    MIX_OFF = off
    MIXT, off = carve("mixt", off, [128, 6, NTOK], BF16)
    for c in range(6):
        for b in range(5):
            lo = MIX_OFF + (c * NTOK + 512 * b) * 2
            reg(("mix", c, b), lo, lo + (1024 if b < 4 else 64) - lo + lo - lo)
    for c in range(6):
        for b in range(5):
            r = S.regions[("mix", c, b)]
            r.lo = MIX_OFF + (c * NTOK + 512 * b) * 2
            r.hi = r.lo + (1024 if b < 4 else 64)
    assert off <= NB, off
    TAIL_OFF = off

    o2 = X_OFF
    U, o2 = carve("u", o2, [128, 2, 527], F32)
    for pc in range(2):
        reg(("u", pc), X_OFF + pc * 527 * 4, 527 * 4)
    SB = []
    for pc in range(2):
        lst = []
        for nm in (("s2", "s4") if pc == 0 else ("s2", "s4", "s8", "s16")):
            a_, o2 = carve(nm, o2, [128, 527], F32); reg((nm, pc), o2 - 2108, 2108)
            lst.append(a_)
        SB.append(lst)
    TMP15, DTB = [], []
    for pc in range(2):
        a_, o2 = carve("tmp15", o2, [128, 16], F32); reg(("tmp15", pc), o2 - 64, 64)
        TMP15.append(a_)
    o2 = (o2 + 63) // 64 * 64
    for pc in range(2):
        a_, o2 = carve("dt", o2, [128, 512], BF16); reg(("dt", pc), o2 - 1024, 1024)
        DTB.append(a_)
    ZOFF = o2
    Z, o2 = carve("z", o2, [128, 3, 514], F32)
    for cc in range(3):
        reg(("z", cc), ZOFF + cc * 514 * 4, 514 * 4)
    GCS, CACC = [], []
    for cc in range(3):
        a_, o2 = carve("gcs", o2, [128, 512], F32); reg(("gcs", cc), o2 - 2048, 2048)
        GCS.append(a_)
        a_, o2 = carve("cacc", o2, [128, 512], F32); reg(("cacc", cc), o2 - 2048, 2048)
        CACC.append(a_)
    US, o2 = carve("us", o2, [128, 2, 4, 23], F32); reg("us", o2 - 736, 736)
    ZS, o2 = carve("zs", o2, [128, 3, 4, 10], F32); reg("zs", o2 - 480, 480)
    TS32, o2 = carve("ts32", o2, [128, 32], F32); reg("ts32", o2 - 128, 128)
    STF, o2 = carve("stf", o2, [128, 384], F32); reg("stf", o2 - 1536, 1536)
    OUTS, o2 = carve("outs", o2, [128, 384], F32); reg("outs", o2 - 1536, 1536)
    OUTS2, o2 = carve("outs2", o2, [128, 384], F32); reg("outs2", o2 - 1536, 1536)
    OUTS_S, o2 = carve("outs_s", o2, [128, 384], F32); reg("outs_s", o2 - 1536, 1536)
    OUTS2_S, o2 = carve("outs2_s", o2, [128, 384], F32); reg("outs2_s", o2 - 1536, 1536)
    TS32B, o2 = carve("ts32b", o2, [128, 5, 32], F32)
    for i in range(5):
        reg(("ts32b", i), o2 - 640 + 128 * i, 128)
    assert o2 <= X_OFF + NTILE * 4096

    o3 = X_OFF
    QKT_S, VAUG_S, ROPE_S = [None, None], [None, None], [None, None]
    QKT_S[0], o3 = carve("qkt", o3, [128, 3, NTOK], BF16)
    VOFF = o3
    VAUG_S[0], o3 = carve("vaug", o3, [128, NTILE, 2, 65], BF16)
    o3 = (o3 + 63) // 64 * 64
    ROFF = o3
    ROPE_S[0], o3 = carve("rope", o3, [128, 17, 2, 4, 8], F32)
    oa = ARENA_OFF
    Q1OFF = oa
    QKT_S[1], oa = carve("qkt1", oa, [128, 3, NTOK], BF16)
    V1OFF = oa
    VAUG_S[1], oa = carve("vaug1", oa, [128, NTILE, 2, 65], BF16)
    oa = (oa + 63) // 64 * 64
    R1OFF = oa
    ROPE_S[1], oa = carve("rope1", oa, [128, 17, 2, 4, 8], F32)
    assert oa <= ARENA_OFF + 11264 * 2, oa
    for sg, (qo, vo, ro) in enumerate(((X_OFF, VOFF, ROFF), (Q1OFF, V1OFF, R1OFF))):
        for a in range(3):
            for ti in range(NTILE):
                reg(("qkt", sg, a, ti), qo + (a * NTOK + ti * 128) * 2, 256 if ti < 16 else 64)
        for ti in range(NTILE):
            reg(("vaug", sg, ti), vo + ti * 260, 260)
        reg(("rope", sg), ro, 4352)
    AOFF = o3
    ACC, o3 = carve("acc", o3, [128, 2, T], F32); reg("acc", AOFF, o3 - AOFF)
    QKVF = []
    for i in range(3):
        a_, o3 = carve("qkvf", o3, [128, 2, 384], F32); reg(("qkvf", i), o3 - 3072, 3072)
        QKVF.append(a_)
    oq = ARENA_OFF + 28672 * 2
    for i in range(2):
        a_, oq = carve("qkvf", oq, [128, 2, 384], F32); reg(("qkvf", 3 + i), oq - 3072, 3072)
        QKVF.append(a_)
    NQ = len(QKVF)
    SQ = []
    for i in range(1):
        a_, o3 = carve("sq", o3, [128, 2, 256], F32); reg(("sq", i), o3 - 2048, 2048)
        SQ.append(a_)
    S4OFF = o3
    SS4, o3 = carve("ss4", o3, [128, 4, 8], F32)
    R4OFF = o3
    RS4, o3 = carve("rs4", o3, [128, 4, 8], F32)
    for i in range(4):
        reg(("ss4", i), S4OFF + 32 * i, 32)
        reg(("rs4", i), R4OFF + 32 * i, 32)
    ROT = []
    for i in range(1):
        a_, o3 = carve("rot", o3, [128, 4, 2, 4, 8], F32); reg(("rot", i), o3 - 1024, 1024)
        ROT.append(a_)
    QKB = []
    for i in range(2):
        a_, o3 = carve("qkb", o3, [128, 2, 3, 128], BF16); reg(("qkb", i), o3 - 1536, 1536)
        QKB.append(a_)
    CST = []
    for i in range(2):
        a_, o3 = carve("cst", o3, [128, 4, 256], F32); reg(("cst", i), o3 - 4096, 4096)
        CST.append(a_)
    KB_, VS, KTS, PSB = [], [], [], []
    for i in range(2):
        a_, o3 = carve("kb", o3, [128, 4, 128], BF16); reg(("kb", i), o3 - 1024, 1024)
        KB_.append(a_)
        a_, o3 = carve("vs", o3, [128, 4, 2, 65], BF16); reg(("vs", i), o3 - 1040, 1040)
        VS.append(a_)
        o3 = (o3 + 63) // 64 * 64
        a_, o3 = carve("kts", o3, [128, 512], BF16); reg(("kts", i), o3 - 1024, 1024)
        KTS.append(a_)
        a_, o3 = carve("psb", o3, [128, 4, 64], BF16); reg(("psb", i), o3 - 512, 512)
        PSB.append(a_)
    QBDG = []
    for i in range(3):
        a_, o3 = carve("qbd", o3, [128, 64], BF16); reg(("qbd", i), o3 - 128, 128)
        QBDG.append(a_)
    PN, o3 = carve("pn", o3, [128, 64], BF16); reg("pn", o3 - 128, 128)
    RD, o3 = carve("rd", o3, [128, 2], F32); reg("rd", o3 - 8, 8)
    o3 = (o3 + 63) // 64 * 64
    YSB, o3 = carve("ysb", o3, [128, 128], BF16); reg("ysb", o3 - 256, 256)
    assert o3 <= X_OFF + NTILE * 4096, (o3 - X_OFF - NTILE * 4096)
    PB = []
    for i in range(4):
        a_, _ = carve("pb", HB_OFF + 1024 * i, [128, 512], BF16); reg(("pb", i), HB_OFF + 1024 * i, 1024)
        PB.append(a_)

    AT = []
    o4 = MIX_OFF
    for i in range(2):
        a, o4 = carve("at", o4, [128, 8, 512], BF16); reg(("at", i), o4 - 8192, 8192)
        AT.append(a)
    RL = []
    for i in range(2):
        a, o4 = carve("rl", o4, [128, 512], F32); reg(("rl", i), o4 - 2048, 2048)
        RL.append(a)
    assert o4 <= MIX_OFF + 6 * NTOK * 2

    def arena_piece(name, col, shape):
        n = int(np.prod(shape[1:]))
        a = ARENA[:, col:col + n]
        if len(shape) == 3:
            a = a.rearrange("p (a b) -> p a b", a=shape[1])
        reg(name, ARENA_OFF + col * 2, n * 2)
        return a

    WA = arena_piece("wa", 0, [128, 8, 1408])
    WQ = [arena_piece(("wq", g), 12288 + 3072 * g, [128, 8, 384]) for g in range(3)]
    WO = arena_piece("wo", 22528, [128, 6, 1024])
    WU = [arena_piece(("wu", i), 16384 * i, [128, 8, 1024]) for i in range(2)]
    WD = [arena_piece(("wd", i), 16384 * i + 8192, [128, 8, 1024]) for i in range(2)]

    for t in range(NTILE):
        S.region(("park", t))

    def tile_rows(t):
        return 128 if t < 16 else 32

    def tile_cols(t):
        return slice(128 * t, 128 * t + tile_rows(t))

    def blk_cols(b):
        return slice(512 * b, 512 * b + (512 if b < 4 else 32))

    def blk_len(b):
        return 512 if b < 4 else 32

    S.dma(lambda e: e.dma_start(out=IDF, in_=c_ident), writes=["idf"], slot="c0")
    S.dma(lambda e: e.dma_start(out=MASKP, in_=c_maskp), writes=["maskp"], slot="c1")
    S.dma(lambda e: e.dma_start(out=MASKS, in_=c_masks.rearrange("g p n v c -> p g n v c")), writes=["masks"], slot="c2")
    S.dma(lambda e: e.dma_start(out=MASKN[0:32], in_=c_maskn.rearrange("g p c -> p g c")), writes=["maskn"], slot="c3")
    S.dma(lambda e: e.dma_start(out=INVW, in_=c_invw), writes=["invw"], slot="c4")
    S.dma(lambda e: e.dma_start(out=INVTAB, in_=c_invtab), writes=["invtab"], slot="c5")
    S.op("dve", lambda e: e.tensor_copy(out=IDB, in_=IDF), reads=["idf"], writes=["idb"])
    S.op("dve", lambda e: e.memset(ONESB, 1.0), writes=["onesb"])
    S.op("dve", lambda e: e.memset(EPSC, EPS), writes=["epsc"])
    S.op("dve", lambda e: e.memset(HSEL, 0.0), writes=["hsel"])
    for h in range(2):
        S.op("dve", lambda e, h=h: e.memset(HSEL[:, h, 64 * h:64 * h + 64], 1.0), writes=["hsel"])
    S.dma(lambda e: e.dma_start(out=MASK2, in_=c_mask2), writes=["mask2"], slot="c6")
    S.op("dve", lambda e: e.memset(PWBD, 0.0), writes=["pwbd"])
    for g in range(3):
        W = WINS[g]
        S.dma(lambda e, g=g, W=W: e.dma_start(out=o_kv_s[g][:, :, 0:W - 8, :], in_=caches[g][:, :, 8:W, :]),
              slot=("c2c", g))
    S.dma(lambda e: e.dma_start(out=o_pool_s[:, :, 0:7, :], in_=spool[:, :, 8:15, :]), slot="c2cp")
    for t in range(NTILE):
        src = xp[128 * t:128 * t + 128, :] if t < 16 else xs
        R = tile_rows(t)
        S.dma(lambda e, t=t, src=src, R=R: e.dma_start(out=X[0:R, t, :], in_=src), writes=[("x", t)], slot=("xl", t))

    wq_state = {"n": 0}

    def wdma(fn, writes, slot):
        S.dma(fn, writes=writes, slot=slot, queue="pool")

    def load_wa(l):
        wv = w_in[l].rearrange("(c p) n -> p c n", p=128)
        wdma(lambda e: e.dma_start(out=WA[:, :, 0:256], in_=wv[:, :, 0:256]), ["wa"], "wa")
        for h in range(4):
            wdma(lambda e, h=h: e.dma_start(out=WA[:, 2 * h:2 * h + 2, 256:1408], in_=wv[:, 2 * h:2 * h + 2, 1408:2560]), ["wa"], "wa")

    def load_wq(l):
        wv = w_in[l].rearrange("(c p) n -> p c n", p=128)
        for g in range(3):
            for j in range(3):
                c0 = 256 + 384 * j + 128 * g
                wdma(lambda e, g=g, j=j, c0=c0: e.dma_start(out=WQ[g][:, :, 128 * j:128 * j + 128], in_=wv[:, :, c0:c0 + 128]),
                     [("wq", g)], ("wq", g))

    def load_wo(l):
        wv = w_out[l].rearrange("(c p) n -> p c n", p=128)
        for h in range(3):
            wdma(lambda e, h=h: e.dma_start(out=WO[:, 2 * h:2 * h + 2, :], in_=wv[:, 2 * h:2 * h + 2, :]), ["wo"], "wo")

    def load_ffn(l, fb):
        i = fb % 2
        wu = w_up[l].rearrange("(c p) f -> p c f", p=128)
        wd = w_down[l][fb * 1024:(fb + 1) * 1024, :].rearrange("(c p) n -> p c n", p=128)
        for h in range(2):
            wdma(lambda e, h=h: e.dma_start(out=WU[i][:, 4 * h:4 * h + 4, :], in_=wu[:, 4 * h:4 * h + 4, fb * 1024:(fb + 1) * 1024]),
                 [("wu", i)], ("wu", i))
        for h in range(2):
            wdma(lambda e, h=h: e.dma_start(out=WD[i][:, 4 * h:4 * h + 4, :], in_=wd[:, 4 * h:4 * h + 4, :]),
                 [("wd", i)], ("wd", i))

    def load_smalls(l):
        for g in range(4):
            pc, hf = g // 2, g % 2
            wdma(lambda e, g=g, pc=pc, hf=hf: e.dma_start(out=PWBD[hf * 64:(hf + 1) * 64, pc, hf * 64:(hf + 1) * 64], in_=pool_w[l, g]),
                 ["pwbd"], "pwbd")
        for pc in range(2):
            S.dma(lambda e, pc=pc: e.dma_start(out=PSCALE[:, pc:pc + 1], in_=pool_scale[l, pc * 128:(pc + 1) * 128].rearrange("(p o) -> p o", o=1)),
                  writes=[("pscale", pc)], slot=("sm_ps", pc))
        for cc in range(3):
            for k in range(3):
                S.dma(lambda e, cc=cc, k=k: e.dma_start(out=CONVW[:, cc, k:k + 1], in_=conv_w[l, k, cc * 128:(cc + 1) * 128].rearrange("(p o) -> p o", o=1)),
                      writes=[("convw", 3 * cc + k)], slot=("sm_cw", 3 * cc + k))
        for j in range(4):
            src = q_norm_g if j < 2 else k_norm_g
            S.dma(lambda e, j=j, src=src: e.dma_start(out=GQK[:, 64 * j:64 * j + 64], in_=src[l].partition_broadcast(128)),
                  writes=[("gqk", j)], slot=("sm_gq", j))

    def load_gbc(gvec):
        S.dma(lambda e: e.dma_start(out=GBC, in_=gvec.partition_broadcast(128)), writes=["gbc"], slot="gbc")

    def norm_and_transpose(gvec, pre=None, have_ss=False):
        if not have_ss:
            load_gbc(gvec)
        if not have_ss:
            S.op("dve", lambda e: e.memset(SS, 0.0), writes=["ss"])
        for t in range(NTILE):
            if have_ss:
                break
            R = tile_rows(t)
            if pre is not None:
                pre(t)
            hb = HB[t % 2]
            S.op("act", lambda e, t=t, R=R, hb=hb: e.activation(out=hb[0:R, :], in_=X[0:R, t, :], func=AF.Square, accum_out=SS[0:R, t:t + 1]),
                 reads=[("x", t)], writes=[("hb", t % 2), "ss"])
        S.op("dve", lambda e: e.tensor_scalar(out=RSTD, in0=SS, scalar1=1.0 / D, scalar2=EPS, op0=ALU.mult, op1=ALU.add),
             reads=["ss"], writes=["rstd"])
        S.op("act", lambda e: e.activation(out=RSTD, in_=RSTD, func=AF.Ln), reads=["rstd"], writes=["rstd"])
        S.op("act", lambda e: e.activation(out=RSTD, in_=RSTD, func=AF.Exp, scale=-0.5), reads=["rstd"], writes=["rstd"])
        for t in range(NTILE):
            R = tile_rows(t)
            hb = HB[t % 2]
            S.op("dve", lambda e, t=t, R=R, hb=hb: e.scalar_tensor_tensor(out=hb[0:R, :], in0=X[0:R, t, :], scalar=RSTD[0:R, t:t + 1],
                                                                       in1=GBC[0:R, :], op0=ALU.mult, op1=ALU.mult),
                 reads=[("x", t), "rstd", "gbc"], writes=[("hb", t % 2)])
            pi = next_ps()
            pst = bank(pi)[:, 0:512].bitcast(BF16).rearrange("p (a b) -> p a b", a=8)
            for c in range(8):
                S.op("pe", lambda e, c=c, R=R, hb=hb, pst=pst: e.transpose(out=pst[:, c, 0:R], in_=hb[0:R, c * 128:(c + 1) * 128], identity=IDB[0:R, 0:R]),
                     reads=[("hb", t % 2), "idb"], writes=[("ps", pi)])
            S.op("act", lambda e, t=t, R=R, pst=pst: e.copy(out=HT[:, :, 128 * t:128 * t + R], in_=pst[:, :, 0:R]),
                 reads=[("ps", pi)], writes=[("ht", t)])

    def pipeline(items, lag=1, depth=4):
        active = []
        pending = list(items)
        tick = 0
        while pending or active:
            assert tick < 200000, "pipeline deadlock"

            started = None
            if pending and len(active) < depth:
                nxt_item = pending[0]
                key_ = nxt_item[0] if isinstance(nxt_item, tuple) else None
                if key_ is None or all(a_[2] != key_ for a_ in active):
                    pending.pop(0)
                    g_ = (nxt_item[1] if isinstance(nxt_item, tuple) else nxt_item)()
                    try:
                        next(g_)
                        started = [g_, tick + lag, key_]
                    except StopIteration:
                        pass
            for a_ in list(active):
                if a_[1] <= tick:
                    try:
                        next(a_[0])
                        a_[1] = tick + lag
                    except StopIteration:
                        active.remove(a_)
            if started:
                active.append(started)
            tick += 1

    def phase_a2a(l):
        S.dma(lambda e: e.dma_start(out=STF[0:60, 0:256], in_=spool[l].rearrange("n r c -> (n r) c")), writes=["stf"], slot="st1")
        for pc in range(2):
            pi = next_ps()
            S.op("pe", lambda e, pc=pc, pi=pi: e.transpose(out=bank(pi)[:, 0:60], in_=STF[0:60, pc * 128:(pc + 1) * 128], identity=IDF[0:60, 0:60]),
                 reads=["stf", "idf"], writes=[("ps", pi)])
            S.op("act", lambda e, pc=pc, pi=pi: e.copy(out=US[:, pc, :, 0:15], in_=bank(pi)[:, 0:60].rearrange("p (n r) -> p n r", n=4)),
                 reads=[("ps", pi)], writes=["us"])
        S.dma(lambda e: e.dma_start(out=OUTS2[0:8, 0:384], in_=sconv[l].rearrange("n r c -> (n r) c")), writes=["outs2"], slot="st2")
        for cc in range(3):
            pi = next_ps()
            S.op("pe", lambda e, cc=cc, pi=pi: e.transpose(out=bank(pi)[:, 0:8], in_=OUTS2[0:8, cc * 128:(cc + 1) * 128], identity=IDF[0:8, 0:8]),
                 reads=["outs2", "idf"], writes=[("ps", pi)])
            S.op("act", lambda e, cc=cc, pi=pi: e.copy(out=ZS[:, cc, :, 0:2], in_=bank(pi)[:, 0:8].rearrange("p (n r) -> p n r", n=4)),
                 reads=[("ps", pi)], writes=["zs"])
        S.op("dve", lambda e: e.memset(U[:, :, 0:15], 0.0), writes=[("u", 0), ("u", 1)])
        S.op("dve", lambda e: e.memset(Z[:, :, 0:2], 0.0), writes=[("z", 0), ("z", 1), ("z", 2)])

        def proj(col0, b):
            pi = hold_ps()
            L = blk_len(b)
            for d in range(8):
                S.op("pe", lambda e, d=d, pi=pi, L=L: e.matmul(bank(pi)[:, 0:L], lhsT=WA[:, d, col0:col0 + 128], rhs=HT[:, d, blk_cols(b)],
                                                               start=(d == 0), stop=(d == 7)),
                     reads=["wa"] + [("ht", t) for t in (range(4 * b, 4 * b + 4) if b < 4 else [16])], writes=[("ps", pi)])
            return pi

        def pool_item(b, pc):
            L = blk_len(b)
            smp = (b == 4)
            pi = proj(128 * pc, b)
            yield
            ukey = "us" if smp else ("u", pc)
            k2, k4, k8, k16, kdt, ktm = ("s2", pc), ("s4", pc), ("s8", pc), ("s16", pc), ("dt", pc), ("tmp15", pc)
            if smp:
                Uv = US[:, pc]
                W = 23
                S.op("act", lambda e: e.copy(out=US[:, pc, :, 15:23], in_=bank(pi)[:, 0:32].rearrange("p (n r) -> p n r", n=4)),
                     reads=[("ps", pi)], writes=["us"])

                def v3(ap):
                    return ap[:, 0:92].rearrange("p (n r) -> p n r", n=4)
                dt = DTB[pc][:, 0:32].rearrange("p (n r) -> p n r", n=4)

                def sl(ap, a, b_):
                    return ap[:, :, a:b_]
            else:
                Uv = U[:, pc]
                W = 527
                if b > 0:
                    S.op("pool", lambda e: e.tensor_copy(out=U[:, pc, 0:15], in_=U[:, pc, 512:527]), reads=[("u", pc)], writes=[("u", pc)])
                S.op("act", lambda e: e.copy(out=U[:, pc, 15:527], in_=bank(pi)[:, 0:512]), reads=[("ps", pi)], writes=[("u", pc)])

                def v3(ap):
                    return ap
                dt = DTB[pc]

                def sl(ap, a, b_):
                    return ap[:, a:b_]
            rel_ps(pi)
            s2, s4 = v3(SB[pc][0]), v3(SB[pc][1])
            yield
            S.op("pool", lambda e: e.tensor_tensor(out=sl(s2, 1, W), in0=sl(Uv, 1, W), in1=sl(Uv, 0, W - 1), op=ALU.add), reads=[ukey], writes=[k2])
            yield
            S.op("pool", lambda e: e.tensor_tensor(out=sl(s4, 3, W), in0=sl(s2, 3, W), in1=sl(s2, 1, W - 2), op=ALU.add), reads=[k2], writes=[k4])
            yield
            if pc == 0:
                wlo, whi, klo, khi = s2, s4, k2, k4
            else:
                s8, s16 = v3(SB[pc][2]), v3(SB[pc][3])
                S.op("pool", lambda e: e.tensor_tensor(out=sl(s8, 7, W), in0=sl(s4, 7, W), in1=sl(s4, 3, W - 4), op=ALU.add), reads=[k4], writes=[k8])
                yield
                S.op("pool", lambda e: e.tensor_tensor(out=sl(s16, 15, W), in0=sl(s8, 15, W), in1=sl(s8, 7, W - 8), op=ALU.add), reads=[k8], writes=[k16])
                yield
                wlo, whi, klo, khi = s8, s16, k8, k16
            for (hf, win, wk) in ((0, wlo, klo), (1, whi, khi)):
                ps_ = slice(hf * 64, hf * 64 + 64)
                S.op("dve", lambda e, ps_=ps_, win=win: e.scalar_tensor_tensor(
                    out=dt[ps_], in0=sl(win, 15, W)[ps_], scalar=INVW[ps_, pc:pc + 1], in1=sl(Uv, 15, W)[ps_], op0=ALU.mult, op1=ALU.subtract),
                    reads=[wk, ukey, "invw"], writes=[kdt])
            if b == 0:
                yield
                for (hf, win, wk) in ((0, wlo, klo), (1, whi, khi)):
                    ps_ = slice(hf * 64, hf * 64 + 64)
                    S.op("dve", lambda e, ps_=ps_, win=win: e.tensor_tensor(out=TMP15[pc][ps_, 0:15], in0=win[ps_, 15:30], in1=INVTAB[ps_, pc, :], op=ALU.mult),
                         reads=[wk, "invtab"], writes=[ktm])
                yield
                S.op("dve", lambda e: e.tensor_tensor(out=DTB[pc][:, 0:15], in0=TMP15[pc][:, 0:15], in1=U[:, pc, 15:30], op=ALU.subtract),
                     reads=[ktm, ("u", pc)], writes=[kdt])
            yield
            po = hold_ps()
            S.op("pe", lambda e: e.matmul(bank(po)[:, 0:L], lhsT=PWBD[:, pc, :], rhs=DTB[pc][:, 0:L], start=True, stop=True),
                 reads=["pwbd", kdt], writes=[("ps", po)])
            yield
            S.op("act", lambda e: e.activation(out=MIXT[:, pc, blk_cols(b)], in_=bank(po)[:, 0:L], func=AF.Copy, scale=PSCALE[:, pc:pc + 1]),
                 reads=[("ps", po)] + K_PS, writes=[("mix", pc, b)])
            rel_ps(po)

        def conv_item(b, cc):
            L = blk_len(b)
            smp = (b == 4)
            pgc = proj(256 + 384 + 128 * cc, b)
            pgh = proj(256 + 768 + 128 * cc, b)
            yield
            zkey = "zs" if smp else ("z", cc)
            kg, kc = ("gcs", cc), ("cacc", cc)
            if smp:
                Zv = ZS[:, cc]
                Wz = 10

                def v3(ap):
                    return ap[:, 0:32].rearrange("p (n r) -> p n r", n=4)

                def sz(a, b_, Zv=Zv):
                    return Zv[:, :, a:b_]
            else:
                Zv = Z[:, cc]
                Wz = 514
                if b > 0:
                    S.op("pool", lambda e: e.tensor_copy(out=Z[:, cc, 0:2], in_=Z[:, cc, 512:514]), reads=[("z", cc)], writes=[("z", cc)])

                def v3(ap):
                    return ap[:, 0:512]

                def sz(a, b_, Zv=Zv):
                    return Zv[:, a:b_]
            Lq = Wz - 2
            S.op("act", lambda e: e.copy(out=v3(GCS[cc]), in_=v3(bank(pgc))), reads=[("ps", pgc)], writes=[kg])
            rel_ps(pgc)
            yield
            S.op("dve", lambda e: e.tensor_tensor(out=sz(2, Wz), in0=v3(bank(pgh)), in1=v3(GCS[cc]), op=ALU.mult), reads=[("ps", pgh), kg], writes=[zkey])
            rel_ps(pgh)
            yield
            S.op("pool", lambda e: e.tensor_scalar(out=v3(CACC[cc]), in0=sz(0, Lq), scalar1=CONVW[:, cc, 0:1], scalar2=None, op0=ALU.mult),
                 reads=[zkey] + K_CW, writes=[kc])
            pgb = proj(256 + 128 * cc, b)
            yield
            for k in (1, 2):
                S.op("dve", lambda e, k=k: e.scalar_tensor_tensor(out=v3(CACC[cc]), in0=sz(k, k + Lq), scalar=CONVW[:, cc, k:k + 1], in1=v3(CACC[cc]),
                                                                  op0=ALU.mult, op1=ALU.add), reads=[zkey, kc] + K_CW, writes=[kc])
                yield
            mo = MIXT[:, 3 + cc, blk_cols(b)]
            if smp:
                mo = mo.rearrange("p (n r) -> p n r", n=4)
            S.op("dve", lambda e: e.tensor_tensor(out=mo, in0=v3(bank(pgb)), in1=v3(CACC[cc]), op=ALU.mult), reads=[("ps", pgb), kc], writes=[("mix", 3 + cc, b)])
            rel_ps(pgb)

        items = []
        for b in range(5):
            for pc in range(2):
                items.append((("p", pc), lambda b=b, pc=pc: pool_item(b, pc)))
            for cc in range(3):
                items.append((("c", cc), lambda b=b, cc=cc: conv_item(b, cc)))
        pipeline(items, depth=5)

        for smp in (False, True):
            o1, k1 = (OUTS_S, "outs_s") if smp else (OUTS, "outs")
            o2_, k2 = (OUTS2_S, "outs2_s") if smp else (OUTS2, "outs2")
            for pc in range(2):
                if smp:
                    S.op("pool", lambda e, pc=pc: e.tensor_copy(out=TS32B[:, pc].rearrange("p (n r) -> p n r", n=4), in_=US[:, pc, :, 15:23]), reads=["us"], writes=[("ts32b", pc)])
                    src, sk = TS32B[:, pc], ("ts32b", pc)
                else:
                    src, sk = U[:, pc, 495:527], ("u", pc)
                pi = next_ps()
                S.op("pe", lambda e, src=src, pi=pi: e.transpose(out=bank(pi)[0:32, 0:128], in_=src, identity=IDF), reads=[sk, "idf"], writes=[("ps", pi)])
                S.op("act", lambda e, pc=pc, pi=pi, o1=o1: e.copy(out=o1[0:32, pc * 128:(pc + 1) * 128], in_=bank(pi)[0:32, 0:128]), reads=[("ps", pi)], writes=[k1])
            if smp:
                for n in range(4):
                    S.dma(lambda e, n=n: e.dma_start(out=o_pool_s[l, n, 7:15, :], in_=OUTS_S[8 * n:8 * n + 8, 0:256]), reads=[k1], slot="so1s")
            else:
                S.dma(lambda e: e.dma_start(out=o_pool_p[l], in_=OUTS[17:32, 0:256]), reads=[k1], slot="so1")
            for cc in range(3):
                if smp:
                    S.op("pool", lambda e, cc=cc: e.tensor_copy(out=TS32B[:, 2 + cc].rearrange("p (n r) -> p n r", n=4), in_=ZS[:, cc, :, 2:10]), reads=["zs"], writes=[("ts32b", 2 + cc)])
                    src, sk = TS32B[:, 2 + cc], ("ts32b", 2 + cc)
                else:
                    src, sk = Z[:, cc, 482:514], ("z", cc)
                pi = next_ps()
                S.op("pe", lambda e, src=src, pi=pi: e.transpose(out=bank(pi)[0:32, 0:128], in_=src, identity=IDF), reads=[sk, "idf"], writes=[("ps", pi)])
                S.op("act", lambda e, cc=cc, pi=pi, o2_=o2_: e.copy(out=o2_[0:32, cc * 128:(cc + 1) * 128], in_=bank(pi)[0:32, 0:128]), reads=[("ps", pi)], writes=[k2])
            if smp:
                for n in range(4):
                    S.dma(lambda e, n=n: e.dma_start(out=o_conv_s[l, n], in_=OUTS2_S[8 * n + 6:8 * n + 8, 0:384]), reads=[k2], slot="so2s")
            else:
                S.dma(lambda e: e.dma_start(out=o_conv_p[l], in_=OUTS2[30:32, 0:384]), reads=[k2], slot="so2")

    class ResPool:
        def __init__(self, n):
            self.free = list(range(n))

        def acquire(self):
            while not self.free:
                yield
            return self.free.pop(0)

        def release(self, i):
            self.free.append(i)

    def acq_bank(pref=(0, 1, 2, 3, 4, 5)):
        while True:
            for i in pref:
                if i not in ps_held:
                    ps_held.add(i)
                    return i
            yield

    def acq_pair():
        while True:
            for p0 in (0, 2, 4):
                if p0 not in ps_held and p0 + 1 not in ps_held:
                    ps_held.add(p0)
                    ps_held.add(p0 + 1)
                    return p0
            yield

    def phase_a2b(l):
        for i in range(3):
            S.op("dve", lambda e, i=i: e.memset(QBDG[i], 0.0), writes=[("qbd", i)])
        for i in range(2):
            S.op("dve", lambda e, i=i: e.memset(VS[i], 1.0), writes=[("vs", i)])
        for i in range(2):
            S.op("dve", lambda e, i=i: e.memset(QKB[i], 0.0), writes=[("qkb", i)])
        PSA = (7, 6)
        psa = [bank(PSA[h])[0:32, 0:65] for h in range(2)]
        first_pv = [True, True]
        P_Q, P_SQ, P_S4, P_ROT, P_QKB = ResPool(NQ), ResPool(1), ResPool(4), ResPool(1), ResPool(2)
        P_PB, P_CST, P_KI = ResPool(4), ResPool(2), ResPool(2)

        def qkv_items(g):
            W = WINS[g]
            sg = g % 2
            QKT, VAUG, ROPE = QKT_S[sg], VAUG_S[sg], ROPE_S[sg]
            kq = lambda a, ti: ("qkt", sg, a, ti)
            kv = lambda ti: ("vaug", sg, ti)
            krope = ("rope", sg)
            S.dma(lambda e: e.dma_start(out=ROPE, in_=c_rope[g]), writes=[krope], slot=("rope", sg))
            S.op("dve", lambda e: e.memset(VAUG, 1.0), writes=[kv(ti) for ti in range(NTILE)])

            def qkv_tile(tis):
                J = len(tis)
                R = tile_rows(tis[0])
                t0_ = tis[0]
                pb0 = (yield from acq_pair()) if J == 2 else (yield from acq_bank())
                pqv = psum[:, pb0 * 512:(pb0 + J) * 512].rearrange("p (j c) -> p j c", j=J)
                pkeys = [("ps", pb0 + j) for j in range(J)]
                for j, ti in enumerate(tis):
                    if ti == 16:
                        csel = slice(T, T + 32)
                        htk = [("ht", 16)]
                    elif g == 0:
                        csel = slice(128 * ti, 128 * ti + 128)
                        htk = [("ht", ti)]
                    elif g == 1:
                        r, kb = ti // 4, ti % 4
                        csel = slice(512 * kb + r, 512 * kb + 512, 4)
                        htk = [("ht", 4 * kb + jj) for jj in range(4)]
                    else:
                        csel = slice(ti, T, 16)
                        htk = [("ht", jj) for jj in range(16)]
                    for d in range(8):
                        S.op("pe", lambda e, d=d, j=j, csel=csel: e.matmul(pqv[0:R, j, 0:384], lhsT=HT[:, d, csel], rhs=WQ[g][:, d, :], start=(d == 0), stop=(d == 7)),
                             reads=[("wq", g)] + htk, writes=[pkeys[j]])
                yield
                qi = yield from P_Q.acquire()
                si = yield from P_SQ.acquire()
                qf = QKVF[qi][:, 0:J, :]
                qk = ("qkvf", qi)
                S.op("act", lambda e: e.copy(out=qf[0:R], in_=pqv[0:R, :, 0:384]), reads=pkeys, writes=[qk])
                sq = SQ[si][:, 0:J, :]
                S.op("act", lambda e: e.activation(out=sq[0:R], in_=pqv[0:R, :, 0:256], func=AF.Square), reads=pkeys, writes=[("sq", si)])
                for j in range(J):
                    rel_ps(pb0 + j)
                yield
                s4 = yield from P_S4.acquire()
                ss = SS4[:, s4, 0:4 * J]
                rs = RS4[:, s4, 0:4 * J]
                S.op("dve", lambda e: e.tensor_reduce(out=ss[0:R], in_=sq[0:R].rearrange("p j (a b) -> p (j a) b", a=4), axis=AX.X, op=ALU.add),
                     reads=[("sq", si)], writes=[("ss4", s4)])
                P_SQ.release(si)
                yield
                S.op("act", lambda e: e.activation(out=rs[0:R], in_=ss[0:R], func=AF.Ln, scale=1.0 / 64, bias=EPSC[0:R, :]),
                     reads=[("ss4", s4), "epsc"], writes=[("rs4", s4)])
                yield
                S.op("act", lambda e: e.activation(out=rs[0:R], in_=rs[0:R], func=AF.Exp, scale=-0.5), reads=[("rs4", s4)], writes=[("rs4", s4)])
                yield
                qk4 = qf[:, :, 0:256].rearrange("p j (a b) -> p j a b", a=4)
                rs4b = rs.rearrange("p (j a) -> p j a", j=J).unsqueeze(3)
                S.op("dve", lambda e: e.tensor_tensor(out=qk4[0:R], in0=qk4[0:R], in1=rs4b[0:R].to_broadcast([R, J, 4, 64]), op=ALU.mult),
                     reads=[qk, ("rs4", s4)], writes=[qk])
                P_S4.release(s4)
                yield
                S.op("dve", lambda e: e.tensor_tensor(out=qf[0:R, :, 0:256], in0=qf[0:R, :, 0:256], in1=GQK[0:R].unsqueeze(1).to_broadcast([R, J, 256]), op=ALU.mult),
                     reads=[qk] + K_GQ, writes=[qk])
                yield
                ri = yield from P_ROT.acquire()
                rot = ROT[ri][:, :, 0:J]
                x1 = qk4[0:R, :, :, 0:8]
                x2 = qk4[0:R, :, :, 8:16]
                cs = ROPE[0:R, t0_:t0_ + J, 0]
                sn = ROPE[0:R, t0_:t0_ + J, 1]
                for (k_, a_, b_) in ((0, x1, cs), (1, x2, sn), (2, x2, cs), (3, x1, sn)):
                    S.op("dve", lambda e, k_=k_, a_=a_, b_=b_: e.tensor_tensor(out=rot[0:R, k_], in0=a_, in1=b_, op=ALU.mult), reads=[qk, krope], writes=[("rot", ri)])
                yield
                S.op("dve", lambda e: e.tensor_tensor(out=x1, in0=rot[0:R, 0], in1=rot[0:R, 1], op=ALU.subtract), reads=[("rot", ri)], writes=[qk])
                S.op("dve", lambda e: e.tensor_tensor(out=x2, in0=rot[0:R, 2], in1=rot[0:R, 3], op=ALU.add), reads=[("rot", ri)], writes=[qk])
                P_ROT.release(ri)
                yield
                bi = yield from P_QKB.acquire()
                qb = QKB[bi][:, 0:J]
                qbq = qb.rearrange("p j a c -> p j (a c)").rearrange("p j (a x) -> p j a x", x=192)[:, :, :, 0:64]
                S.op("pool", lambda e: e.tensor_copy(out=qbq[0:R], in_=qf[0:R, :, 0:128].rearrange("p j (a c) -> p j a c", a=2)), reads=[qk], writes=[("qkb", bi)])
                S.op("act", lambda e: e.copy(out=qb[0:R, :, 2, :], in_=qf[0:R, :, 128:256]), reads=[qk], writes=[("qkb", bi)])
                S.op("act", lambda e: e.copy(out=VAUG[0:R, t0_:t0_ + J, :, 0:64], in_=qf[0:R, :, 256:384].rearrange("p j (h c) -> p j h c", h=2)),
                     reads=[qk], writes=[kv(ti) for ti in tis])
                for j, ti in enumerate(tis):
                    need = (ti == 16) or (g == 2) or (g == 1 and ti % 4 == 3) or (g == 0 and ti == 15)
                    if not need:
                        continue
                    if ti == 16:
                        for n in range(4):
                            S.dma(lambda e, n=n, j=j: e.dma_start(out=o_kv_s[g][l, n, W - 8:W, :], in_=qf[8 * n:8 * n + 8, j, 128:384]),
                                  reads=[qk], slot=("kvo", qi))
                    else:
                        if g == 0:
                            dst = o_kv_p[0][l]
                        elif g == 1:
                            dst = o_kv_p[1][l].rearrange("(i r) c -> r i c", r=4)[ti // 4]
                        else:
                            dst = o_kv_p[2][l].rearrange("(i r) c -> r i c", r=16)[ti]
                        S.dma(lambda e, j=j, dst=dst: e.dma_start(out=dst, in_=qf[:, j, 128:384]), reads=[qk], slot=("kvo", qi))
                P_Q.release(qi)
                yield
                pt = yield from acq_bank((5, 4, 3, 2, 1, 0))
                pst = bank(pt)[:, 0:192 * J].bitcast(BF16).rearrange("p (j a b) -> p j a b", j=J, a=3)
                for j in range(J):
                    for a in range(3):
                        S.op("pe", lambda e, a=a, j=j: e.transpose(out=pst[:, j, a, 0:R], in_=qb[0:R, j, a, :], identity=IDB[0:R, 0:R]),
                             reads=[("qkb", bi), "idb"], writes=[("ps", pt)])
                P_QKB.release(bi)
                yield
                if J == 2:
                    qdst = QKT[:, :, 128 * t0_:128 * t0_ + 256].rearrange("p a (j r) -> p a j r", j=2)
                else:
                    qdst = QKT[:, :, 128 * t0_:128 * t0_ + R].unsqueeze(2)
                S.op("act", lambda e: e.copy(out=qdst[:, :, :, 0:R], in_=pst[:, :, :, 0:R].rearrange("p j a r -> p a j r")), reads=[("ps", pt)],
                     writes=[kq(a, ti) for a in range(3) for ti in tis])
                rel_ps(pt)

            return [(lambda tis=tis: qkv_tile(tis)) for tis in ([[2 * i, 2 * i + 1] for i in range(8)] + [[16]])]

        def att_items(g):
            sg = g % 2
            QKT, VAUG = QKT_S[sg], VAUG_S[sg]
            kq = lambda a, ti: ("qkt", sg, a, ti)
            kv = lambda ti: ("vaug", sg, ti)
            ncls = (1, 4, 16)[g]
            nb = 16 // ncls

            def att_unit(r, qb):
                tq = r * nb + qb
                kbs = [(1, qb)] if qb == 0 else [(0, qb - 1), (1, qb)]
                c0 = 256 if qb == 0 else 0
                bi = yield from P_PB.acquire()
                pi = yield from acq_bank()
                pS = bank(pi)
                for (ki, kb) in kbs:
                    tk = r * nb + kb
                    S.op("pe", lambda e, ki=ki, tk=tk: e.matmul(pS[:, ki * 256:ki * 256 + 256], lhsT=QKT[:, 2, 128 * tk:128 * tk + 128],
                                                              rhs=QKT[:, 0:2, 128 * tq:128 * tq + 128], start=True, stop=False),
                         reads=[kq(2, tk), kq(0, tq), kq(1, tq)], writes=[("ps", pi)])
                    S.op("pe", lambda e, ki=ki: e.matmul(pS[:, ki * 256:ki * 256 + 256], lhsT=IDB, rhs=MASKP[:, ki * 256:ki * 256 + 256], start=False, stop=True),
                         reads=["idb", "maskp"], writes=[("ps", pi)])
                yield
                P = PB[bi]
                S.op("act", lambda e: e.activation(out=P[:, c0:512], in_=pS[:, c0:512], func=AF.Exp, scale=0.125), reads=[("ps", pi)], writes=[("pb", bi)])
                rel_ps(pi)
                yield
                po = yield from acq_bank()
                pO = bank(po)
                for h in range(2):
                    hs = slice(64 * h, 64 * h + 64)
                    for idx, (ki, kb) in enumerate(kbs):
                        tk = r * nb + kb
                        S.op("pe", lambda e, hs=hs, ki=ki, h=h, idx=idx, tk=tk: e.matmul(
                            pO[hs, 0:128], lhsT=VAUG[:, tk, h, 0:64], rhs=P[:, (ki * 2 + h) * 128:(ki * 2 + h) * 128 + 128],
                            start=(idx == 0), stop=(idx == len(kbs) - 1)), reads=[kv(tk), ("pb", bi)], writes=[("ps", po)])
                nd = 2 * len(kbs)
                for idx, (ki, kb) in enumerate(kbs):
                    for h in range(2):
                        S.op("pe", lambda e, ki=ki, idx=idx, h=h: e.matmul(pO[:, 128:256], lhsT=HSEL[:, h, :], rhs=P[:, (ki * 2 + h) * 128:(ki * 2 + h) * 128 + 128],
                                                                        start=(idx == 0 and h == 0), stop=(2 * idx + h == nd - 1)), reads=["hsel", ("pb", bi)], writes=[("ps", po)])
                P_PB.release(bi)
                yield
                if g == 0:
                    qsel = slice(128 * qb, 128 * qb + 128)
                elif g == 1:
                    qsel = slice(512 * qb + r, 512 * qb + 512, 4)
                else:
                    qsel = slice(r, T, 16)
                pov = pO[:, 0:256].rearrange("p (a b) -> p a b", a=2)
                if g == 0:
                    S.op("act", lambda e: e.copy(out=ACC[:, :, qsel], in_=pov), reads=[("ps", po)], writes=["acc"])
                else:
                    S.op("dve", lambda e: e.tensor_tensor(out=ACC[:, :, qsel], in0=pov, in1=ACC[:, :, qsel], op=ALU.add), reads=[("ps", po), "acc"], writes=["acc"])
                rel_ps(po)

            return [(lambda r=r, qb=qb: att_unit(r, qb)) for r in range(ncls) for qb in range(nb)]

        def smp_items(g):
            W = WINS[g]
            sg = g % 2
            QKT, VAUG = QKT_S[sg], VAUG_S[sg]
            kq = lambda a, ti: ("qkt", sg, a, ti)
            for h in range(2):
                hs = slice(64 * h, 64 * h + 64)
                S.op("act", lambda e, h=h, hs=hs: e.copy(out=QBDG[g][hs, 32 * h:32 * h + 32], in_=QKT[hs, h, 2048:2080]), reads=[kq(h, 16)], writes=[("qbd", g)])
            pi = next_free_bank()
            S.op("pe", lambda e: e.matmul(bank(pi)[0:32, 0:64], lhsT=QKT[:, 2, 2048:2080], rhs=QBDG[g], start=True, stop=True), reads=[kq(2, 16), ("qbd", g)], writes=[("ps", pi)])
            S.op("act", lambda e: e.activation(out=PN[0:32], in_=bank(pi)[0:32, 0:64], func=AF.Exp, scale=0.125), reads=[("ps", pi)], writes=["pn"])
            S.op("dve", lambda e: e.tensor_tensor(out=PN[0:32], in0=PN[0:32], in1=MASKN[0:32, g, :], op=ALU.mult), reads=["pn", "maskn"], writes=["pn"])
            for h in range(2):
                S.op("pe", lambda e, h=h, st=first_pv[h]: e.matmul(psa[h], lhsT=PN[0:32, 32 * h:32 * h + 32], rhs=VAUG[0:32, 16, h, :], start=st, stop=False),
                     reads=["pn", ("vaug", sg, 16)], writes=[("ps", PSA[h])])
                first_pv[h] = False
            ntile = 8 if g == 2 else W // 128

            def smp_chunk(n, c0):
                nt = min(4, ntile - c0)
                ci = yield from P_CST.acquire()
                cst = CST[ci]
                if g == 2:
                    src = caches[2][l, n].rearrange("(i j) c -> i j c", j=16)[:, c0:c0 + nt, :]
                else:
                    src = caches[g][l, n, 128 * c0:128 * (c0 + nt), :].rearrange("(i p) c -> p i c", p=128)
                S.dma(lambda e: e.dma_start(out=cst[:, 0:nt, :], in_=src), writes=[("cst", ci)], slot=("cst", ci))
                yield
                ki = yield from P_KI.acquire()
                S.op("dve", lambda e: e.tensor_copy(out=KB_[ki][:, 0:nt, :], in_=cst[:, 0:nt, 0:128]), reads=[("cst", ci)], writes=[("kb", ki)])
                S.op("act", lambda e: e.copy(out=VS[ki][:, 0:nt, :, 0:64], in_=cst[:, 0:nt, 128:256].rearrange("p i (h c) -> p i h c", h=2)),
                     reads=[("cst", ci)], writes=[("vs", ki)])
                P_CST.release(ci)
                yield
                pt = yield from acq_bank()
                pst = bank(pt)[:, 0:256].bitcast(BF16).rearrange("p (a b) -> p a b", a=4)
                for i in range(nt):
                    S.op("pe", lambda e, i=i: e.transpose(out=pst[:, i, :], in_=KB_[ki][:, i, :], identity=IDB), reads=[("kb", ki), "idb"], writes=[("ps", pt)])
                yield
                S.op("act", lambda e: e.copy(out=KTS[ki][:, 0:128 * nt], in_=bank(pt)[:, 0:64 * nt].bitcast(BF16)), reads=[("ps", pt)], writes=[("kts", ki)])
                rel_ps(pt)
                yield
                pq_ = yield from acq_bank()
                for i in range(nt):
                    S.op("pe", lambda e, i=i: e.matmul(bank(pq_)[:, 64 * i:64 * i + 64], lhsT=KTS[ki][:, 128 * i:128 * i + 128], rhs=QBDG[g], start=True, stop=True),
                         reads=[("kts", ki), ("qbd", g)], writes=[("ps", pq_)])
                yield
                psb = PSB[ki]
                S.op("act", lambda e: e.activation(out=psb[:, 0:nt, :], in_=bank(pq_)[:, 0:64 * nt].rearrange("p (i c) -> p i c", i=nt), func=AF.Exp, scale=0.125),
                     reads=[("ps", pq_)], writes=[("psb", ki)])
                rel_ps(pq_)
                yield
                if g == 2:
                    S.op("dve", lambda e: e.tensor_tensor(out=psb[:, 0:nt, :], in0=psb[:, 0:nt, :], in1=MASK2[:, n, c0:c0 + nt, :], op=ALU.mult),
                         reads=[("psb", ki), "mask2"], writes=[("psb", ki)])
                else:
                    i0 = 0
                    if c0 == 0:
                        S.op("dve", lambda e: e.tensor_tensor(out=psb[:, 0, :], in0=psb[:, 0, :], in1=MASKS[:, g, n, 0, :], op=ALU.mult), reads=[("psb", ki), "masks"], writes=[("psb", ki)])
                        i0 = 1
                    if nt > i0:
                        S.op("dve", lambda e, i0=i0: e.tensor_tensor(
                            out=psb[:, i0:nt, :], in0=psb[:, i0:nt, :], in1=MASKS[:, g, n, 1:2, :].to_broadcast([128, nt - i0, 64]), op=ALU.mult),
                            reads=[("psb", ki), "masks"], writes=[("psb", ki)])
                yield
                last_chunk = (g == 2 and n == 3 and c0 + nt == ntile)
                for i in range(nt):
                    for h in range(2):
                        S.op("pe", lambda e, i=i, h=h, sp_=(last_chunk and i == nt - 1): e.matmul(
                            psa[h], lhsT=psb[:, i, 32 * h:32 * h + 32], rhs=VS[ki][:, i, h, :], start=False, stop=sp_),
                            reads=[("psb", ki), ("vs", ki)], writes=[("ps", PSA[h])])
                P_KI.release(ki)

            return [(lambda n=n, c0=c0: smp_chunk(n, c0)) for n in range(4) for c0 in range(0, ntile, 4)]

        def next_free_bank():
            for i in (5, 4, 3, 2, 1, 0):
                if i not in ps_held:
                    return i
            raise AssertionError("no free PSUM bank")

        def merge(*lists, rate=None):
            out = []
            tot = max(len(x) for x in lists)
            pos = [0] * len(lists)
            rate = rate or [1] * len(lists)
            for step in range(tot):
                for li, x in enumerate(lists):
                    want = min(len(x), (step + 1) * len(x) * rate[li] // tot)
                    while pos[li] < want:
                        out.append(x[pos[li]])
                        pos[li] += 1
            return out

        pipeline(qkv_items(0), depth=8)
        if stop == "a2b_qkv0":
            return True
        for g in range(3):
            att = att_items(g)
            smp = smp_items(g)
            nxtq = qkv_items(g + 1) if g < 2 else []
            pipeline(merge(att, smp, nxtq, rate=[1, 2, 4]) if nxtq else merge(att, smp, rate=[1, 2]), depth=10)
            assert not ps_held, ps_held
            if stop == "a2b_s%d" % g:
                return True

        for b in range(4):
            cs_ = slice(512 * b, 512 * b + 512)
            S.op("dve", lambda e, cs_=cs_: e.reciprocal(out=ACC[:, 1, cs_], in_=ACC[:, 1, cs_]), reads=["acc"], writes=["acc"])
            S.op("dve", lambda e, cs_=cs_: e.tensor_tensor(out=MIXT[:, 2, cs_], in0=ACC[:, 0, cs_], in1=ACC[:, 1, cs_], op=ALU.mult), reads=["acc"], writes=[("mix", 2, b)])
        for h in range(2):
            S.op("dve", lambda e, h=h: e.tensor_copy(out=RD[0:32, h:h + 1], in_=psa[h][:, 64:65]), reads=[("ps", PSA[h])], writes=["rd"])
        S.op("dve", lambda e: e.reciprocal(out=RD[0:32], in_=RD[0:32]), reads=["rd"], writes=["rd"])
        for h in range(2):
            S.op("dve", lambda e, h=h: e.tensor_scalar(out=YSB[0:32, 64 * h:64 * h + 64], in0=psa[h][:, 0:64], scalar1=RD[0:32, h:h + 1], scalar2=None, op0=ALU.mult),
                 reads=[("ps", PSA[h]), "rd"], writes=["ysb"])
        pt = next_ps()
        pst = bank(pt)[:, 0:16].bitcast(BF16)
        S.op("pe", lambda e, pst=pst: e.transpose(out=pst, in_=YSB[0:32, :], identity=IDB[0:32, 0:32]), reads=["ysb", "idb"], writes=[("ps", pt)])
        S.op("act", lambda e, pst=pst: e.copy(out=MIXT[:, 2, 2048:2080], in_=pst), reads=[("ps", pt)], writes=[("mix", 2, 4)])

    def phase_c(l):
        def pre(t):
            R = tile_rows(t)
            if l == 0:
                src = xp[128 * t:128 * t + 128, :] if t < 16 else xs
            else:
                src = xpark[t, 0:R, :]
            S.dma(lambda e, t=t, R=R, src=src: e.dma_start(out=X[0:R, t, :], in_=src), reads=([("park", t)] if l > 0 else []), writes=[("x", t)], slot=("xl", t))
            b = t // 4
            for hf in range(2):
                pi = next_ps()
                for c in range(6):
                    S.op("pe", lambda e, c=c, pi=pi, R=R, t=t, hf=hf: e.matmul(bank(pi)[0:R, :], lhsT=MIXT[:, c, tile_cols(t)], rhs=WO[:, c, 512 * hf:512 * hf + 512],
                                                                               start=(c == 0), stop=(c == 5)), reads=["wo", ("mix", c, b)], writes=[("ps", pi)])
                S.op("dve", lambda e, pi=pi, R=R, t=t, hf=hf: e.tensor_tensor(out=X[0:R, t, 512 * hf:512 * hf + 512], in0=bank(pi)[0:R, :], in1=X[0:R, t, 512 * hf:512 * hf + 512], op=ALU.add),
                     reads=[("ps", pi), ("x", t)], writes=[("x", t)])
        norm_and_transpose(norm2_g[l], pre=pre)

    def phase_d(l, prefetch_next):
        last = (l == DEPTH - 1)
        if not last:
            S.op("dve", lambda e: e.memset(SS, 0.0), writes=["ss"])
        for fb in range(4):
            i = fb % 2

            def up(b, fb=fb, i=i):
                L = blk_len(b)
                at = AT[b % 2]
                for fc in range(8):
                    pi = next_ps()
                    for d in range(8):
                        S.op("pe", lambda e, d=d, fc=fc, pi=pi, L=L: e.matmul(bank(pi)[:, 0:L], lhsT=WU[i][:, d, 128 * fc:128 * fc + 128], rhs=HT[:, d, blk_cols(b)],
                                                                          start=(d == 0), stop=(d == 7)),
                             reads=[("wu", i)] + [("ht", t) for t in (range(4 * b, 4 * b + 4) if b < 4 else [16])], writes=[("ps", pi)])
                    ri = fc % 2
                    S.op("act", lambda e, pi=pi, L=L, ri=ri: e.activation(out=RL[ri][:, 0:L], in_=bank(pi)[:, 0:L], func=AF.Relu), reads=[("ps", pi)], writes=[("rl", ri)])
                    S.op("dve", lambda e, pi=pi, L=L, ri=ri, at=at, fc=fc: e.tensor_tensor(out=at[:, fc, 0:L], in0=bank(pi)[:, 0:L], in1=RL[ri][:, 0:L], op=ALU.mult),
                         reads=[("ps", pi), ("rl", ri)], writes=[("at", b % 2)])

            def down(b, fb=fb, i=i):
                at = AT[b % 2]
                tiles = range(4 * b, 4 * b + 4) if b < 4 else [16]
                for t in tiles:
                    R = tile_rows(t)
                    lo = 128 * (t % 4) if b < 4 else 0
                    for hf in range(2):
                        pi = next_ps()
                        for fc in range(8):
                            S.op("pe", lambda e, fc=fc, pi=pi, R=R, lo=lo, hf=hf, at=at: e.matmul(bank(pi)[0:R, :], lhsT=at[:, fc, lo:lo + R], rhs=WD[i][:, fc, 512 * hf:512 * hf + 512],
                                                                                          start=(fc == 0), stop=(fc == 7)), reads=[("wd", i), ("at", b % 2)], writes=[("ps", pi)])
                        S.op("dve", lambda e, pi=pi, R=R, t=t, hf=hf: e.tensor_tensor(out=X[0:R, t, 512 * hf:512 * hf + 512], in0=bank(pi)[0:R, :], in1=X[0:R, t, 512 * hf:512 * hf + 512], op=ALU.add),
                             reads=[("ps", pi), ("x", t)], writes=[("x", t)])
                    if fb == 3:
                        if last:
                            dst = yp[128 * t:128 * t + 128, :] if t < 16 else ys
                            S.dma(lambda e, t=t, R=R, dst=dst: e.dma_start(out=dst, in_=X[0:R, t, :]), reads=[("x", t)], slot=("xo", t))
                        else:
                            S.dma(lambda e, t=t, R=R: e.dma_start(out=xpark[t, 0:R, :], in_=X[0:R, t, :]), reads=[("x", t)], writes=[("park", t)], slot=("xo", t))
                            S.op("act", lambda e, t=t, R=R: e.activation(out=HB[t % 2][0:R, :], in_=X[0:R, t, :], func=AF.Square, accum_out=SS[0:R, t:t + 1]),
                                 reads=[("x", t)], writes=[("hb", t % 2), "ss"])

            up(0)
            for b in range(5):
                if b + 1 < 5:
                    up(b + 1)
                down(b)
            if fb == 1 and not last:
                load_gbc(norm1_g[l + 1])
                load_smalls(l + 1)
            if fb + 2 < 4:
                load_ffn(l, fb + 2)
            elif prefetch_next is not None:
                prefetch_next(fb)

    def fin():
        S.emit(es)
        es.close()
        return nc

    if stop == "setup":
        return fin()
    load_wa(0)
    load_wq(0)
    load_wo(0)
    for l in range(DEPTH):
        if l == 0:
            load_smalls(l)
        norm_and_transpose(norm1_g[l], have_ss=(l > 0))
        if stop == "a1":
            return fin()
        phase_a2a(l)
        if stop == "a2a":
            return fin()
        if phase_a2b(l) or stop == "a2b":
            return fin()
        load_ffn(l, 0)
        phase_c(l)
        if stop == "c":
            return fin()
        load_ffn(l, 1)

        def prefetch_next(fb, l=l):
            if l + 1 < DEPTH:
                if fb == 2:
                    load_wa(l + 1)
                if fb == 3:
                    load_wq(l + 1)
                    load_wo(l + 1)
        phase_d(l, prefetch_next)
        if stop == "d":
            return fin()
    S.emit(es)
    es.close()
    return nc


_NC_CACHE = {}


def kernel(x_prompt, x_sample, state_pool, state_conv, cache_kv_w128, cache_kv_w512, cache_kv_w2048,
           norm1_g, w_in, q_norm_g, k_norm_g, pool_w, pool_scale, conv_w, w_out, norm2_g, w_up, w_down):
    f = lambda a: np.ascontiguousarray(np.asarray(a, dtype=np.float32))
    consts = make_consts()
    shared = {
        "norm1_g": f(norm1_g), "w_in": f(w_in), "q_norm_g": f(q_norm_g), "k_norm_g": f(k_norm_g),
        "pool_w": f(pool_w), "pool_scale": f(pool_scale), "conv_w": f(conv_w), "w_out": f(w_out),
        "norm2_g": f(norm2_g), "w_up": f(w_up), "w_down": f(w_down),
    }
    shared.update(consts)
    x_prompt = f(x_prompt); x_sample = f(x_sample)
    state_pool = f(state_pool); state_conv = f(state_conv)
    c128 = f(cache_kv_w128); c512 = f(cache_kv_w512); c2048 = f(cache_kv_w2048)
    in_maps = []
    for c in range(NCORES):
        s = slice(4 * c, 4 * c + 4)
        m = dict(shared)
        m["xp"] = x_prompt[c]
        m["xs"] = np.ascontiguousarray(x_sample[s].reshape(32, D))
        m["spool"] = np.ascontiguousarray(state_pool[:, s])
        m["sconv"] = np.ascontiguousarray(state_conv[:, s])
        m["c128"] = np.ascontiguousarray(c128[:, s].reshape(DEPTH, 4, 128, 256))
        m["c512"] = np.ascontiguousarray(c512[:, s].reshape(DEPTH, 4, 512, 256))
        m["c2048"] = np.ascontiguousarray(c2048[:, s].reshape(DEPTH, 4, 2048, 256))
        in_maps.append(m)
    if "nc" not in _NC_CACHE:
        _NC_CACHE["nc"] = build_program()
    nc = _NC_CACHE["nc"]
    res = run_bass_kernel_spmd(nc, in_maps, core_ids=list(range(NCORES)))
    R = res.results
    cat = lambda k, ax: np.concatenate([np.asarray(r[k]) for r in R], axis=ax)
    y_prompt = np.stack([np.asarray(r["yp"]) for r in R], 0)
    y_sample = np.concatenate([np.asarray(r["ys"]).reshape(4, 8, D) for r in R], 0)
    pool_p = np.stack([np.asarray(r["pool_p"]) for r in R], 1)
    conv_p = np.stack([np.asarray(r["conv_p"]) for r in R], 1)
    kvp = [np.stack([np.asarray(r[k]) for r in R], 1).reshape(DEPTH, NCORES, w, 2, 2, 64)
           for k, w in (("kv128_p", 128), ("kv512_p", 512), ("kv2048_p", 2048))]
    pool_s = cat("pool_s", 1)
    conv_s = cat("conv_s", 1)
    kvs = [cat(k, 1).reshape(DEPTH, 32, w, 2, 2, 64) for k, w in (("kv128_s", 128), ("kv512_s", 512), ("kv2048_s", 2048))]
    outs = (y_prompt, y_sample, pool_p, conv_p, kvp[0], kvp[1], kvp[2], pool_s, conv_s, kvs[0], kvs[1], kvs[2])
    return tuple(np.ascontiguousarray(o, dtype=np.float32) for o in outs)
```

```python
import types
from contextlib import ExitStack

import ml_dtypes
import numpy as np

import concourse.bass as bass
import concourse.mybir as mybir
from concourse.bass_utils import run_bass_kernel_spmd

F32 = mybir.dt.float32
BF16 = mybir.dt.bfloat16
U8 = mybir.dt.uint8
ALU = mybir.AluOpType
AF = mybir.ActivationFunctionType
AX = mybir.AxisListType

NCORES = 8
D = 1024
T = 2048
TS = 32
NTOK = T + TS
NTILE = 17
DEPTH = 2
DFF = 4096
EPS = 1e-6
WINS = (128, 512, 2048)
DILS = (1, 4, 16)
COMPUTE = ("pe", "act", "dve", "pool")


def _snap(fn):
    if fn.__closure__ is None:
        return fn
    cells = tuple(types.CellType(c.cell_contents) for c in fn.__closure__)
    return types.FunctionType(fn.__code__, fn.__globals__, fn.__name__, fn.__defaults__, cells)


class Op:
    __slots__ = ("eng", "fn", "waits", "signal", "sigval", "idx", "dma_slot", "dma_val", "queue", "wait_vals")

    def __init__(self, eng, fn):
        self.eng = eng
        self.fn = _snap(fn)
        self.waits = []
        self.wait_vals = {}
        self.signal = False
        self.sigval = None
        self.idx = None
        self.dma_slot = None
        self.dma_val = None
        self.queue = None


class Region:
    __slots__ = ("name", "lo", "hi", "writer", "readers", "overl")

    def __init__(self, name, lo, hi):
        self.name, self.lo, self.hi = name, lo, hi
        self.writer = None
        self.readers = {}
        self.overl = None


class Sched:
    def __init__(self, nc):
        self.nc = nc
        self.ops = {e: [] for e in COMPUTE + ("sp",)}
        self.regions = {}
        self.phys = []
        self.dma_slots = {}
        self.waited = {}

    def region(self, name, lo=None, hi=None):
        r = self.regions.get(name)
        if r is None:
            r = Region(name, lo, hi)
            self.regions[name] = r
            if lo is not None:
                for o in self.phys:
                    if o.lo < hi and lo < o.hi:
                        if o.overl is None:
                            o.overl = []
                        o.overl.append(r)
                        if r.overl is None:
                            r.overl = []
                        r.overl.append(o)
                self.phys.append(r)
        return r

    def _regs(self, keys):
        out = []
        for k in keys:
            r = self.regions[k]
            out.append(r)
            if r.overl:
                out.extend(r.overl)
        return out

    def _deps(self, op, reads, writes):
        deps = []
        for r in self._regs(reads):
            if r.writer is not None:
                deps.append(r.writer)
        for r in self._regs(writes):
            if r.writer is not None:
                deps.append(r.writer)
            deps.extend(r.readers.values())
        best = {}
        for d in deps:
            if d is op:
                continue
            if d.dma_slot is not None:
                key = ("dma", d.dma_slot)
                v = self.dma_slots[d.dma_slot] - (16 if (op.dma_slot == d.dma_slot) else 0)
            else:
                if d.eng == op.eng and op.dma_slot is None and d.eng == "pe":
                    continue
                key = ("eng", d.eng)
                v = d.idx
            if key not in best or v > best[key][0]:
                best[key] = (v, d)
        wq = op.eng
        for key, (v, d) in best.items():
            wk = (wq, key)
            if self.waited.get(wk, -1) >= v:
                continue
            self.waited[wk] = v
            if d.dma_slot is None:
                d.signal = True
            else:
                op.wait_vals[id(d)] = v
            op.waits.append(d)

    def _commit(self, op, reads, writes):
        tag = ("dma", op.dma_slot) if op.dma_slot is not None else op.eng
        for k in reads:
            self.regions[k].readers[tag] = op
        for k in writes:
            r = self.regions[k]
            r.writer = op
            r.readers = {}

    def op(self, eng, fn, reads=(), writes=()):
        o = Op(eng, fn)
        o.idx = len(self.ops[eng])
        self._deps(o, reads, writes)
        self.ops[eng].append(o)
        self._commit(o, reads, writes)
        return o

    def dma(self, fn, reads=(), writes=(), slot=None, queue="sp"):
        o = Op(queue, fn)
        o.dma_slot = slot
        self.dma_slots[slot] = self.dma_slots.get(slot, 0) + 16
        o.dma_val = self.dma_slots[slot]
        o.idx = len(self.ops[queue])
        self._deps(o, reads, writes)
        self.ops[queue].append(o)
        self._commit(o, reads, writes)
        return o

    def emit(self, es):
        nc = self.nc
        sems = {}
        for e in COMPUTE:
            sems[("eng", e)] = es.enter_context(nc.semaphore("s_" + e))
        for i, s in enumerate(self.dma_slots):
            sems[("dma", s)] = es.enter_context(nc.semaphore("d%d" % i))
        for e in COMPUTE:
            c = 0
            for o in self.ops[e]:
                if o.signal:
                    c += 1
                    o.sigval = c
        final_waits = dict(self.dma_slots)

        def run(engname, eng):
            for o in self.ops[engname]:
                for d in o.waits:
                    if d.dma_slot is not None:
                        eng.wait_ge(sems[("dma", d.dma_slot)], o.wait_vals[id(d)])
                    else:
                        eng.wait_ge(sems[("eng", d.eng)], d.sigval)
                ins = o.fn(eng)
                if o.dma_slot is not None:
                    ins.then_inc(sems[("dma", o.dma_slot)], 16)
                elif o.signal:
                    ins.then_inc(sems[("eng", o.eng)], 1)
            if engname == "sp":
                for s, v in final_waits.items():
                    eng.wait_ge(sems[("dma", s)], v)

        block = es.enter_context(nc.Block())

        @block.tensor
        def _(e):
            run("pe", e)

        @block.scalar
        def _(e):
            run("act", e)

        @block.vector
        def _(e):
            run("dve", e)

        @block.gpsimd
        def _(e):
            run("pool", e)

        @block.sync
        def _(e):
            run("sp", e)


def _tile_positions(g, ti):
    if ti == 16:
        return None
    if g == 0:
        return 128 * ti + np.arange(128)
    if g == 1:
        r, kb = ti // 4, ti % 4
        return 512 * kb + 4 * np.arange(128) + r
    return 16 * np.arange(128) + ti


def make_consts():
    bf = ml_dtypes.bfloat16
    c = {}
    c["ident"] = np.eye(128, dtype=np.float32)
    k = np.arange(128)[:, None]
    q = np.arange(128)[None, :]
    prev = np.where(k >= q, 0.0, -1000.0).astype(np.float32)
    cur = np.where(k <= q, 0.0, -1000.0).astype(np.float32)
    c["maskp"] = np.concatenate([prev, prev, cur, cur], axis=1).astype(bf)
    ms = np.zeros((3, 128, 4, 2, 2, 4, 8), np.float32)
    mn = np.zeros((3, 32, 2, 4, 8), np.float32)
    for g in range(3):
        dil = DILS[g]
        p = np.arange(128)[:, None]
        t = np.arange(8)[None, :]
        base = ((t - p) % dil == 0).astype(np.float32)
        v0 = base * (p >= t)
        for n in range(4):
            for h in range(2):
                ms[g, :, n, 0, h, n, :] = v0
                ms[g, :, n, 1, h, n, :] = base
        for n in range(4):
            for tp in range(8):
                for tq in range(8):
                    if tp <= tq and (tq - tp) % dil == 0:
                        mn[g, n * 8 + tp, :, n, tq] = 1.0
    c["masks"] = ms.reshape(3, 128, 4, 2, 64)[0:2].astype(bf)
    m2 = np.zeros((128, 4, 8, 2, 4, 8), np.float32)
    for n in range(4):
        for j in range(8):
            m2[:, n, j, :, n, j] = 1.0
    c["mask2"] = m2.reshape(128, 4, 8, 64).astype(bf)
    c["maskn"] = mn.reshape(3, 32, 64).astype(bf)
    half = 8
    inv = np.power(np.float32(500000.0), -np.arange(half, dtype=np.float32) / half).astype(np.float32)
    rope = np.zeros((3, 128, 17, 2, 4, 8), np.float32)
    for g in range(3):
        for ti in range(17):
            if ti < 16:
                pos = _tile_positions(g, ti).astype(np.float32)
            else:
                pos = np.zeros(128, np.float32)
                pos[:32] = (8192 + (np.arange(32) % 8)).astype(np.float32)
            ang = (pos[:, None] * inv[None, :]).astype(np.float32)
            rope[g, :, ti, 0] = np.cos(ang)[:, None, :]
            rope[g, :, ti, 1] = np.sin(ang)[:, None, :]
    c["rope"] = rope
    wins = np.array([[2, 4], [8, 16]], np.float32)
    invw = np.zeros((128, 2), np.float32)
    invtab = np.zeros((128, 2, 15), np.float32)
    for pc in range(2):
        for hf in range(2):
            w = wins[pc, hf]
            invw[hf * 64:(hf + 1) * 64, pc] = 1.0 / w
            invtab[hf * 64:(hf + 1) * 64, pc, :] = 1.0 / np.minimum(np.arange(15) + 1, w)
    c["invw"] = invw
    c["invtab"] = invtab
    return c


def build_program(stop=None):
    nc = bass.Bass("TRN2", target_bir_lowering=False)

    def din(name, shape, dt=F32):
        return nc.dram_tensor(name, list(shape), dt, kind="ExternalInput").ap()

    def dout(name, shape):
        return nc.dram_tensor(name, list(shape), F32, kind="ExternalOutput").ap()

    xp = din("xp", [T, D])
    xs = din("xs", [TS, D])
    spool = din("spool", [DEPTH, 4, 15, 256])
    sconv = din("sconv", [DEPTH, 4, 2, 384])
    caches = [din("c128", [DEPTH, 4, 128, 256]), din("c512", [DEPTH, 4, 512, 256]), din("c2048", [DEPTH, 4, 2048, 256])]
    norm1_g = din("norm1_g", [DEPTH, D])
    w_in = din("w_in", [DEPTH, D, 2560])
    q_norm_g = din("q_norm_g", [DEPTH, 64])
    k_norm_g = din("k_norm_g", [DEPTH, 64])
    pool_w = din("pool_w", [DEPTH, 4, 64, 64])
    pool_scale = din("pool_scale", [DEPTH, 256])
    conv_w = din("conv_w", [DEPTH, 3, 384])
    w_out = din("w_out", [DEPTH, 768, D])
    norm2_g = din("norm2_g", [DEPTH, D])
    w_up = din("w_up", [DEPTH, D, DFF])
    w_down = din("w_down", [DEPTH, DFF, D])
    c_ident = din("ident", [128, 128])
    c_maskp = din("maskp", [128, 512], BF16)
    c_masks = din("masks", [2, 128, 4, 2, 64], BF16)
    c_mask2 = din("mask2", [128, 4, 8, 64], BF16)
    c_maskn = din("maskn", [3, 32, 64], BF16)
    c_rope = din("rope", [3, 128, 17, 2, 4, 8])
    c_invw = din("invw", [128, 2])
    c_invtab = din("invtab", [128, 2, 15])

    yp = dout("yp", [T, D])
    ys = dout("ys", [TS, D])
    o_pool_p = dout("pool_p", [DEPTH, 15, 256])
    o_conv_p = dout("conv_p", [DEPTH, 2, 384])
    o_kv_p = [dout("kv128_p", [DEPTH, 128, 256]), dout("kv512_p", [DEPTH, 512, 256]), dout("kv2048_p", [DEPTH, 2048, 256])]
    o_pool_s = dout("pool_s", [DEPTH, 4, 15, 256])
    o_conv_s = dout("conv_s", [DEPTH, 4, 2, 384])
    o_kv_s = [dout("kv128_s", [DEPTH, 4, 128, 256]), dout("kv512_s", [DEPTH, 4, 512, 256]), dout("kv2048_s", [DEPTH, 4, 2048, 256])]
    xpark = nc.dram_tensor("xpark", [NTILE, 128, D], F32, kind="Internal").ap()

    es = ExitStack()
    S = Sched(nc)
    NB = 212800
    big = es.enter_context(nc.sbuf_tensor("big", [128, NB], U8))
    psum = es.enter_context(nc.psum_tensor("psum", [128, 8 * 512], F32))

    def bank(i):
        return psum[:, i * 512:(i + 1) * 512]

    for i in range(8):
        S.region(("ps", i))
    ps_rr = [0]

    ps_held = set()

    def next_ps():
        for _ in range(8):
            i = ps_rr[0]
            ps_rr[0] = (i + 1) % 6
            if i not in ps_held:
                return i
        raise AssertionError("no free PSUM bank")

    def hold_ps():
        i = next_ps()
        ps_held.add(i)
        return i

    def rel_ps(i):
        ps_held.discard(i)

    def hold_ps_pref(order):
        for i in order:
            if i not in ps_held:
                ps_held.add(i)
                return i
        raise AssertionError("no free PSUM bank")

    def hold_ps_pair():
        for p0 in (0, 2, 4):
            if p0 not in ps_held and p0 + 1 not in ps_held:
                ps_held.add(p0)
                ps_held.add(p0 + 1)
                return p0
        raise AssertionError("no free PSUM bank pair")

    def carve(name, off, shape, dt):
        esz = 4 if dt == F32 else 2
        n = int(np.prod(shape[1:])) * esz
        assert off + n <= NB, (name, off, n)
        a = big[:, off:off + n].bitcast(dt)
        if len(shape) == 3:
            a = a.rearrange("p (a b) -> p a b", a=shape[1])
        elif len(shape) == 4:
            a = a.rearrange("p (a b c) -> p a b c", a=shape[1], b=shape[2])
        elif len(shape) == 5:
            a = a.rearrange("p (a b c d) -> p a b c d", a=shape[1], b=shape[2], c=shape[3])
        return a, off + n

    def reg(name, off, nbytes):
        S.region(name, off, off + nbytes)

    off = 0
    IDB, off = carve("idb", off, [128, 128], BF16); reg("idb", off - 256, 256)
    IDF, off = carve("idf", off, [128, 128], F32); reg("idf", off - 512, 512)
    ONESB, off = carve("onesb", off, [128, 128], BF16); reg("onesb", off - 256, 256)
    EPSC, off = carve("epsc", off, [128, 1], F32); reg("epsc", off - 4, 4)
    off = (off + 63) // 64 * 64
    HSEL, off = carve("hsel", off, [128, 2, 128], BF16); reg("hsel", off - 512, 512)
    off = (off + 63) // 64 * 64
    MASKP, off = carve("maskp", off, [128, 512], BF16); reg("maskp", off - 1024, 1024)
    MASKS, off = carve("masks", off, [128, 2, 4, 2, 64], BF16); reg("masks", off - 2048, 2048)
    MASK2, off = carve("mask2", off, [128, 4, 8, 64], BF16); reg("mask2", off - 4096, 4096)
    MASKN, off = carve("maskn", off, [128, 3, 64], BF16); reg("maskn", off - 384, 384)
    INVW, off = carve("invw", off, [128, 2], F32); reg("invw", off - 8, 8)
    INVTAB, off = carve("invtab", off, [128, 2, 15], F32); reg("invtab", off - 120, 120)
    PWBD, off = carve("pwbd", off, [128, 2, 128], BF16); reg("pwbd", off - 512, 512)
    PSCALE, off = carve("pscale", off, [128, 2], F32)
    CONVW, off = carve("convw", off, [128, 3, 3], F32)
    GQK, off = carve("gqk", off, [128, 256], F32)
    K_PS = [("pscale", i) for i in range(2)]
    K_CW = [("convw", i) for i in range(9)]
    K_GQ = [("gqk", i) for i in range(4)]
    for k_ in K_PS + K_CW + K_GQ:
        S.region(k_)
    GBC, off = carve("gbc", off, [128, 1024], F32); reg("gbc", off - 4096, 4096)
    SS, off = carve("ss", off, [128, 17], F32); reg("ss", off - 68, 68)
    RSTD, off = carve("rstd", off, [128, 17], F32); reg("rstd", off - 68, 68)
    off = (off + 63) // 64 * 64
    HB_OFF = off
    HB = []
    for i in range(2):
        a, off = carve("hb", off, [128, 1024], BF16); reg(("hb", i), off - 2048, 2048)
        HB.append(a)
    ARENA_OFF = off
    ARENA, off = carve("arena", off, [128, 32768], BF16)
    HT, off = carve("ht", off, [128, 8, NTOK], BF16)
    for t in range(NTILE):
        S.region(("ht", t))
    X_OFF = off
    X, off = carve("x", off, [128, NTILE, D], F32)
    for t in range(NTILE):
        reg(("x", t), X_OFF + t * 4096, 4096)
    MIX_OFF = off
    MIXT, off = carve("mixt", off, [128, 6, NTOK], BF16)
    for c in range(6):
        for b in range(5):
            lo = MIX_OFF + (c * NTOK + 512 * b) * 2
            reg(("mix", c, b), lo, lo + (1024 if b < 4 else 64) - lo + lo - lo)
    for c in range(6):
        for b in range(5):
            r = S.regions[("mix", c, b)]
            r.lo = MIX_OFF + (c * NTOK + 512 * b) * 2
            r.hi = r.lo + (1024 if b < 4 else 64)
    assert off <= NB, off
    TAIL_OFF = off

    o2 = X_OFF
    U, o2 = carve("u", o2, [128, 2, 527], F32)
    for pc in range(2):
        reg(("u", pc), X_OFF + pc * 527 * 4, 527 * 4)
    SB = []
    for pc in range(2):
        lst = []
        for nm in (("s2", "s4") if pc == 0 else ("s2", "s4", "s8", "s16")):
            a_, o2 = carve(nm, o2, [128, 527], F32); reg((nm, pc), o2 - 2108, 2108)
            lst.append(a_)
        SB.append(lst)
    TMP15, DTB = [], []
    for pc in range(2):
        a_, o2 = carve("tmp15", o2, [128, 16], F32); reg(("tmp15", pc), o2 - 64, 64)
        TMP15.append(a_)
    o2 = (o2 + 63) // 64 * 64
    for pc in range(2):
        a_, o2 = carve("dt", o2, [128, 512], BF16); reg(("dt", pc), o2 - 1024, 1024)
        DTB.append(a_)
    ZOFF = o2
    Z, o2 = carve("z", o2, [128, 3, 514], F32)
    for cc in range(3):
        reg(("z", cc), ZOFF + cc * 514 * 4, 514 * 4)
    GCS, CACC = [], []
    for cc in range(3):
        a_, o2 = carve("gcs", o2, [128, 512], F32); reg(("gcs", cc), o2 - 2048, 2048)
        GCS.append(a_)
        a_, o2 = carve("cacc", o2, [128, 512], F32); reg(("cacc", cc), o2 - 2048, 2048)
        CACC.append(a_)
    US, o2 = carve("us", o2, [128, 2, 4, 23], F32); reg("us", o2 - 736, 736)
    ZS, o2 = carve("zs", o2, [128, 3, 4, 10], F32); reg("zs", o2 - 480, 480)
    TS32, o2 = carve("ts32", o2, [128, 32], F32); reg("ts32", o2 - 128, 128)
    STF, o2 = carve("stf", o2, [128, 384], F32); reg("stf", o2 - 1536, 1536)
    OUTS, o2 = carve("outs", o2, [128, 384], F32); reg("outs", o2 - 1536, 1536)
    OUTS2, o2 = carve("outs2", o2, [128, 384], F32); reg("outs2", o2 - 1536, 1536)
    OUTS_S, o2 = carve("outs_s", o2, [128, 384], F32); reg("outs_s", o2 - 1536, 1536)
    OUTS2_S, o2 = carve("outs2_s", o2, [128, 384], F32); reg("outs2_s", o2 - 1536, 1536)
    TS32B, o2 = carve("ts32b", o2, [128, 5, 32], F32)
    for i in range(5):
        reg(("ts32b", i), o2 - 640 + 128 * i, 128)
    assert o2 <= X_OFF + NTILE * 4096

    o3 = X_OFF
    QKT_S, VAUG_S, ROPE_S = [None, None], [None, None], [None, None]
    QKT_S[0], o3 = carve("qkt", o3, [128, 3, NTOK], BF16)
    VOFF = o3
    VAUG_S[0], o3 = carve("vaug", o3, [128, NTILE, 2, 65], BF16)
    o3 = (o3 + 63) // 64 * 64
    ROFF = o3
    ROPE_S[0], o3 = carve("rope", o3, [128, 17, 2, 4, 8], F32)
    oa = ARENA_OFF
    Q1OFF = oa
    QKT_S[1], oa = carve("qkt1", oa, [128, 3, NTOK], BF16)
    V1OFF = oa
    VAUG_S[1], oa = carve("vaug1", oa, [128, NTILE, 2, 65], BF16)
    oa = (oa + 63) // 64 * 64
    R1OFF = oa
    ROPE_S[1], oa = carve("rope1", oa, [128, 17, 2, 4, 8], F32)
    assert oa <= ARENA_OFF + 11264 * 2, oa
    for sg, (qo, vo, ro) in enumerate(((X_OFF, VOFF, ROFF), (Q1OFF, V1OFF, R1OFF))):
        for a in range(3):
            for ti in range(NTILE):
                reg(("qkt", sg, a, ti), qo + (a * NTOK + ti * 128) * 2, 256 if ti < 16 else 64)
        for ti in range(NTILE):
            reg(("vaug", sg, ti), vo + ti * 260, 260)
        reg(("rope", sg), ro, 4352)
    AOFF = o3
    ACC, o3 = carve("acc", o3, [128, 2, T], F32); reg("acc", AOFF, o3 - AOFF)
    QKVF = []
    for i in range(3):
        a_, o3 = carve("qkvf", o3, [128, 2, 384], F32); reg(("qkvf", i), o3 - 3072, 3072)
        QKVF.append(a_)
    oq = ARENA_OFF + 28672 * 2
    for i in range(2):
        a_, oq = carve("qkvf", oq, [128, 2, 384], F32); reg(("qkvf", 3 + i), oq - 3072, 3072)
        QKVF.append(a_)
    NQ = len(QKVF)
    SQ = []
    for i in range(1):
        a_, o3 = carve("sq", o3, [128, 2, 256], F32); reg(("sq", i), o3 - 2048, 2048)
        SQ.append(a_)
    S4OFF = o3
    SS4, o3 = carve("ss4", o3, [128, 4, 8], F32)
    R4OFF = o3
    RS4, o3 = carve("rs4", o3, [128, 4, 8], F32)
    for i in range(4):
        reg(("ss4", i), S4OFF + 32 * i, 32)
        reg(("rs4", i), R4OFF + 32 * i, 32)
    ROT = []
    for i in range(1):
        a_, o3 = carve("rot", o3, [128, 4, 2, 4, 8], F32); reg(("rot", i), o3 - 1024, 1024)
        ROT.append(a_)
    QKB = []
    for i in range(2):
        a_, o3 = carve("qkb", o3, [128, 2, 3, 128], BF16); reg(("qkb", i), o3 - 1536, 1536)
        QKB.append(a_)
    CST = []
    for i in range(2):
        a_, o3 = carve("cst", o3, [128, 4, 256], F32); reg(("cst", i), o3 - 4096, 4096)
        CST.append(a_)
    KB_, VS, KTS, PSB = [], [], [], []
    for i in range(2):
        a_, o3 = carve("kb", o3, [128, 4, 128], BF16); reg(("kb", i), o3 - 1024, 1024)
        KB_.append(a_)
        a_, o3 = carve("vs", o3, [128, 4, 2, 65], BF16); reg(("vs", i), o3 - 1040, 1040)
        VS.append(a_)
        o3 = (o3 + 63) // 64 * 64
        a_, o3 = carve("kts", o3, [128, 512], BF16); reg(("kts", i), o3 - 1024, 1024)
        KTS.append(a_)
        a_, o3 = carve("psb", o3, [128, 4, 64], BF16); reg(("psb", i), o3 - 512, 512)
        PSB.append(a_)
    QBDG = []
    for i in range(3):
        a_, o3 = carve("qbd", o3, [128, 64], BF16); reg(("qbd", i), o3 - 128, 128)
        QBDG.append(a_)
    PN, o3 = carve("pn", o3, [128, 64], BF16); reg("pn", o3 - 128, 128)
    RD, o3 = carve("rd", o3, [128, 2], F32); reg("rd", o3 - 8, 8)
    o3 = (o3 + 63) // 64 * 64
    YSB, o3 = carve("ysb", o3, [128, 128], BF16); reg("ysb", o3 - 256, 256)
    assert o3 <= X_OFF + NTILE * 4096, (o3 - X_OFF - NTILE * 4096)
    PB = []
    for i in range(4):
        a_, _ = carve("pb", HB_OFF + 1024 * i, [128, 512], BF16); reg(("pb", i), HB_OFF + 1024 * i, 1024)
        PB.append(a_)

    AT = []
    o4 = MIX_OFF
    for i in range(2):
        a, o4 = carve("at", o4, [128, 8, 512], BF16); reg(("at", i), o4 - 8192, 8192)
        AT.append(a)
    RL = []
    for i in range(2):
        a, o4 = carve("rl", o4, [128, 512], F32); reg(("rl", i), o4 - 2048, 2048)
        RL.append(a)
    assert o4 <= MIX_OFF + 6 * NTOK * 2

    def arena_piece(name, col, shape):
        n = int(np.prod(shape[1:]))
        a = ARENA[:, col:col + n]
        if len(shape) == 3:
            a = a.rearrange("p (a b) -> p a b", a=shape[1])
        reg(name, ARENA_OFF + col * 2, n * 2)
        return a

    WA = arena_piece("wa", 0, [128, 8, 1408])
    WQ = [arena_piece(("wq", g), 12288 + 3072 * g, [128, 8, 384]) for g in range(3)]
    WO = arena_piece("wo", 22528, [128, 6, 1024])
    WU = [arena_piece(("wu", i), 16384 * i, [128, 8, 1024]) for i in range(2)]
    WD = [arena_piece(("wd", i), 16384 * i + 8192, [128, 8, 1024]) for i in range(2)]

    for t in range(NTILE):
        S.region(("park", t))

    def tile_rows(t):
        return 128 if t < 16 else 32

    def tile_cols(t):
        return slice(128 * t, 128 * t + tile_rows(t))

    def blk_cols(b):
        return slice(512 * b, 512 * b + (512 if b < 4 else 32))

    def blk_len(b):
        return 512 if b < 4 else 32

    S.dma(lambda e: e.dma_start(out=IDF, in_=c_ident), writes=["idf"], slot="c0")
    S.dma(lambda e: e.dma_start(out=MASKP, in_=c_maskp), writes=["maskp"], slot="c1")
    S.dma(lambda e: e.dma_start(out=MASKS, in_=c_masks.rearrange("g p n v c -> p g n v c")), writes=["masks"], slot="c2")
    S.dma(lambda e: e.dma_start(out=MASKN[0:32], in_=c_maskn.rearrange("g p c -> p g c")), writes=["maskn"], slot="c3")
    S.dma(lambda e: e.dma_start(out=INVW, in_=c_invw), writes=["invw"], slot="c4")
    S.dma(lambda e: e.dma_start(out=INVTAB, in_=c_invtab), writes=["invtab"], slot="c5")
    S.op("dve", lambda e: e.tensor_copy(out=IDB, in_=IDF), reads=["idf"], writes=["idb"])
    S.op("dve", lambda e: e.memset(ONESB, 1.0), writes=["onesb"])
    S.op("dve", lambda e: e.memset(EPSC, EPS), writes=["epsc"])
    S.op("dve", lambda e: e.memset(HSEL, 0.0), writes=["hsel"])
    for h in range(2):
        S.op("dve", lambda e, h=h: e.memset(HSEL[:, h, 64 * h:64 * h + 64], 1.0), writes=["hsel"])
    S.dma(lambda e: e.dma_start(out=MASK2, in_=c_mask2), writes=["mask2"], slot="c6")
    S.op("dve", lambda e: e.memset(PWBD, 0.0), writes=["pwbd"])
    for g in range(3):
        W = WINS[g]
        S.dma(lambda e, g=g, W=W: e.dma_start(out=o_kv_s[g][:, :, 0:W - 8, :], in_=caches[g][:, :, 8:W, :]),
              slot=("c2c", g))
    S.dma(lambda e: e.dma_start(out=o_pool_s[:, :, 0:7, :], in_=spool[:, :, 8:15, :]), slot="c2cp")
    for t in range(NTILE):
        src = xp[128 * t:128 * t + 128, :] if t < 16 else xs
        R = tile_rows(t)
        S.dma(lambda e, t=t, src=src, R=R: e.dma_start(out=X[0:R, t, :], in_=src), writes=[("x", t)], slot=("xl", t))

    wq_state = {"n": 0}

    def wdma(fn, writes, slot):
        S.dma(fn, writes=writes, slot=slot, queue="pool")

    def load_wa(l):
        wv = w_in[l].rearrange("(c p) n -> p c n", p=128)
        wdma(lambda e: e.dma_start(out=WA[:, :, 0:256], in_=wv[:, :, 0:256]), ["wa"], "wa")
        for h in range(4):
            wdma(lambda e, h=h: e.dma_start(out=WA[:, 2 * h:2 * h + 2, 256:1408], in_=wv[:, 2 * h:2 * h + 2, 1408:2560]), ["wa"], "wa")

    def load_wq(l):
        wv = w_in[l].rearrange("(c p) n -> p c n", p=128)
        for g in range(3):
            for j in range(3):
                c0 = 256 + 384 * j + 128 * g
                wdma(lambda e, g=g, j=j, c0=c0: e.dma_start(out=WQ[g][:, :, 128 * j:128 * j + 128], in_=wv[:, :, c0:c0 + 128]),
                     [("wq", g)], ("wq", g))

    def load_wo(l):
        wv = w_out[l].rearrange("(c p) n -> p c n", p=128)
        for h in range(3):
            wdma(lambda e, h=h: e.dma_start(out=WO[:, 2 * h:2 * h + 2, :], in_=wv[:, 2 * h:2 * h + 2, :]), ["wo"], "wo")

    def load_ffn(l, fb):
        i = fb % 2
        wu = w_up[l].rearrange("(c p) f -> p c f", p=128)
        wd = w_down[l][fb * 1024:(fb + 1) * 1024, :].rearrange("(c p) n -> p c n", p=128)
        for h in range(2):
            wdma(lambda e, h=h: e.dma_start(out=WU[i][:, 4 * h:4 * h + 4, :], in_=wu[:, 4 * h:4 * h + 4, fb * 1024:(fb + 1) * 1024]),
                 [("wu", i)], ("wu", i))
        for h in range(2):
            wdma(lambda e, h=h: e.dma_start(out=WD[i][:, 4 * h:4 * h + 4, :], in_=wd[:, 4 * h:4 * h + 4, :]),
                 [("wd", i)], ("wd", i))

    def load_smalls(l):
        for g in range(4):
            pc, hf = g // 2, g % 2
            wdma(lambda e, g=g, pc=pc, hf=hf: e.dma_start(out=PWBD[hf * 64:(hf + 1) * 64, pc, hf * 64:(hf + 1) * 64], in_=pool_w[l, g]),
                 ["pwbd"], "pwbd")
        for pc in range(2):
            S.dma(lambda e, pc=pc: e.dma_start(out=PSCALE[:, pc:pc + 1], in_=pool_scale[l, pc * 128:(pc + 1) * 128].rearrange("(p o) -> p o", o=1)),
                  writes=[("pscale", pc)], slot=("sm_ps", pc))
        for cc in range(3):
            for k in range(3):
                S.dma(lambda e, cc=cc, k=k: e.dma_start(out=CONVW[:, cc, k:k + 1], in_=conv_w[l, k, cc * 128:(cc + 1) * 128].rearrange("(p o) -> p o", o=1)),
                      writes=[("convw", 3 * cc + k)], slot=("sm_cw", 3 * cc + k))
        for j in range(4):
            src = q_norm_g if j < 2 else k_norm_g
            S.dma(lambda e, j=j, src=src: e.dma_start(out=GQK[:, 64 * j:64 * j + 64], in_=src[l].partition_broadcast(128)),
                  writes=[("gqk", j)], slot=("sm_gq", j))

    def load_gbc(gvec):
        S.dma(lambda e: e.dma_start(out=GBC, in_=gvec.partition_broadcast(128)), writes=["gbc"], slot="gbc")

    def norm_and_transpose(gvec, pre=None, have_ss=False):
        if not have_ss:
            load_gbc(gvec)
        if not have_ss:
            S.op("dve", lambda e: e.memset(SS, 0.0), writes=["ss"])
        for t in range(NTILE):
            if have_ss:
                break
            R = tile_rows(t)
            if pre is not None:
                pre(t)
            hb = HB[t % 2]
            S.op("act", lambda e, t=t, R=R, hb=hb: e.activation(out=hb[0:R, :], in_=X[0:R, t, :], func=AF.Square, accum_out=SS[0:R, t:t + 1]),
                 reads=[("x", t)], writes=[("hb", t % 2), "ss"])
        S.op("dve", lambda e: e.tensor_scalar(out=RSTD, in0=SS, scalar1=1.0 / D, scalar2=EPS, op0=ALU.mult, op1=ALU.add),
             reads=["ss"], writes=["rstd"])
        S.op("act", lambda e: e.activation(out=RSTD, in_=RSTD, func=AF.Ln), reads=["rstd"], writes=["rstd"])
        S.op("act", lambda e: e.activation(out=RSTD, in_=RSTD, func=AF.Exp, scale=-0.5), reads=["rstd"], writes=["rstd"])
        for t in range(NTILE):
            R = tile_rows(t)
            hb = HB[t % 2]
            S.op("dve", lambda e, t=t, R=R, hb=hb: e.scalar_tensor_tensor(out=hb[0:R, :], in0=X[0:R, t, :], scalar=RSTD[0:R, t:t + 1],
                                                                       in1=GBC[0:R, :], op0=ALU.mult, op1=ALU.mult),
                 reads=[("x", t), "rstd", "gbc"], writes=[("hb", t % 2)])
            pi = next_ps()
            pst = bank(pi)[:, 0:512].bitcast(BF16).rearrange("p (a b) -> p a b", a=8)
            for c in range(8):
                S.op("pe", lambda e, c=c, R=R, hb=hb, pst=pst: e.transpose(out=pst[:, c, 0:R], in_=hb[0:R, c * 128:(c + 1) * 128], identity=IDB[0:R, 0:R]),
                     reads=[("hb", t % 2), "idb"], writes=[("ps", pi)])
            S.op("act", lambda e, t=t, R=R, pst=pst: e.copy(out=HT[:, :, 128 * t:128 * t + R], in_=pst[:, :, 0:R]),
                 reads=[("ps", pi)], writes=[("ht", t)])

    def pipeline(items, lag=1, depth=4):
        active = []
        pending = list(items)
        tick = 0
        while pending or active:
            assert tick < 200000, "pipeline deadlock"

            started = None
            if pending and len(active) < depth:
                nxt_item = pending[0]
                key_ = nxt_item[0] if isinstance(nxt_item, tuple) else None
                if key_ is None or all(a_[2] != key_ for a_ in active):
                    pending.pop(0)
                    g_ = (nxt_item[1] if isinstance(nxt_item, tuple) else nxt_item)()
                    try:
                        next(g_)
                        started = [g_, tick + lag, key_]
                    except StopIteration:
                        pass
            for a_ in list(active):
                if a_[1] <= tick:
                    try:
                        next(a_[0])
                        a_[1] = tick + lag
                    except StopIteration:
                        active.remove(a_)
            if started:
                active.append(started)
            tick += 1

    def phase_a2a(l):
        S.dma(lambda e: e.dma_start(out=STF[0:60, 0:256], in_=spool[l].rearrange("n r c -> (n r) c")), writes=["stf"], slot="st1")
        for pc in range(2):
            pi = next_ps()
            S.op("pe", lambda e, pc=pc, pi=pi: e.transpose(out=bank(pi)[:, 0:60], in_=STF[0:60, pc * 128:(pc + 1) * 128], identity=IDF[0:60, 0:60]),
                 reads=["stf", "idf"], writes=[("ps", pi)])
            S.op("act", lambda e, pc=pc, pi=pi: e.copy(out=US[:, pc, :, 0:15], in_=bank(pi)[:, 0:60].rearrange("p (n r) -> p n r", n=4)),
                 reads=[("ps", pi)], writes=["us"])
        S.dma(lambda e: e.dma_start(out=OUTS2[0:8, 0:384], in_=sconv[l].rearrange("n r c -> (n r) c")), writes=["outs2"], slot="st2")
        for cc in range(3):
            pi = next_ps()
            S.op("pe", lambda e, cc=cc, pi=pi: e.transpose(out=bank(pi)[:, 0:8], in_=OUTS2[0:8, cc * 128:(cc + 1) * 128], identity=IDF[0:8, 0:8]),
                 reads=["outs2", "idf"], writes=[("ps", pi)])
            S.op("act", lambda e, cc=cc, pi=pi: e.copy(out=ZS[:, cc, :, 0:2], in_=bank(pi)[:, 0:8].rearrange("p (n r) -> p n r", n=4)),
                 reads=[("ps", pi)], writes=["zs"])
        S.op("dve", lambda e: e.memset(U[:, :, 0:15], 0.0), writes=[("u", 0), ("u", 1)])
        S.op("dve", lambda e: e.memset(Z[:, :, 0:2], 0.0), writes=[("z", 0), ("z", 1), ("z", 2)])

        def proj(col0, b):
            pi = hold_ps()
            L = blk_len(b)
            for d in range(8):
                S.op("pe", lambda e, d=d, pi=pi, L=L: e.matmul(bank(pi)[:, 0:L], lhsT=WA[:, d, col0:col0 + 128], rhs=HT[:, d, blk_cols(b)],
                                                               start=(d == 0), stop=(d == 7)),
                     reads=["wa"] + [("ht", t) for t in (range(4 * b, 4 * b + 4) if b < 4 else [16])], writes=[("ps", pi)])
            return pi

        def pool_item(b, pc):
            L = blk_len(b)
            smp = (b == 4)
            pi = proj(128 * pc, b)
            yield
            ukey = "us" if smp else ("u", pc)
            k2, k4, k8, k16, kdt, ktm = ("s2", pc), ("s4", pc), ("s8", pc), ("s16", pc), ("dt", pc), ("tmp15", pc)
            if smp:
                Uv = US[:, pc]
                W = 23
                S.op("act", lambda e: e.copy(out=US[:, pc, :, 15:23], in_=bank(pi)[:, 0:32].rearrange("p (n r) -> p n r", n=4)),
                     reads=[("ps", pi)], writes=["us"])

                def v3(ap):
                    return ap[:, 0:92].rearrange("p (n r) -> p n r", n=4)
                dt = DTB[pc][:, 0:32].rearrange("p (n r) -> p n r", n=4)

                def sl(ap, a, b_):
                    return ap[:, :, a:b_]
            else:
                Uv = U[:, pc]
                W = 527
                if b > 0:
                    S.op("pool", lambda e: e.tensor_copy(out=U[:, pc, 0:15], in_=U[:, pc, 512:527]), reads=[("u", pc)], writes=[("u", pc)])
                S.op("act", lambda e: e.copy(out=U[:, pc, 15:527], in_=bank(pi)[:, 0:512]), reads=[("ps", pi)], writes=[("u", pc)])

                def v3(ap):
                    return ap
                dt = DTB[pc]

                def sl(ap, a, b_):
                    return ap[:, a:b_]
            rel_ps(pi)
            s2, s4 = v3(SB[pc][0]), v3(SB[pc][1])
            yield
            S.op("pool", lambda e: e.tensor_tensor(out=sl(s2, 1, W), in0=sl(Uv, 1, W), in1=sl(Uv, 0, W - 1), op=ALU.add), reads=[ukey], writes=[k2])
            yield
            S.op("pool", lambda e: e.tensor_tensor(out=sl(s4, 3, W), in0=sl(s2, 3, W), in1=sl(s2, 1, W - 2), op=ALU.add), reads=[k2], writes=[k4])
            yield
            if pc == 0:
                wlo, whi, klo, khi = s2, s4, k2, k4
            else:
                s8, s16 = v3(SB[pc][2]), v3(SB[pc][3])
                S.op("pool", lambda e: e.tensor_tensor(out=sl(s8, 7, W), in0=sl(s4, 7, W), in1=sl(s4, 3, W - 4), op=ALU.add), reads=[k4], writes=[k8])
                yield
                S.op("pool", lambda e: e.tensor_tensor(out=sl(s16, 15, W), in0=sl(s8, 15, W), in1=sl(s8, 7, W - 8), op=ALU.add), reads=[k8], writes=[k16])
                yield
                wlo, whi, klo, khi = s8, s16, k8, k16
            for (hf, win, wk) in ((0, wlo, klo), (1, whi, khi)):
                ps_ = slice(hf * 64, hf * 64 + 64)
                S.op("dve", lambda e, ps_=ps_, win=win: e.scalar_tensor_tensor(
                    out=dt[ps_], in0=sl(win, 15, W)[ps_], scalar=INVW[ps_, pc:pc + 1], in1=sl(Uv, 15, W)[ps_], op0=ALU.mult, op1=ALU.subtract),
                    reads=[wk, ukey, "invw"], writes=[kdt])
            if b == 0:
                yield
                for (hf, win, wk) in ((0, wlo, klo), (1, whi, khi)):
                    ps_ = slice(hf * 64, hf * 64 + 64)
                    S.op("dve", lambda e, ps_=ps_, win=win: e.tensor_tensor(out=TMP15[pc][ps_, 0:15], in0=win[ps_, 15:30], in1=INVTAB[ps_, pc, :], op=ALU.mult),
                         reads=[wk, "invtab"], writes=[ktm])
                yield
                S.op("dve", lambda e: e.tensor_tensor(out=DTB[pc][:, 0:15], in0=TMP15[pc][:, 0:15], in1=U[:, pc, 15:30], op=ALU.subtract),
                     reads=[ktm, ("u", pc)], writes=[kdt])
            yield
            po = hold_ps()
            S.op("pe", lambda e: e.matmul(bank(po)[:, 0:L], lhsT=PWBD[:, pc, :], rhs=DTB[pc][:, 0:L], start=True, stop=True),
                 reads=["pwbd", kdt], writes=[("ps", po)])
            yield
            S.op("act", lambda e: e.activation(out=MIXT[:, pc, blk_cols(b)], in_=bank(po)[:, 0:L], func=AF.Copy, scale=PSCALE[:, pc:pc + 1]),
                 reads=[("ps", po)] + K_PS, writes=[("mix", pc, b)])
            rel_ps(po)

        def conv_item(b, cc):
            L = blk_len(b)
            smp = (b == 4)
            pgc = proj(256 + 384 + 128 * cc, b)
            pgh = proj(256 + 768 + 128 * cc, b)
            yield
            zkey = "zs" if smp else ("z", cc)
            kg, kc = ("gcs", cc), ("cacc", cc)
            if smp:
                Zv = ZS[:, cc]
                Wz = 10

                def v3(ap):
                    return ap[:, 0:32].rearrange("p (n r) -> p n r", n=4)

                def sz(a, b_, Zv=Zv):
                    return Zv[:, :, a:b_]
            else:
                Zv = Z[:, cc]
                Wz = 514
                if b > 0:
                    S.op("pool", lambda e: e.tensor_copy(out=Z[:, cc, 0:2], in_=Z[:, cc, 512:514]), reads=[("z", cc)], writes=[("z", cc)])

                def v3(ap):
                    return ap[:, 0:512]

                def sz(a, b_, Zv=Zv):
                    return Zv[:, a:b_]
            Lq = Wz - 2
            S.op("act", lambda e: e.copy(out=v3(GCS[cc]), in_=v3(bank(pgc))), reads=[("ps", pgc)], writes=[kg])
            rel_ps(pgc)
            yield
            S.op("dve", lambda e: e.tensor_tensor(out=sz(2, Wz), in0=v3(bank(pgh)), in1=v3(GCS[cc]), op=ALU.mult), reads=[("ps", pgh), kg], writes=[zkey])
            rel_ps(pgh)
            yield
            S.op("pool", lambda e: e.tensor_scalar(out=v3(CACC[cc]), in0=sz(0, Lq), scalar1=CONVW[:, cc, 0:1], scalar2=None, op0=ALU.mult),
                 reads=[zkey] + K_CW, writes=[kc])
            pgb = proj(256 + 128 * cc, b)
            yield
            for k in (1, 2):
                S.op("dve", lambda e, k=k: e.scalar_tensor_tensor(out=v3(CACC[cc]), in0=sz(k, k + Lq), scalar=CONVW[:, cc, k:k + 1], in1=v3(CACC[cc]),
                                                                  op0=ALU.mult, op1=ALU.add), reads=[zkey, kc] + K_CW, writes=[kc])
                yield
            mo = MIXT[:, 3 + cc, blk_cols(b)]
            if smp:
                mo = mo.rearrange("p (n r) -> p n r", n=4)
            S.op("dve", lambda e: e.tensor_tensor(out=mo, in0=v3(bank(pgb)), in1=v3(CACC[cc]), op=ALU.mult), reads=[("ps", pgb), kc], writes=[("mix", 3 + cc, b)])
            rel_ps(pgb)

        items = []
        for b in range(5):
            for pc in range(2):
                items.append((("p", pc), lambda b=b, pc=pc: pool_item(b, pc)))
            for cc in range(3):
                items.append((("c", cc), lambda b=b, cc=cc: conv_item(b, cc)))
        pipeline(items, depth=5)

        for smp in (False, True):
            o1, k1 = (OUTS_S, "outs_s") if smp else (OUTS, "outs")
            o2_, k2 = (OUTS2_S, "outs2_s") if smp else (OUTS2, "outs2")
            for pc in range(2):
                if smp:
                    S.op("pool", lambda e, pc=pc: e.tensor_copy(out=TS32B[:, pc].rearrange("p (n r) -> p n r", n=4), in_=US[:, pc, :, 15:23]), reads=["us"], writes=[("ts32b", pc)])
                    src, sk = TS32B[:, pc], ("ts32b", pc)
                else:
                    src, sk = U[:, pc, 495:527], ("u", pc)
                pi = next_ps()
                S.op("pe", lambda e, src=src, pi=pi: e.transpose(out=bank(pi)[0:32, 0:128], in_=src, identity=IDF), reads=[sk, "idf"], writes=[("ps", pi)])
                S.op("act", lambda e, pc=pc, pi=pi, o1=o1: e.copy(out=o1[0:32, pc * 128:(pc + 1) * 128], in_=bank(pi)[0:32, 0:128]), reads=[("ps", pi)], writes=[k1])
            if smp:
                for n in range(4):
                    S.dma(lambda e, n=n: e.dma_start(out=o_pool_s[l, n, 7:15, :], in_=OUTS_S[8 * n:8 * n + 8, 0:256]), reads=[k1], slot="so1s")
            else:
                S.dma(lambda e: e.dma_start(out=o_pool_p[l], in_=OUTS[17:32, 0:256]), reads=[k1], slot="so1")
            for cc in range(3):
                if smp:
                    S.op("pool", lambda e, cc=cc: e.tensor_copy(out=TS32B[:, 2 + cc].rearrange("p (n r) -> p n r", n=4), in_=ZS[:, cc, :, 2:10]), reads=["zs"], writes=[("ts32b", 2 + cc)])
                    src, sk = TS32B[:, 2 + cc], ("ts32b", 2 + cc)
                else:
                    src, sk = Z[:, cc, 482:514], ("z", cc)
                pi = next_ps()
                S.op("pe", lambda e, src=src, pi=pi: e.transpose(out=bank(pi)[0:32, 0:128], in_=src, identity=IDF), reads=[sk, "idf"], writes=[("ps", pi)])
                S.op("act", lambda e, cc=cc, pi=pi, o2_=o2_: e.copy(out=o2_[0:32, cc * 128:(cc + 1) * 128], in_=bank(pi)[0:32, 0:128]), reads=[("ps", pi)], writes=[k2])
            if smp:
                for n in range(4):
                    S.dma(lambda e, n=n: e.dma_start(out=o_conv_s[l, n], in_=OUTS2_S[8 * n + 6:8 * n + 8, 0:384]), reads=[k2], slot="so2s")
            else:
                S.dma(lambda e: e.dma_start(out=o_conv_p[l], in_=OUTS2[30:32, 0:384]), reads=[k2], slot="so2")

    class ResPool:
        def __init__(self, n):
            self.free = list(range(n))

        def acquire(self):
            while not self.free:
                yield
            return self.free.pop(0)

        def release(self, i):
            self.free.append(i)

    def acq_bank(pref=(0, 1, 2, 3, 4, 5)):
        while True:
            for i in pref:
                if i not in ps_held:
                    ps_held.add(i)
                    return i
            yield

    def acq_pair():
        while True:
            for p0 in (0, 2, 4):
                if p0 not in ps_held and p0 + 1 not in ps_held:
                    ps_held.add(p0)
                    ps_held.add(p0 + 1)
                    return p0
            yield

    def phase_a2b(l):
        for i in range(3):
            S.op("dve", lambda e, i=i: e.memset(QBDG[i], 0.0), writes=[("qbd", i)])
        for i in range(2):
            S.op("dve", lambda e, i=i: e.memset(VS[i], 1.0), writes=[("vs", i)])
        for i in range(2):
            S.op("dve", lambda e, i=i: e.memset(QKB[i], 0.0), writes=[("qkb", i)])
        PSA = (7, 6)
        psa = [bank(PSA[h])[0:32, 0:65] for h in range(2)]
        first_pv = [True, True]
        P_Q, P_SQ, P_S4, P_ROT, P_QKB = ResPool(NQ), ResPool(1), ResPool(4), ResPool(1), ResPool(2)
        P_PB, P_CST, P_KI = ResPool(4), ResPool(2), ResPool(2)

        def qkv_items(g):
            W = WINS[g]
            sg = g % 2
            QKT, VAUG, ROPE = QKT_S[sg], VAUG_S[sg], ROPE_S[sg]
            kq = lambda a, ti: ("qkt", sg, a, ti)
            kv = lambda ti: ("vaug", sg, ti)
            krope = ("rope", sg)
            S.dma(lambda e: e.dma_start(out=ROPE, in_=c_rope[g]), writes=[krope], slot=("rope", sg))
            S.op("dve", lambda e: e.memset(VAUG, 1.0), writes=[kv(ti) for ti in range(NTILE)])

            def qkv_tile(tis):
                J = len(tis)
                R = tile_rows(tis[0])
                t0_ = tis[0]
                pb0 = (yield from acq_pair()) if J == 2 else (yield from acq_bank())
                pqv = psum[:, pb0 * 512:(pb0 + J) * 512].rearrange("p (j c) -> p j c", j=J)
                pkeys = [("ps", pb0 + j) for j in range(J)]
                for j, ti in enumerate(tis):
                    if ti == 16:
                        csel = slice(T, T + 32)
                        htk = [("ht", 16)]
                    elif g == 0:
                        csel = slice(128 * ti, 128 * ti + 128)
                        htk = [("ht", ti)]
                    elif g == 1:
                        r, kb = ti // 4, ti % 4
                        csel = slice(512 * kb + r, 512 * kb + 512, 4)
                        htk = [("ht", 4 * kb + jj) for jj in range(4)]
                    else:
                        csel = slice(ti, T, 16)
                        htk = [("ht", jj) for jj in range(16)]
                    for d in range(8):
                        S.op("pe", lambda e, d=d, j=j, csel=csel: e.matmul(pqv[0:R, j, 0:384], lhsT=HT[:, d, csel], rhs=WQ[g][:, d, :], start=(d == 0), stop=(d == 7)),
                             reads=[("wq", g)] + htk, writes=[pkeys[j]])
                yield
                qi = yield from P_Q.acquire()
                si = yield from P_SQ.acquire()
                qf = QKVF[qi][:, 0:J, :]
                qk = ("qkvf", qi)
                S.op("act", lambda e: e.copy(out=qf[0:R], in_=pqv[0:R, :, 0:384]), reads=pkeys, writes=[qk])
                sq = SQ[si][:, 0:J, :]
                S.op("act", lambda e: e.activation(out=sq[0:R], in_=pqv[0:R, :, 0:256], func=AF.Square), reads=pkeys, writes=[("sq", si)])
                for j in range(J):
                    rel_ps(pb0 + j)
                yield
                s4 = yield from P_S4.acquire()
                ss = SS4[:, s4, 0:4 * J]
                rs = RS4[:, s4, 0:4 * J]
                S.op("dve", lambda e: e.tensor_reduce(out=ss[0:R], in_=sq[0:R].rearrange("p j (a b) -> p (j a) b", a=4), axis=AX.X, op=ALU.add),
                     reads=[("sq", si)], writes=[("ss4", s4)])
                P_SQ.release(si)
                yield
                S.op("act", lambda e: e.activation(out=rs[0:R], in_=ss[0:R], func=AF.Ln, scale=1.0 / 64, bias=EPSC[0:R, :]),
                     reads=[("ss4", s4), "epsc"], writes=[("rs4", s4)])
                yield
                S.op("act", lambda e: e.activation(out=rs[0:R], in_=rs[0:R], func=AF.Exp, scale=-0.5), reads=[("rs4", s4)], writes=[("rs4", s4)])
                yield
                qk4 = qf[:, :, 0:256].rearrange("p j (a b) -> p j a b", a=4)
                rs4b = rs.rearrange("p (j a) -> p j a", j=J).unsqueeze(3)
                S.op("dve", lambda e: e.tensor_tensor(out=qk4[0:R], in0=qk4[0:R], in1=rs4b[0:R].to_broadcast([R, J, 4, 64]), op=ALU.mult),
                     reads=[qk, ("rs4", s4)], writes=[qk])
                P_S4.release(s4)
                yield
                S.op("dve", lambda e: e.tensor_tensor(out=qf[0:R, :, 0:256], in0=qf[0:R, :, 0:256], in1=GQK[0:R].unsqueeze(1).to_broadcast([R, J, 256]), op=ALU.mult),
                     reads=[qk] + K_GQ, writes=[qk])
                yield
                ri = yield from P_ROT.acquire()
                rot = ROT[ri][:, :, 0:J]
                x1 = qk4[0:R, :, :, 0:8]
                x2 = qk4[0:R, :, :, 8:16]
                cs = ROPE[0:R, t0_:t0_ + J, 0]
                sn = ROPE[0:R, t0_:t0_ + J, 1]
                for (k_, a_, b_) in ((0, x1, cs), (1, x2, sn), (2, x2, cs), (3, x1, sn)):
                    S.op("dve", lambda e, k_=k_, a_=a_, b_=b_: e.tensor_tensor(out=rot[0:R, k_], in0=a_, in1=b_, op=ALU.mult), reads=[qk, krope], writes=[("rot", ri)])
                yield
                S.op("dve", lambda e: e.tensor_tensor(out=x1, in0=rot[0:R, 0], in1=rot[0:R, 1], op=ALU.subtract), reads=[("rot", ri)], writes=[qk])
                S.op("dve", lambda e: e.tensor_tensor(out=x2, in0=rot[0:R, 2], in1=rot[0:R, 3], op=ALU.add), reads=[("rot", ri)], writes=[qk])
                P_ROT.release(ri)
                yield
                bi = yield from P_QKB.acquire()
                qb = QKB[bi][:, 0:J]
                qbq = qb.rearrange("p j a c -> p j (a c)").rearrange("p j (a x) -> p j a x", x=192)[:, :, :, 0:64]
                S.op("pool", lambda e: e.tensor_copy(out=qbq[0:R], in_=qf[0:R, :, 0:128].rearrange("p j (a c) -> p j a c", a=2)), reads=[qk], writes=[("qkb", bi)])
                S.op("act", lambda e: e.copy(out=qb[0:R, :, 2, :], in_=qf[0:R, :, 128:256]), reads=[qk], writes=[("qkb", bi)])
                S.op("act", lambda e: e.copy(out=VAUG[0:R, t0_:t0_ + J, :, 0:64], in_=qf[0:R, :, 256:384].rearrange("p j (h c) -> p j h c", h=2)),
                     reads=[qk], writes=[kv(ti) for ti in tis])
                for j, ti in enumerate(tis):
                    need = (ti == 16) or (g == 2) or (g == 1 and ti % 4 == 3) or (g == 0 and ti == 15)
                    if not need:
                        continue
                    if ti == 16:
                        for n in range(4):
                            S.dma(lambda e, n=n, j=j: e.dma_start(out=o_kv_s[g][l, n, W - 8:W, :], in_=qf[8 * n:8 * n + 8, j, 128:384]),
                                  reads=[qk], slot=("kvo", qi))
                    else:
                        if g == 0:
                            dst = o_kv_p[0][l]
                        elif g == 1:
                            dst = o_kv_p[1][l].rearrange("(i r) c -> r i c", r=4)[ti // 4]
                        else:
                            dst = o_kv_p[2][l].rearrange("(i r) c -> r i c", r=16)[ti]
                        S.dma(lambda e, j=j, dst=dst: e.dma_start(out=dst, in_=qf[:, j, 128:384]), reads=[qk], slot=("kvo", qi))
                P_Q.release(qi)
                yield
                pt = yield from acq_bank((5, 4, 3, 2, 1, 0))
                pst = bank(pt)[:, 0:192 * J].bitcast(BF16).rearrange("p (j a b) -> p j a b", j=J, a=3)
                for j in range(J):
                    for a in range(3):
                        S.op("pe", lambda e, a=a, j=j: e.transpose(out=pst[:, j, a, 0:R], in_=qb[0:R, j, a, :], identity=IDB[0:R, 0:R]),
                             reads=[("qkb", bi), "idb"], writes=[("ps", pt)])
                P_QKB.release(bi)
                yield
                if J == 2:
                    qdst = QKT[:, :, 128 * t0_:128 * t0_ + 256].rearrange("p a (j r) -> p a j r", j=2)
                else:
                    qdst = QKT[:, :, 128 * t0_:128 * t0_ + R].unsqueeze(2)
                S.op("act", lambda e: e.copy(out=qdst[:, :, :, 0:R], in_=pst[:, :, :, 0:R].rearrange("p j a r -> p a j r")), reads=[("ps", pt)],
                     writes=[kq(a, ti) for a in range(3) for ti in tis])
                rel_ps(pt)

            return [(lambda tis=tis: qkv_tile(tis)) for tis in ([[2 * i, 2 * i + 1] for i in range(8)] + [[16]])]

        def att_items(g):
            sg = g % 2
            QKT, VAUG = QKT_S[sg], VAUG_S[sg]
            kq = lambda a, ti: ("qkt", sg, a, ti)
            kv = lambda ti: ("vaug", sg, ti)
            ncls = (1, 4, 16)[g]
            nb = 16 // ncls

            def att_unit(r, qb):
                tq = r * nb + qb
                kbs = [(1, qb)] if qb == 0 else [(0, qb - 1), (1, qb)]
                c0 = 256 if qb == 0 else 0
                bi = yield from P_PB.acquire()
                pi = yield from acq_bank()
                pS = bank(pi)
                for (ki, kb) in kbs:
                    tk = r * nb + kb
                    S.op("pe", lambda e, ki=ki, tk=tk: e.matmul(pS[:, ki * 256:ki * 256 + 256], lhsT=QKT[:, 2, 128 * tk:128 * tk + 128],
                                                              rhs=QKT[:, 0:2, 128 * tq:128 * tq + 128], start=True, stop=False),
                         reads=[kq(2, tk), kq(0, tq), kq(1, tq)], writes=[("ps", pi)])
                    S.op("pe", lambda e, ki=ki: e.matmul(pS[:, ki * 256:ki * 256 + 256], lhsT=IDB, rhs=MASKP[:, ki * 256:ki * 256 + 256], start=False, stop=True),
                         reads=["idb", "maskp"], writes=[("ps", pi)])
                yield
                P = PB[bi]
                S.op("act", lambda e: e.activation(out=P[:, c0:512], in_=pS[:, c0:512], func=AF.Exp, scale=0.125), reads=[("ps", pi)], writes=[("pb", bi)])
                rel_ps(pi)
                yield
                po = yield from acq_bank()
                pO = bank(po)
                for h in range(2):
                    hs = slice(64 * h, 64 * h + 64)
                    for idx, (ki, kb) in enumerate(kbs):
                        tk = r * nb + kb
                        S.op("pe", lambda e, hs=hs, ki=ki, h=h, idx=idx, tk=tk: e.matmul(
                            pO[hs, 0:128], lhsT=VAUG[:, tk, h, 0:64], rhs=P[:, (ki * 2 + h) * 128:(ki * 2 + h) * 128 + 128],
                            start=(idx == 0), stop=(idx == len(kbs) - 1)), reads=[kv(tk), ("pb", bi)], writes=[("ps", po)])
                nd = 2 * len(kbs)
                for idx, (ki, kb) in enumerate(kbs):
                    for h in range(2):
                        S.op("pe", lambda e, ki=ki, idx=idx, h=h: e.matmul(pO[:, 128:256], lhsT=HSEL[:, h, :], rhs=P[:, (ki * 2 + h) * 128:(ki * 2 + h) * 128 + 128],
                                                                        start=(idx == 0 and h == 0), stop=(2 * idx + h == nd - 1)), reads=["hsel", ("pb", bi)], writes=[("ps", po)])
                P_PB.release(bi)
                yield
                if g == 0:
                    qsel = slice(128 * qb, 128 * qb + 128)
                elif g == 1:
                    qsel = slice(512 * qb + r, 512 * qb + 512, 4)
                else:
                    qsel = slice(r, T, 16)
                pov = pO[:, 0:256].rearrange("p (a b) -> p a b", a=2)
                if g == 0:
                    S.op("act", lambda e: e.copy(out=ACC[:, :, qsel], in_=pov), reads=[("ps", po)], writes=["acc"])
                else:
                    S.op("dve", lambda e: e.tensor_tensor(out=ACC[:, :, qsel], in0=pov, in1=ACC[:, :, qsel], op=ALU.add), reads=[("ps", po), "acc"], writes=["acc"])
                rel_ps(po)

            return [(lambda r=r, qb=qb: att_unit(r, qb)) for r in range(ncls) for qb in range(nb)]

        def smp_items(g):
            W = WINS[g]
            sg = g % 2
            QKT, VAUG = QKT_S[sg], VAUG_S[sg]
            kq = lambda a, ti: ("qkt", sg, a, ti)
            for h in range(2):
                hs = slice(64 * h, 64 * h + 64)
                S.op("act", lambda e, h=h, hs=hs: e.copy(out=QBDG[g][hs, 32 * h:32 * h + 32], in_=QKT[hs, h, 2048:2080]), reads=[kq(h, 16)], writes=[("qbd", g)])
            pi = next_free_bank()
            S.op("pe", lambda e: e.matmul(bank(pi)[0:32, 0:64], lhsT=QKT[:, 2, 2048:2080], rhs=QBDG[g], start=True, stop=True), reads=[kq(2, 16), ("qbd", g)], writes=[("ps", pi)])
            S.op("act", lambda e: e.activation(out=PN[0:32], in_=bank(pi)[0:32, 0:64], func=AF.Exp, scale=0.125), reads=[("ps", pi)], writes=["pn"])
            S.op("dve", lambda e: e.tensor_tensor(out=PN[0:32], in0=PN[0:32], in1=MASKN[0:32, g, :], op=ALU.mult), reads=["pn", "maskn"], writes=["pn"])
            for h in range(2):
                S.op("pe", lambda e, h=h, st=first_pv[h]: e.matmul(psa[h], lhsT=PN[0:32, 32 * h:32 * h + 32], rhs=VAUG[0:32, 16, h, :], start=st, stop=False),
                     reads=["pn", ("vaug", sg, 16)], writes=[("ps", PSA[h])])
                first_pv[h] = False
            ntile = 8 if g == 2 else W // 128

            def smp_chunk(n, c0):
                nt = min(4, ntile - c0)
                ci = yield from P_CST.acquire()
                cst = CST[ci]
                if g == 2:
                    src = caches[2][l, n].rearrange("(i j) c -> i j c", j=16)[:, c0:c0 + nt, :]
                else:
                    src = caches[g][l, n, 128 * c0:128 * (c0 + nt), :].rearrange("(i p) c -> p i c", p=128)
                S.dma(lambda e: e.dma_start(out=cst[:, 0:nt, :], in_=src), writes=[("cst", ci)], slot=("cst", ci))
                yield
                ki = yield from P_KI.acquire()
                S.op("dve", lambda e: e.tensor_copy(out=KB_[ki][:, 0:nt, :], in_=cst[:, 0:nt, 0:128]), reads=[("cst", ci)], writes=[("kb", ki)])
                S.op("act", lambda e: e.copy(out=VS[ki][:, 0:nt, :, 0:64], in_=cst[:, 0:nt, 128:256].rearrange("p i (h c) -> p i h c", h=2)),
                     reads=[("cst", ci)], writes=[("vs", ki)])
                P_CST.release(ci)
                yield
                pt = yield from acq_bank()
                pst = bank(pt)[:, 0:256].bitcast(BF16).rearrange("p (a b) -> p a b", a=4)
                for i in range(nt):
                    S.op("pe", lambda e, i=i: e.transpose(out=pst[:, i, :], in_=KB_[ki][:, i, :], identity=IDB), reads=[("kb", ki), "idb"], writes=[("ps", pt)])
                yield
                S.op("act", lambda e: e.copy(out=KTS[ki][:, 0:128 * nt], in_=bank(pt)[:, 0:64 * nt].bitcast(BF16)), reads=[("ps", pt)], writes=[("kts", ki)])
                rel_ps(pt)
                yield
                pq_ = yield from acq_bank()
                for i in range(nt):
                    S.op("pe", lambda e, i=i: e.matmul(bank(pq_)[:, 64 * i:64 * i + 64], lhsT=KTS[ki][:, 128 * i:128 * i + 128], rhs=QBDG[g], start=True, stop=True),
                         reads=[("kts", ki), ("qbd", g)], writes=[("ps", pq_)])
                yield
                psb = PSB[ki]
                S.op("act", lambda e: e.activation(out=psb[:, 0:nt, :], in_=bank(pq_)[:, 0:64 * nt].rearrange("p (i c) -> p i c", i=nt), func=AF.Exp, scale=0.125),
                     reads=[("ps", pq_)], writes=[("psb", ki)])
                rel_ps(pq_)
                yield
                if g == 2:
                    S.op("dve", lambda e: e.tensor_tensor(out=psb[:, 0:nt, :], in0=psb[:, 0:nt, :], in1=MASK2[:, n, c0:c0 + nt, :], op=ALU.mult),
                         reads=[("psb", ki), "mask2"], writes=[("psb", ki)])
                else:
                    i0 = 0
                    if c0 == 0:
                        S.op("dve", lambda e: e.tensor_tensor(out=psb[:, 0, :], in0=psb[:, 0, :], in1=MASKS[:, g, n, 0, :], op=ALU.mult), reads=[("psb", ki), "masks"], writes=[("psb", ki)])
                        i0 = 1
                    if nt > i0:
                        S.op("dve", lambda e, i0=i0: e.tensor_tensor(
                            out=psb[:, i0:nt, :], in0=psb[:, i0:nt, :], in1=MASKS[:, g, n, 1:2, :].to_broadcast([128, nt - i0, 64]), op=ALU.mult),
                            reads=[("psb", ki), "masks"], writes=[("psb", ki)])
                yield
                last_chunk = (g == 2 and n == 3 and c0 + nt == ntile)
                for i in range(nt):
                    for h in range(2):
                        S.op("pe", lambda e, i=i, h=h, sp_=(last_chunk and i == nt - 1): e.matmul(
                            psa[h], lhsT=psb[:, i, 32 * h:32 * h + 32], rhs=VS[ki][:, i, h, :], start=False, stop=sp_),
                            reads=[("psb", ki), ("vs", ki)], writes=[("ps", PSA[h])])
                P_KI.release(ki)

            return [(lambda n=n, c0=c0: smp_chunk(n, c0)) for n in range(4) for c0 in range(0, ntile, 4)]

        def next_free_bank():
            for i in (5, 4, 3, 2, 1, 0):
                if i not in ps_held:
                    return i
            raise AssertionError("no free PSUM bank")

        def merge(*lists, rate=None):
            out = []
            tot = max(len(x) for x in lists)
            pos = [0] * len(lists)
            rate = rate or [1] * len(lists)
            for step in range(tot):
                for li, x in enumerate(lists):
                    want = min(len(x), (step + 1) * len(x) * rate[li] // tot)
                    while pos[li] < want:
                        out.append(x[pos[li]])
                        pos[li] += 1
            return out

        pipeline(qkv_items(0), depth=8)
        if stop == "a2b_qkv0":
            return True
        for g in range(3):
            att = att_items(g)
            smp = smp_items(g)
            nxtq = qkv_items(g + 1) if g < 2 else []
            pipeline(merge(att, smp, nxtq, rate=[1, 1, 3]) if nxtq else merge(att, smp), depth=10)
            assert not ps_held, ps_held
            if stop == "a2b_s%d" % g:
                return True

        for b in range(4):
            cs_ = slice(512 * b, 512 * b + 512)
            S.op("dve", lambda e, cs_=cs_: e.reciprocal(out=ACC[:, 1, cs_], in_=ACC[:, 1, cs_]), reads=["acc"], writes=["acc"])
            S.op("dve", lambda e, cs_=cs_: e.tensor_tensor(out=MIXT[:, 2, cs_], in0=ACC[:, 0, cs_], in1=ACC[:, 1, cs_], op=ALU.mult), reads=["acc"], writes=[("mix", 2, b)])
        for h in range(2):
            S.op("dve", lambda e, h=h: e.tensor_copy(out=RD[0:32, h:h + 1], in_=psa[h][:, 64:65]), reads=[("ps", PSA[h])], writes=["rd"])
        S.op("dve", lambda e: e.reciprocal(out=RD[0:32], in_=RD[0:32]), reads=["rd"], writes=["rd"])
        for h in range(2):
            S.op("dve", lambda e, h=h: e.tensor_scalar(out=YSB[0:32, 64 * h:64 * h + 64], in0=psa[h][:, 0:64], scalar1=RD[0:32, h:h + 1], scalar2=None, op0=ALU.mult),
                 reads=[("ps", PSA[h]), "rd"], writes=["ysb"])
        pt = next_ps()
        pst = bank(pt)[:, 0:16].bitcast(BF16)
        S.op("pe", lambda e, pst=pst: e.transpose(out=pst, in_=YSB[0:32, :], identity=IDB[0:32, 0:32]), reads=["ysb", "idb"], writes=[("ps", pt)])
        S.op("act", lambda e, pst=pst: e.copy(out=MIXT[:, 2, 2048:2080], in_=pst), reads=[("ps", pt)], writes=[("mix", 2, 4)])

    def phase_c(l):
        def pre(t):
            R = tile_rows(t)
            if l == 0:
                src = xp[128 * t:128 * t + 128, :] if t < 16 else xs
            else:
                src = xpark[t, 0:R, :]
            S.dma(lambda e, t=t, R=R, src=src: e.dma_start(out=X[0:R, t, :], in_=src), reads=([("park", t)] if l > 0 else []), writes=[("x", t)], slot=("xl", t))
            b = t // 4
            for hf in range(2):
                pi = next_ps()
                for c in range(6):
                    S.op("pe", lambda e, c=c, pi=pi, R=R, t=t, hf=hf: e.matmul(bank(pi)[0:R, :], lhsT=MIXT[:, c, tile_cols(t)], rhs=WO[:, c, 512 * hf:512 * hf + 512],
                                                                               start=(c == 0), stop=(c == 5)), reads=["wo", ("mix", c, b)], writes=[("ps", pi)])
                S.op("dve", lambda e, pi=pi, R=R, t=t, hf=hf: e.tensor_tensor(out=X[0:R, t, 512 * hf:512 * hf + 512], in0=bank(pi)[0:R, :], in1=X[0:R, t, 512 * hf:512 * hf + 512], op=ALU.add),
                     reads=[("ps", pi), ("x", t)], writes=[("x", t)])
        norm_and_transpose(norm2_g[l], pre=pre)

    def phase_d(l, prefetch_next):
        last = (l == DEPTH - 1)
        if not last:
            S.op("dve", lambda e: e.memset(SS, 0.0), writes=["ss"])
        for fb in range(4):
            i = fb % 2

            def up(b, fb=fb, i=i):
                L = blk_len(b)
                at = AT[b % 2]
                for fc in range(8):
                    pi = next_ps()
                    for d in range(8):
                        S.op("pe", lambda e, d=d, fc=fc, pi=pi, L=L: e.matmul(bank(pi)[:, 0:L], lhsT=WU[i][:, d, 128 * fc:128 * fc + 128], rhs=HT[:, d, blk_cols(b)],
                                                                          start=(d == 0), stop=(d == 7)),
                             reads=[("wu", i)] + [("ht", t) for t in (range(4 * b, 4 * b + 4) if b < 4 else [16])], writes=[("ps", pi)])
                    ri = fc % 2
                    S.op("act", lambda e, pi=pi, L=L, ri=ri: e.activation(out=RL[ri][:, 0:L], in_=bank(pi)[:, 0:L], func=AF.Relu), reads=[("ps", pi)], writes=[("rl", ri)])
                    S.op("dve", lambda e, pi=pi, L=L, ri=ri, at=at, fc=fc: e.tensor_tensor(out=at[:, fc, 0:L], in0=bank(pi)[:, 0:L], in1=RL[ri][:, 0:L], op=ALU.mult),
                         reads=[("ps", pi), ("rl", ri)], writes=[("at", b % 2)])

            def down(b, fb=fb, i=i):
                at = AT[b % 2]
                tiles = range(4 * b, 4 * b + 4) if b < 4 else [16]
                for t in tiles:
                    R = tile_rows(t)
                    lo = 128 * (t % 4) if b < 4 else 0
                    for hf in range(2):
                        pi = next_ps()
                        for fc in range(8):
                            S.op("pe", lambda e, fc=fc, pi=pi, R=R, lo=lo, hf=hf, at=at: e.matmul(bank(pi)[0:R, :], lhsT=at[:, fc, lo:lo + R], rhs=WD[i][:, fc, 512 * hf:512 * hf + 512],
                                                                                          start=(fc == 0), stop=(fc == 7)), reads=[("wd", i), ("at", b % 2)], writes=[("ps", pi)])
                        S.op("dve", lambda e, pi=pi, R=R, t=t, hf=hf: e.tensor_tensor(out=X[0:R, t, 512 * hf:512 * hf + 512], in0=bank(pi)[0:R, :], in1=X[0:R, t, 512 * hf:512 * hf + 512], op=ALU.add),
                             reads=[("ps", pi), ("x", t)], writes=[("x", t)])
                    if fb == 3:
                        if last:
                            dst = yp[128 * t:128 * t + 128, :] if t < 16 else ys
                            S.dma(lambda e, t=t, R=R, dst=dst: e.dma_start(out=dst, in_=X[0:R, t, :]), reads=[("x", t)], slot=("xo", t))
                        else:
                            S.dma(lambda e, t=t, R=R: e.dma_start(out=xpark[t, 0:R, :], in_=X[0:R, t, :]), reads=[("x", t)], writes=[("park", t)], slot=("xo", t))
                            S.op("act", lambda e, t=t, R=R: e.activation(out=HB[t % 2][0:R, :], in_=X[0:R, t, :], func=AF.Square, accum_out=SS[0:R, t:t + 1]),
                                 reads=[("x", t)], writes=[("hb", t % 2), "ss"])

            up(0)
            for b in range(5):
                if b + 1 < 5:
                    up(b + 1)
                down(b)
            if fb == 1 and not last:
                load_gbc(norm1_g[l + 1])
                load_smalls(l + 1)
            if fb + 2 < 4:
                load_ffn(l, fb + 2)
            elif prefetch_next is not None:
                prefetch_next(fb)

    def fin():
        S.emit(es)
        es.close()
        return nc

    if stop == "setup":
        return fin()
    load_wa(0)
    load_wq(0)
    load_wo(0)
    for l in range(DEPTH):
        if l == 0:
            load_smalls(l)
        norm_and_transpose(norm1_g[l], have_ss=(l > 0))
        if stop == "a1":
            return fin()
        phase_a2a(l)
        if stop == "a2a":
            return fin()
        if phase_a2b(l) or stop == "a2b":
            return fin()
        load_ffn(l, 0)
        phase_c(l)
        if stop == "c":
            return fin()
        load_ffn(l, 1)

        def prefetch_next(fb, l=l):
            if l + 1 < DEPTH:
                if fb == 2:
                    load_wa(l + 1)
                if fb == 3:
                    load_wq(l + 1)
                    load_wo(l + 1)
        phase_d(l, prefetch_next)
        if stop == "d":
            return fin()
    S.emit(es)
    es.close()
    return nc


_NC_CACHE = {}


def kernel(x_prompt, x_sample, state_pool, state_conv, cache_kv_w128, cache_kv_w512, cache_kv_w2048,
           norm1_g, w_in, q_norm_g, k_norm_g, pool_w, pool_scale, conv_w, w_out, norm2_g, w_up, w_down):
    f = lambda a: np.ascontiguousarray(np.asarray(a, dtype=np.float32))
    consts = make_consts()
    shared = {
        "norm1_g": f(norm1_g), "w_in": f(w_in), "q_norm_g": f(q_norm_g), "k_norm_g": f(k_norm_g),
        "pool_w": f(pool_w), "pool_scale": f(pool_scale), "conv_w": f(conv_w), "w_out": f(w_out),
        "norm2_g": f(norm2_g), "w_up": f(w_up), "w_down": f(w_down),
    }
    shared.update(consts)
    x_prompt = f(x_prompt); x_sample = f(x_sample)
    state_pool = f(state_pool); state_conv = f(state_conv)
    c128 = f(cache_kv_w128); c512 = f(cache_kv_w512); c2048 = f(cache_kv_w2048)
    in_maps = []
    for c in range(NCORES):
        s = slice(4 * c, 4 * c + 4)
        m = dict(shared)
        m["xp"] = x_prompt[c]
        m["xs"] = np.ascontiguousarray(x_sample[s].reshape(32, D))
        m["spool"] = np.ascontiguousarray(state_pool[:, s])
        m["sconv"] = np.ascontiguousarray(state_conv[:, s])
        m["c128"] = np.ascontiguousarray(c128[:, s].reshape(DEPTH, 4, 128, 256))
        m["c512"] = np.ascontiguousarray(c512[:, s].reshape(DEPTH, 4, 512, 256))
        m["c2048"] = np.ascontiguousarray(c2048[:, s].reshape(DEPTH, 4, 2048, 256))
        in_maps.append(m)
    if "nc" not in _NC_CACHE:
        _NC_CACHE["nc"] = build_program()
    nc = _NC_CACHE["nc"]
    res = run_bass_kernel_spmd(nc, in_maps, core_ids=list(range(NCORES)))
    R = res.results
    cat = lambda k, ax: np.concatenate([np.asarray(r[k]) for r in R], axis=ax)
    y_prompt = np.stack([np.asarray(r["yp"]) for r in R], 0)
    y_sample = np.concatenate([np.asarray(r["ys"]).reshape(4, 8, D) for r in R], 0)
    pool_p = np.stack([np.asarray(r["pool_p"]) for r in R], 1)
    conv_p = np.stack([np.asarray(r["conv_p"]) for r in R], 1)
    kvp = [np.stack([np.asarray(r[k]) for r in R], 1).reshape(DEPTH, NCORES, w, 2, 2, 64)
           for k, w in (("kv128_p", 128), ("kv512_p", 512), ("kv2048_p", 2048))]
    pool_s = cat("pool_s", 1)
    conv_s = cat("conv_s", 1)
    kvs = [cat(k, 1).reshape(DEPTH, 32, w, 2, 2, 64) for k, w in (("kv128_s", 128), ("kv512_s", 512), ("kv2048_s", 2048))]
    outs = (y_prompt, y_sample, pool_p, conv_p, kvp[0], kvp[1], kvp[2], pool_s, conv_s, kvs[0], kvs[1], kvs[2])
    return tuple(np.ascontiguousarray(o, dtype=np.float32) for o in outs)
```

```python
import types
from contextlib import ExitStack

import ml_dtypes
import numpy as np

import concourse.bass as bass
import concourse.mybir as mybir
from concourse.bass_utils import run_bass_kernel_spmd

F32 = mybir.dt.float32
BF16 = mybir.dt.bfloat16
U8 = mybir.dt.uint8
ALU = mybir.AluOpType
AF = mybir.ActivationFunctionType
AX = mybir.AxisListType

NCORES = 8
D = 1024
T = 2048
TS = 32
NTOK = T + TS
NTILE = 17
DEPTH = 2
DFF = 4096
EPS = 1e-6
WINS = (128, 512, 2048)
DILS = (1, 4, 16)
COMPUTE = ("pe", "act", "dve", "pool")


def _snap(fn):
    if fn.__closure__ is None:
        return fn
    cells = tuple(types.CellType(c.cell_contents) for c in fn.__closure__)
    return types.FunctionType(fn.__code__, fn.__globals__, fn.__name__, fn.__defaults__, cells)


class Op:
    __slots__ = ("eng", "fn", "waits", "signal", "sigval", "idx", "dma_slot", "dma_val", "queue", "wait_vals")

    def __init__(self, eng, fn):
        self.eng = eng
        self.fn = _snap(fn)
        self.waits = []
        self.wait_vals = {}
        self.signal = False
        self.sigval = None
        self.idx = None
        self.dma_slot = None
        self.dma_val = None
        self.queue = None


class Region:
    __slots__ = ("name", "lo", "hi", "writer", "readers", "overl")

    def __init__(self, name, lo, hi):
        self.name, self.lo, self.hi = name, lo, hi
        self.writer = None
        self.readers = {}
        self.overl = None


class Sched:
    def __init__(self, nc):
        self.nc = nc
        self.ops = {e: [] for e in COMPUTE + ("sp",)}
        self.regions = {}
        self.phys = []
        self.dma_slots = {}
        self.waited = {}

    def region(self, name, lo=None, hi=None):
        r = self.regions.get(name)
        if r is None:
            r = Region(name, lo, hi)
            self.regions[name] = r
            if lo is not None:
                for o in self.phys:
                    if o.lo < hi and lo < o.hi:
                        if o.overl is None:
                            o.overl = []
                        o.overl.append(r)
                        if r.overl is None:
                            r.overl = []
                        r.overl.append(o)
                self.phys.append(r)
        return r

    def _regs(self, keys):
        out = []
        for k in keys:
            r = self.regions[k]
            out.append(r)
            if r.overl:
                out.extend(r.overl)
        return out

    def _deps(self, op, reads, writes):
        deps = []
        for r in self._regs(reads):
            if r.writer is not None:
                deps.append(r.writer)
        for r in self._regs(writes):
            if r.writer is not None:
                deps.append(r.writer)
            deps.extend(r.readers.values())
        best = {}
        for d in deps:
            if d is op:
                continue
            if d.dma_slot is not None:
                key = ("dma", d.dma_slot)
                v = self.dma_slots[d.dma_slot] - (16 if (op.dma_slot == d.dma_slot) else 0)
            else:
                if d.eng == op.eng and op.dma_slot is None and d.eng == "pe":
                    continue
                key = ("eng", d.eng)
                v = d.idx
            if key not in best or v > best[key][0]:
                best[key] = (v, d)
        wq = op.eng
        for key, (v, d) in best.items():
            wk = (wq, key)
            if self.waited.get(wk, -1) >= v:
                continue
            self.waited[wk] = v
            if d.dma_slot is None:
                d.signal = True
            else:
                op.wait_vals[id(d)] = v
            op.waits.append(d)

    def _commit(self, op, reads, writes):
        tag = ("dma", op.dma_slot) if op.dma_slot is not None else op.eng
        for k in reads:
            self.regions[k].readers[tag] = op
        for k in writes:
            r = self.regions[k]
            r.writer = op
            r.readers = {}

    def op(self, eng, fn, reads=(), writes=()):
        o = Op(eng, fn)
        o.idx = len(self.ops[eng])
        self._deps(o, reads, writes)
        self.ops[eng].append(o)
        self._commit(o, reads, writes)
        return o

    def dma(self, fn, reads=(), writes=(), slot=None, queue="sp"):
        o = Op(queue, fn)
        o.dma_slot = slot
        self.dma_slots[slot] = self.dma_slots.get(slot, 0) + 16
        o.dma_val = self.dma_slots[slot]
        o.idx = len(self.ops[queue])
        self._deps(o, reads, writes)
        self.ops[queue].append(o)
        self._commit(o, reads, writes)
        return o

    def emit(self, es):
        nc = self.nc
        sems = {}
        for e in COMPUTE:
            sems[("eng", e)] = es.enter_context(nc.semaphore("s_" + e))
        for i, s in enumerate(self.dma_slots):
            sems[("dma", s)] = es.enter_context(nc.semaphore("d%d" % i))
        for e in COMPUTE:
            c = 0
            for o in self.ops[e]:
                if o.signal:
                    c += 1
                    o.sigval = c
        final_waits = dict(self.dma_slots)

        def run(engname, eng):
            for o in self.ops[engname]:
                for d in o.waits:
                    if d.dma_slot is not None:
                        eng.wait_ge(sems[("dma", d.dma_slot)], o.wait_vals[id(d)])
                    else:
                        eng.wait_ge(sems[("eng", d.eng)], d.sigval)
                ins = o.fn(eng)
                if o.dma_slot is not None:
                    ins.then_inc(sems[("dma", o.dma_slot)], 16)
                elif o.signal:
                    ins.then_inc(sems[("eng", o.eng)], 1)
            if engname == "sp":
                for s, v in final_waits.items():
                    eng.wait_ge(sems[("dma", s)], v)

        block = es.enter_context(nc.Block())

        @block.tensor
        def _(e):
            run("pe", e)

        @block.scalar
        def _(e):
            run("act", e)

        @block.vector
        def _(e):
            run("dve", e)

        @block.gpsimd
        def _(e):
            run("pool", e)

        @block.sync
        def _(e):
            run("sp", e)


def _tile_positions(g, ti):
    if ti == 16:
        return None
    if g == 0:
        return 128 * ti + np.arange(128)
    if g == 1:
        r, kb = ti // 4, ti % 4
        return 512 * kb + 4 * np.arange(128) + r
    return 16 * np.arange(128) + ti


def make_consts():
    bf = ml_dtypes.bfloat16
    c = {}
    c["ident"] = np.eye(128, dtype=np.float32)
    k = np.arange(128)[:, None]
    q = np.arange(128)[None, :]
    prev = np.where(k >= q, 0.0, -1000.0).astype(np.float32)
    cur = np.where(k <= q, 0.0, -1000.0).astype(np.float32)
    c["maskp"] = np.concatenate([prev, prev, cur, cur], axis=1).astype(bf)
    ms = np.zeros((3, 128, 4, 2, 2, 4, 8), np.float32)
    mn = np.zeros((3, 32, 2, 4, 8), np.float32)
    for g in range(3):
        dil = DILS[g]
        p = np.arange(128)[:, None]
        t = np.arange(8)[None, :]
        base = ((t - p) % dil == 0).astype(np.float32)
        v0 = base * (p >= t)
        for n in range(4):
            for h in range(2):
                ms[g, :, n, 0, h, n, :] = v0
                ms[g, :, n, 1, h, n, :] = base
        for n in range(4):
            for tp in range(8):
                for tq in range(8):
                    if tp <= tq and (tq - tp) % dil == 0:
                        mn[g, n * 8 + tp, :, n, tq] = 1.0
    c["masks"] = ms.reshape(3, 128, 4, 2, 64)[0:2].astype(bf)
    m2 = np.zeros((128, 4, 8, 2, 4, 8), np.float32)
    for n in range(4):
        for j in range(8):
            m2[:, n, j, :, n, j] = 1.0
    c["mask2"] = m2.reshape(128, 4, 8, 64).astype(bf)
    c["maskn"] = mn.reshape(3, 32, 64).astype(bf)
    half = 8
    inv = np.power(np.float32(500000.0), -np.arange(half, dtype=np.float32) / half).astype(np.float32)
    rope = np.zeros((3, 128, 17, 2, 4, 8), np.float32)
    for g in range(3):
        for ti in range(17):
            if ti < 16:
                pos = _tile_positions(g, ti).astype(np.float32)
            else:
                pos = np.zeros(128, np.float32)
                pos[:32] = (8192 + (np.arange(32) % 8)).astype(np.float32)
            ang = (pos[:, None] * inv[None, :]).astype(np.float32)
            rope[g, :, ti, 0] = np.cos(ang)[:, None, :]
            rope[g, :, ti, 1] = np.sin(ang)[:, None, :]
    c["rope"] = rope
    wins = np.array([[2, 4], [8, 16]], np.float32)
    invw = np.zeros((128, 2), np.float32)
    invtab = np.zeros((128, 2, 15), np.float32)
    for pc in range(2):
        for hf in range(2):
            w = wins[pc, hf]
            invw[hf * 64:(hf + 1) * 64, pc] = 1.0 / w
            invtab[hf * 64:(hf + 1) * 64, pc, :] = 1.0 / np.minimum(np.arange(15) + 1, w)
    c["invw"] = invw
    c["invtab"] = invtab
    return c


def build_program(stop=None):
    nc = bass.Bass("TRN2", target_bir_lowering=False)

    def din(name, shape, dt=F32):
        return nc.dram_tensor(name, list(shape), dt, kind="ExternalInput").ap()

    def dout(name, shape):
        return nc.dram_tensor(name, list(shape), F32, kind="ExternalOutput").ap()

    xp = din("xp", [T, D])
    xs = din("xs", [TS, D])
    spool = din("spool", [DEPTH, 4, 15, 256])
    sconv = din("sconv", [DEPTH, 4, 2, 384])
    caches = [din("c128", [DEPTH, 4, 128, 256]), din("c512", [DEPTH, 4, 512, 256]), din("c2048", [DEPTH, 4, 2048, 256])]
    norm1_g = din("norm1_g", [DEPTH, D])
    w_in = din("w_in", [DEPTH, D, 2560])
    q_norm_g = din("q_norm_g", [DEPTH, 64])
    k_norm_g = din("k_norm_g", [DEPTH, 64])
    pool_w = din("pool_w", [DEPTH, 4, 64, 64])
    pool_scale = din("pool_scale", [DEPTH, 256])
    conv_w = din("conv_w", [DEPTH, 3, 384])
    w_out = din("w_out", [DEPTH, 768, D])
    norm2_g = din("norm2_g", [DEPTH, D])
    w_up = din("w_up", [DEPTH, D, DFF])
    w_down = din("w_down", [DEPTH, DFF, D])
    c_ident = din("ident", [128, 128])
    c_maskp = din("maskp", [128, 512], BF16)
    c_masks = din("masks", [2, 128, 4, 2, 64], BF16)
    c_mask2 = din("mask2", [128, 4, 8, 64], BF16)
    c_maskn = din("maskn", [3, 32, 64], BF16)
    c_rope = din("rope", [3, 128, 17, 2, 4, 8])
    c_invw = din("invw", [128, 2])
    c_invtab = din("invtab", [128, 2, 15])

    yp = dout("yp", [T, D])
    ys = dout("ys", [TS, D])
    o_pool_p = dout("pool_p", [DEPTH, 15, 256])
    o_conv_p = dout("conv_p", [DEPTH, 2, 384])
    o_kv_p = [dout("kv128_p", [DEPTH, 128, 256]), dout("kv512_p", [DEPTH, 512, 256]), dout("kv2048_p", [DEPTH, 2048, 256])]
    o_pool_s = dout("pool_s", [DEPTH, 4, 15, 256])
    o_conv_s = dout("conv_s", [DEPTH, 4, 2, 384])
    o_kv_s = [dout("kv128_s", [DEPTH, 4, 128, 256]), dout("kv512_s", [DEPTH, 4, 512, 256]), dout("kv2048_s", [DEPTH, 4, 2048, 256])]
    xpark = nc.dram_tensor("xpark", [NTILE, 128, D], F32, kind="Internal").ap()

    es = ExitStack()
    S = Sched(nc)
    NB = 212800
    big = es.enter_context(nc.sbuf_tensor("big", [128, NB], U8))
    psum = es.enter_context(nc.psum_tensor("psum", [128, 8 * 512], F32))

    def bank(i):
        return psum[:, i * 512:(i + 1) * 512]

    for i in range(8):
        S.region(("ps", i))
    ps_rr = [0]

    ps_held = set()

    def next_ps():
        for _ in range(8):
            i = ps_rr[0]
            ps_rr[0] = (i + 1) % 6
            if i not in ps_held:
                return i
        raise AssertionError("no free PSUM bank")

    def hold_ps():
        i = next_ps()
        ps_held.add(i)
        return i

    def rel_ps(i):
        ps_held.discard(i)

    def hold_ps_pref(order):
        for i in order:
            if i not in ps_held:
                ps_held.add(i)
                return i
        raise AssertionError("no free PSUM bank")

    def hold_ps_pair():
        for p0 in (0, 2, 4):
            if p0 not in ps_held and p0 + 1 not in ps_held:
                ps_held.add(p0)
                ps_held.add(p0 + 1)
                return p0
        raise AssertionError("no free PSUM bank pair")

    def carve(name, off, shape, dt):
        esz = 4 if dt == F32 else 2
        n = int(np.prod(shape[1:])) * esz
        assert off + n <= NB, (name, off, n)
        a = big[:, off:off + n].bitcast(dt)
        if len(shape) == 3:
            a = a.rearrange("p (a b) -> p a b", a=shape[1])
        elif len(shape) == 4:
            a = a.rearrange("p (a b c) -> p a b c", a=shape[1], b=shape[2])
        elif len(shape) == 5:
            a = a.rearrange("p (a b c d) -> p a b c d", a=shape[1], b=shape[2], c=shape[3])
        return a, off + n

    def reg(name, off, nbytes):
        S.region(name, off, off + nbytes)

    off = 0
    IDB, off = carve("idb", off, [128, 128], BF16); reg("idb", off - 256, 256)
    IDF, off = carve("idf", off, [128, 128], F32); reg("idf", off - 512, 512)
    ONESB, off = carve("onesb", off, [128, 128], BF16); reg("onesb", off - 256, 256)
    EPSC, off = carve("epsc", off, [128, 1], F32); reg("epsc", off - 4, 4)
    off = (off + 63) // 64 * 64
    HSEL, off = carve("hsel", off, [128, 2, 128], BF16); reg("hsel", off - 512, 512)
    off = (off + 63) // 64 * 64
    MASKP, off = carve("maskp", off, [128, 512], BF16); reg("maskp", off - 1024, 1024)
    MASKS, off = carve("masks", off, [128, 2, 4, 2, 64], BF16); reg("masks", off - 2048, 2048)
    MASK2, off = carve("mask2", off, [128, 4, 8, 64], BF16); reg("mask2", off - 4096, 4096)
    MASKN, off = carve("maskn", off, [128, 3, 64], BF16); reg("maskn", off - 384, 384)
    INVW, off = carve("invw", off, [128, 2], F32); reg("invw", off - 8, 8)
    INVTAB, off = carve("invtab", off, [128, 2, 15], F32); reg("invtab", off - 120, 120)
    PWBD, off = carve("pwbd", off, [128, 2, 128], BF16); reg("pwbd", off - 512, 512)
    PSCALE, off = carve("pscale", off, [128, 2], F32)
    CONVW, off = carve("convw", off, [128, 3, 3], F32)
    GQK, off = carve("gqk", off, [128, 256], F32)
    K_PS = [("pscale", i) for i in range(2)]
    K_CW = [("convw", i) for i in range(9)]
    K_GQ = [("gqk", i) for i in range(4)]
    for k_ in K_PS + K_CW + K_GQ:
        S.region(k_)
    GBC, off = carve("gbc", off, [128, 1024], F32); reg("gbc", off - 4096, 4096)
    SS, off = carve("ss", off, [128, 17], F32); reg("ss", off - 68, 68)
    RSTD, off = carve("rstd", off, [128, 17], F32); reg("rstd", off - 68, 68)
    off = (off + 63) // 64 * 64
    HB_OFF = off
    HB = []
    for i in range(2):
        a, off = carve("hb", off, [128, 1024], BF16); reg(("hb", i), off - 2048, 2048)
        HB.append(a)
    ARENA_OFF = off
    ARENA, off = carve("arena", off, [128, 32768], BF16)
    HT, off = carve("ht", off, [128, 8, NTOK], BF16)
    for t in range(NTILE):
        S.region(("ht", t))
    X_OFF = off
    X, off = carve("x", off, [128, NTILE, D], F32)
    for t in range(NTILE):
        reg(("x", t), X_OFF + t * 4096, 4096)
    MIX_OFF = off
    MIXT, off = carve("mixt", off, [128, 6, NTOK], BF16)
    for c in range(6):
        for b in range(5):
            lo = MIX_OFF + (c * NTOK + 512 * b) * 2
            reg(("mix", c, b), lo, lo + (1024 if b < 4 else 64) - lo + lo - lo)
    for c in range(6):
        for b in range(5):
            r = S.regions[("mix", c, b)]
            r.lo = MIX_OFF + (c * NTOK + 512 * b) * 2
            r.hi = r.lo + (1024 if b < 4 else 64)
    assert off <= NB, off
    TAIL_OFF = off

    o2 = X_OFF
    U, o2 = carve("u", o2, [128, 2, 527], F32)
    for pc in range(2):
        reg(("u", pc), X_OFF + pc * 527 * 4, 527 * 4)
    SB = []
    for pc in range(2):
        lst = []
        for nm in (("s2", "s4") if pc == 0 else ("s2", "s4", "s8", "s16")):
            a_, o2 = carve(nm, o2, [128, 527], F32); reg((nm, pc), o2 - 2108, 2108)
            lst.append(a_)
        SB.append(lst)
    TMP15, DTB = [], []
    for pc in range(2):
        a_, o2 = carve("tmp15", o2, [128, 16], F32); reg(("tmp15", pc), o2 - 64, 64)
        TMP15.append(a_)
    o2 = (o2 + 63) // 64 * 64
    for pc in range(2):
        a_, o2 = carve("dt", o2, [128, 512], BF16); reg(("dt", pc), o2 - 1024, 1024)
        DTB.append(a_)
    ZOFF = o2
    Z, o2 = carve("z", o2, [128, 3, 514], F32)
    for cc in range(3):
        reg(("z", cc), ZOFF + cc * 514 * 4, 514 * 4)
    GCS, CACC = [], []
    for cc in range(3):
        a_, o2 = carve("gcs", o2, [128, 512], F32); reg(("gcs", cc), o2 - 2048, 2048)
        GCS.append(a_)
        a_, o2 = carve("cacc", o2, [128, 512], F32); reg(("cacc", cc), o2 - 2048, 2048)
        CACC.append(a_)
    US, o2 = carve("us", o2, [128, 2, 4, 23], F32); reg("us", o2 - 736, 736)
    ZS, o2 = carve("zs", o2, [128, 3, 4, 10], F32); reg("zs", o2 - 480, 480)
    TS32, o2 = carve("ts32", o2, [128, 32], F32); reg("ts32", o2 - 128, 128)
    STF, o2 = carve("stf", o2, [128, 384], F32); reg("stf", o2 - 1536, 1536)
    OUTS, o2 = carve("outs", o2, [128, 384], F32); reg("outs", o2 - 1536, 1536)
    OUTS2, o2 = carve("outs2", o2, [128, 384], F32); reg("outs2", o2 - 1536, 1536)
    OUTS_S, o2 = carve("outs_s", o2, [128, 384], F32); reg("outs_s", o2 - 1536, 1536)
    OUTS2_S, o2 = carve("outs2_s", o2, [128, 384], F32); reg("outs2_s", o2 - 1536, 1536)
    TS32B, o2 = carve("ts32b", o2, [128, 5, 32], F32)
    for i in range(5):
        reg(("ts32b", i), o2 - 640 + 128 * i, 128)
    assert o2 <= X_OFF + NTILE * 4096

    o3 = X_OFF
    QKT_S, VAUG_S, ROPE_S = [None, None], [None, None], [None, None]
    QKT_S[0], o3 = carve("qkt", o3, [128, 3, NTOK], BF16)
    VOFF = o3
    VAUG_S[0], o3 = carve("vaug", o3, [128, NTILE, 2, 65], BF16)
    o3 = (o3 + 63) // 64 * 64
    ROFF = o3
    ROPE_S[0], o3 = carve("rope", o3, [128, 17, 2, 4, 8], F32)
    oa = ARENA_OFF
    Q1OFF = oa
    QKT_S[1], oa = carve("qkt1", oa, [128, 3, NTOK], BF16)
    V1OFF = oa
    VAUG_S[1], oa = carve("vaug1", oa, [128, NTILE, 2, 65], BF16)
    oa = (oa + 63) // 64 * 64
    R1OFF = oa
    ROPE_S[1], oa = carve("rope1", oa, [128, 17, 2, 4, 8], F32)
    assert oa <= ARENA_OFF + 11264 * 2, oa
    for sg, (qo, vo, ro) in enumerate(((X_OFF, VOFF, ROFF), (Q1OFF, V1OFF, R1OFF))):
        for a in range(3):
            for ti in range(NTILE):
                reg(("qkt", sg, a, ti), qo + (a * NTOK + ti * 128) * 2, 256 if ti < 16 else 64)
        for ti in range(NTILE):
            reg(("vaug", sg, ti), vo + ti * 260, 260)
        reg(("rope", sg), ro, 4352)
    AOFF = o3
    ACC, o3 = carve("acc", o3, [128, 2, T], F32); reg("acc", AOFF, o3 - AOFF)
    QKVF = []
    for i in range(3):
        a_, o3 = carve("qkvf", o3, [128, 2, 384], F32); reg(("qkvf", i), o3 - 3072, 3072)
        QKVF.append(a_)
    oq = ARENA_OFF + 28672 * 2
    for i in range(2):
        a_, oq = carve("qkvf", oq, [128, 2, 384], F32); reg(("qkvf", 3 + i), oq - 3072, 3072)
        QKVF.append(a_)
    NQ = len(QKVF)
    SQ = []
    for i in range(1):
        a_, o3 = carve("sq", o3, [128, 2, 256], F32); reg(("sq", i), o3 - 2048, 2048)
        SQ.append(a_)
    S4OFF = o3
    SS4, o3 = carve("ss4", o3, [128, 4, 8], F32)
    R4OFF = o3
    RS4, o3 = carve("rs4", o3, [128, 4, 8], F32)
    for i in range(4):
        reg(("ss4", i), S4OFF + 32 * i, 32)
        reg(("rs4", i), R4OFF + 32 * i, 32)
    ROT = []
    for i in range(1):
        a_, o3 = carve("rot", o3, [128, 4, 2, 4, 8], F32); reg(("rot", i), o3 - 1024, 1024)
        ROT.append(a_)
    QKB = []
    for i in range(2):
        a_, o3 = carve("qkb", o3, [128, 2, 3, 128], BF16); reg(("qkb", i), o3 - 1536, 1536)
        QKB.append(a_)
    CST = []
    for i in range(2):
        a_, o3 = carve("cst", o3, [128, 4, 256], F32); reg(("cst", i), o3 - 4096, 4096)
        CST.append(a_)
    KB_, VS, KTS, PSB = [], [], [], []
    for i in range(2):
        a_, o3 = carve("kb", o3, [128, 4, 128], BF16); reg(("kb", i), o3 - 1024, 1024)
        KB_.append(a_)
        a_, o3 = carve("vs", o3, [128, 4, 2, 65], BF16); reg(("vs", i), o3 - 1040, 1040)
        VS.append(a_)
        o3 = (o3 + 63) // 64 * 64
        a_, o3 = carve("kts", o3, [128, 512], BF16); reg(("kts", i), o3 - 1024, 1024)
        KTS.append(a_)
        a_, o3 = carve("psb", o3, [128, 4, 64], BF16); reg(("psb", i), o3 - 512, 512)
        PSB.append(a_)
    QBDG = []
    for i in range(3):
        a_, o3 = carve("qbd", o3, [128, 64], BF16); reg(("qbd", i), o3 - 128, 128)
        QBDG.append(a_)
    PN, o3 = carve("pn", o3, [128, 64], BF16); reg("pn", o3 - 128, 128)
    RD, o3 = carve("rd", o3, [128, 2], F32); reg("rd", o3 - 8, 8)
    o3 = (o3 + 63) // 64 * 64
    YSB, o3 = carve("ysb", o3, [128, 128], BF16); reg("ysb", o3 - 256, 256)
    assert o3 <= X_OFF + NTILE * 4096, (o3 - X_OFF - NTILE * 4096)
    PB = []
    for i in range(4):
        a_, _ = carve("pb", HB_OFF + 1024 * i, [128, 512], BF16); reg(("pb", i), HB_OFF + 1024 * i, 1024)
        PB.append(a_)

    AT = []
    o4 = MIX_OFF
    for i in range(2):
        a, o4 = carve("at", o4, [128, 8, 512], BF16); reg(("at", i), o4 - 8192, 8192)
        AT.append(a)
    RL = []
    for i in range(2):
        a, o4 = carve("rl", o4, [128, 512], F32); reg(("rl", i), o4 - 2048, 2048)
        RL.append(a)
    assert o4 <= MIX_OFF + 6 * NTOK * 2

    def arena_piece(name, col, shape):
        n = int(np.prod(shape[1:]))
        a = ARENA[:, col:col + n]
        if len(shape) == 3:
            a = a.rearrange("p (a b) -> p a b", a=shape[1])
        reg(name, ARENA_OFF + col * 2, n * 2)
        return a

    WA = arena_piece("wa", 0, [128, 8, 1408])
    WQ = [arena_piece(("wq", g), 12288 + 3072 * g, [128, 8, 384]) for g in range(3)]
    WO = arena_piece("wo", 22528, [128, 6, 1024])
    WU = [arena_piece(("wu", i), 16384 * i, [128, 8, 1024]) for i in range(2)]
    WD = [arena_piece(("wd", i), 16384 * i + 8192, [128, 8, 1024]) for i in range(2)]

    for t in range(NTILE):
        S.region(("park", t))

    def tile_rows(t):
        return 128 if t < 16 else 32

    def tile_cols(t):
        return slice(128 * t, 128 * t + tile_rows(t))

    def blk_cols(b):
        return slice(512 * b, 512 * b + (512 if b < 4 else 32))

    def blk_len(b):
        return 512 if b < 4 else 32

    S.dma(lambda e: e.dma_start(out=IDF, in_=c_ident), writes=["idf"], slot="c0")
    S.dma(lambda e: e.dma_start(out=MASKP, in_=c_maskp), writes=["maskp"], slot="c1")
    S.dma(lambda e: e.dma_start(out=MASKS, in_=c_masks.rearrange("g p n v c -> p g n v c")), writes=["masks"], slot="c2")
    S.dma(lambda e: e.dma_start(out=MASKN[0:32], in_=c_maskn.rearrange("g p c -> p g c")), writes=["maskn"], slot="c3")
    S.dma(lambda e: e.dma_start(out=INVW, in_=c_invw), writes=["invw"], slot="c4")
    S.dma(lambda e: e.dma_start(out=INVTAB, in_=c_invtab), writes=["invtab"], slot="c5")
    S.op("dve", lambda e: e.tensor_copy(out=IDB, in_=IDF), reads=["idf"], writes=["idb"])
    S.op("dve", lambda e: e.memset(ONESB, 1.0), writes=["onesb"])
    S.op("dve", lambda e: e.memset(EPSC, EPS), writes=["epsc"])
    S.op("dve", lambda e: e.memset(HSEL, 0.0), writes=["hsel"])
    for h in range(2):
        S.op("dve", lambda e, h=h: e.memset(HSEL[:, h, 64 * h:64 * h + 64], 1.0), writes=["hsel"])
    S.dma(lambda e: e.dma_start(out=MASK2, in_=c_mask2), writes=["mask2"], slot="c6")
    S.op("dve", lambda e: e.memset(PWBD, 0.0), writes=["pwbd"])
    for g in range(3):
        W = WINS[g]
        S.dma(lambda e, g=g, W=W: e.dma_start(out=o_kv_s[g][:, :, 0:W - 8, :], in_=caches[g][:, :, 8:W, :]),
              slot=("c2c", g))
    S.dma(lambda e: e.dma_start(out=o_pool_s[:, :, 0:7, :], in_=spool[:, :, 8:15, :]), slot="c2cp")
    for t in range(NTILE):
        src = xp[128 * t:128 * t + 128, :] if t < 16 else xs
        R = tile_rows(t)
        S.dma(lambda e, t=t, src=src, R=R: e.dma_start(out=X[0:R, t, :], in_=src), writes=[("x", t)], slot=("xl", t))

    wq_state = {"n": 0}

    def wdma(fn, writes, slot):
        S.dma(fn, writes=writes, slot=slot, queue="pool")

    def load_wa(l):
        wv = w_in[l].rearrange("(c p) n -> p c n", p=128)
        wdma(lambda e: e.dma_start(out=WA[:, :, 0:256], in_=wv[:, :, 0:256]), ["wa"], "wa")
        for h in range(4):
            wdma(lambda e, h=h: e.dma_start(out=WA[:, 2 * h:2 * h + 2, 256:1408], in_=wv[:, 2 * h:2 * h + 2, 1408:2560]), ["wa"], "wa")

    def load_wq(l):
        wv = w_in[l].rearrange("(c p) n -> p c n", p=128)
        for g in range(3):
            for j in range(3):
                c0 = 256 + 384 * j + 128 * g
                wdma(lambda e, g=g, j=j, c0=c0: e.dma_start(out=WQ[g][:, :, 128 * j:128 * j + 128], in_=wv[:, :, c0:c0 + 128]),
                     [("wq", g)], ("wq", g))

    def load_wo(l):
        wv = w_out[l].rearrange("(c p) n -> p c n", p=128)
        for h in range(3):
            wdma(lambda e, h=h: e.dma_start(out=WO[:, 2 * h:2 * h + 2, :], in_=wv[:, 2 * h:2 * h + 2, :]), ["wo"], "wo")

    def load_ffn(l, fb):
        i = fb % 2
        wu = w_up[l].rearrange("(c p) f -> p c f", p=128)
        wd = w_down[l][fb * 1024:(fb + 1) * 1024, :].rearrange("(c p) n -> p c n", p=128)
        for h in range(2):
            wdma(lambda e, h=h: e.dma_start(out=WU[i][:, 4 * h:4 * h + 4, :], in_=wu[:, 4 * h:4 * h + 4, fb * 1024:(fb + 1) * 1024]),
                 [("wu", i)], ("wu", i))
        for h in range(2):
            wdma(lambda e, h=h: e.dma_start(out=WD[i][:, 4 * h:4 * h + 4, :], in_=wd[:, 4 * h:4 * h + 4, :]),
                 [("wd", i)], ("wd", i))

    def load_smalls(l):
        for g in range(4):
            pc, hf = g // 2, g % 2
            wdma(lambda e, g=g, pc=pc, hf=hf: e.dma_start(out=PWBD[hf * 64:(hf + 1) * 64, pc, hf * 64:(hf + 1) * 64], in_=pool_w[l, g]),
                 ["pwbd"], "pwbd")
        for pc in range(2):
            S.dma(lambda e, pc=pc: e.dma_start(out=PSCALE[:, pc:pc + 1], in_=pool_scale[l, pc * 128:(pc + 1) * 128].rearrange("(p o) -> p o", o=1)),
                  writes=[("pscale", pc)], slot=("sm_ps", pc))
        for cc in range(3):
            for k in range(3):
                S.dma(lambda e, cc=cc, k=k: e.dma_start(out=CONVW[:, cc, k:k + 1], in_=conv_w[l, k, cc * 128:(cc + 1) * 128].rearrange("(p o) -> p o", o=1)),
                      writes=[("convw", 3 * cc + k)], slot=("sm_cw", 3 * cc + k))
        for j in range(4):
            src = q_norm_g if j < 2 else k_norm_g
            S.dma(lambda e, j=j, src=src: e.dma_start(out=GQK[:, 64 * j:64 * j + 64], in_=src[l].partition_broadcast(128)),
                  writes=[("gqk", j)], slot=("sm_gq", j))

    def load_gbc(gvec):
        S.dma(lambda e: e.dma_start(out=GBC, in_=gvec.partition_broadcast(128)), writes=["gbc"], slot="gbc")

    def norm_and_transpose(gvec, pre=None, have_ss=False):
        if not have_ss:
            load_gbc(gvec)
        if not have_ss:
            S.op("dve", lambda e: e.memset(SS, 0.0), writes=["ss"])
        for t in range(NTILE):
            if have_ss:
                break
            R = tile_rows(t)
            if pre is not None:
                pre(t)
            hb = HB[t % 2]
            S.op("act", lambda e, t=t, R=R, hb=hb: e.activation(out=hb[0:R, :], in_=X[0:R, t, :], func=AF.Square, accum_out=SS[0:R, t:t + 1]),
                 reads=[("x", t)], writes=[("hb", t % 2), "ss"])
        S.op("dve", lambda e: e.tensor_scalar(out=RSTD, in0=SS, scalar1=1.0 / D, scalar2=EPS, op0=ALU.mult, op1=ALU.add),
             reads=["ss"], writes=["rstd"])
        S.op("act", lambda e: e.activation(out=RSTD, in_=RSTD, func=AF.Ln), reads=["rstd"], writes=["rstd"])
        S.op("act", lambda e: e.activation(out=RSTD, in_=RSTD, func=AF.Exp, scale=-0.5), reads=["rstd"], writes=["rstd"])
        for t in range(NTILE):
            R = tile_rows(t)
            hb = HB[t % 2]
            S.op("dve", lambda e, t=t, R=R, hb=hb: e.scalar_tensor_tensor(out=hb[0:R, :], in0=X[0:R, t, :], scalar=RSTD[0:R, t:t + 1],
                                                                       in1=GBC[0:R, :], op0=ALU.mult, op1=ALU.mult),
                 reads=[("x", t), "rstd", "gbc"], writes=[("hb", t % 2)])
            pi = next_ps()
            pst = bank(pi)[:, 0:512].bitcast(BF16).rearrange("p (a b) -> p a b", a=8)
            for c in range(8):
                S.op("pe", lambda e, c=c, R=R, hb=hb, pst=pst: e.transpose(out=pst[:, c, 0:R], in_=hb[0:R, c * 128:(c + 1) * 128], identity=IDB[0:R, 0:R]),
                     reads=[("hb", t % 2), "idb"], writes=[("ps", pi)])
            S.op("act", lambda e, t=t, R=R, pst=pst: e.copy(out=HT[:, :, 128 * t:128 * t + R], in_=pst[:, :, 0:R]),
                 reads=[("ps", pi)], writes=[("ht", t)])

    def pipeline(items, lag=1, depth=4):
        active = []
        pending = list(items)
        tick = 0
        while pending or active:
            assert tick < 200000, "pipeline deadlock"

            started = None
            if pending and len(active) < depth:
                nxt_item = pending[0]
                key_ = nxt_item[0] if isinstance(nxt_item, tuple) else None
                if key_ is None or all(a_[2] != key_ for a_ in active):
                    pending.pop(0)
                    g_ = (nxt_item[1] if isinstance(nxt_item, tuple) else nxt_item)()
                    try:
                        next(g_)
                        started = [g_, tick + lag, key_]
                    except StopIteration:
                        pass
            for a_ in list(active):
                if a_[1] <= tick:
                    try:
                        next(a_[0])
                        a_[1] = tick + lag
                    except StopIteration:
                        active.remove(a_)
            if started:
                active.append(started)
            tick += 1

    def phase_a2a(l):
        S.dma(lambda e: e.dma_start(out=STF[0:60, 0:256], in_=spool[l].rearrange("n r c -> (n r) c")), writes=["stf"], slot="st1")
        for pc in range(2):
            pi = next_ps()
            S.op("pe", lambda e, pc=pc, pi=pi: e.transpose(out=bank(pi)[:, 0:60], in_=STF[0:60, pc * 128:(pc + 1) * 128], identity=IDF[0:60, 0:60]),
                 reads=["stf", "idf"], writes=[("ps", pi)])
            S.op("act", lambda e, pc=pc, pi=pi: e.copy(out=US[:, pc, :, 0:15], in_=bank(pi)[:, 0:60].rearrange("p (n r) -> p n r", n=4)),
                 reads=[("ps", pi)], writes=["us"])
        S.dma(lambda e: e.dma_start(out=OUTS2[0:8, 0:384], in_=sconv[l].rearrange("n r c -> (n r) c")), writes=["outs2"], slot="st2")
        for cc in range(3):
            pi = next_ps()
            S.op("pe", lambda e, cc=cc, pi=pi: e.transpose(out=bank(pi)[:, 0:8], in_=OUTS2[0:8, cc * 128:(cc + 1) * 128], identity=IDF[0:8, 0:8]),
                 reads=["outs2", "idf"], writes=[("ps", pi)])
            S.op("act", lambda e, cc=cc, pi=pi: e.copy(out=ZS[:, cc, :, 0:2], in_=bank(pi)[:, 0:8].rearrange("p (n r) -> p n r", n=4)),
                 reads=[("ps", pi)], writes=["zs"])
        S.op("dve", lambda e: e.memset(U[:, :, 0:15], 0.0), writes=[("u", 0), ("u", 1)])
        S.op("dve", lambda e: e.memset(Z[:, :, 0:2], 0.0), writes=[("z", 0), ("z", 1), ("z", 2)])

        def proj(col0, b):
            pi = hold_ps()
            L = blk_len(b)
            for d in range(8):
                S.op("pe", lambda e, d=d, pi=pi, L=L: e.matmul(bank(pi)[:, 0:L], lhsT=WA[:, d, col0:col0 + 128], rhs=HT[:, d, blk_cols(b)],
                                                               start=(d == 0), stop=(d == 7)),
                     reads=["wa"] + [("ht", t) for t in (range(4 * b, 4 * b + 4) if b < 4 else [16])], writes=[("ps", pi)])
            return pi

        def pool_item(b, pc):
            L = blk_len(b)
            smp = (b == 4)
            pi = proj(128 * pc, b)
            yield
            ukey = "us" if smp else ("u", pc)
            k2, k4, k8, k16, kdt, ktm = ("s2", pc), ("s4", pc), ("s8", pc), ("s16", pc), ("dt", pc), ("tmp15", pc)
            if smp:
                Uv = US[:, pc]
                W = 23
                S.op("act", lambda e: e.copy(out=US[:, pc, :, 15:23], in_=bank(pi)[:, 0:32].rearrange("p (n r) -> p n r", n=4)),
                     reads=[("ps", pi)], writes=["us"])

                def v3(ap):
                    return ap[:, 0:92].rearrange("p (n r) -> p n r", n=4)
                dt = DTB[pc][:, 0:32].rearrange("p (n r) -> p n r", n=4)

                def sl(ap, a, b_):
                    return ap[:, :, a:b_]
            else:
                Uv = U[:, pc]
                W = 527
                if b > 0:
                    S.op("pool", lambda e: e.tensor_copy(out=U[:, pc, 0:15], in_=U[:, pc, 512:527]), reads=[("u", pc)], writes=[("u", pc)])
                S.op("act", lambda e: e.copy(out=U[:, pc, 15:527], in_=bank(pi)[:, 0:512]), reads=[("ps", pi)], writes=[("u", pc)])

                def v3(ap):
                    return ap
                dt = DTB[pc]

                def sl(ap, a, b_):
                    return ap[:, a:b_]
            rel_ps(pi)
            s2, s4 = v3(SB[pc][0]), v3(SB[pc][1])
            yield
            S.op("pool", lambda e: e.tensor_tensor(out=sl(s2, 1, W), in0=sl(Uv, 1, W), in1=sl(Uv, 0, W - 1), op=ALU.add), reads=[ukey], writes=[k2])
            yield
            S.op("pool", lambda e: e.tensor_tensor(out=sl(s4, 3, W), in0=sl(s2, 3, W), in1=sl(s2, 1, W - 2), op=ALU.add), reads=[k2], writes=[k4])
            yield
            if pc == 0:
                wlo, whi, klo, khi = s2, s4, k2, k4
            else:
                s8, s16 = v3(SB[pc][2]), v3(SB[pc][3])
                S.op("pool", lambda e: e.tensor_tensor(out=sl(s8, 7, W), in0=sl(s4, 7, W), in1=sl(s4, 3, W - 4), op=ALU.add), reads=[k4], writes=[k8])
                yield
                S.op("pool", lambda e: e.tensor_tensor(out=sl(s16, 15, W), in0=sl(s8, 15, W), in1=sl(s8, 7, W - 8), op=ALU.add), reads=[k8], writes=[k16])
                yield
                wlo, whi, klo, khi = s8, s16, k8, k16
            for (hf, win, wk) in ((0, wlo, klo), (1, whi, khi)):
                ps_ = slice(hf * 64, hf * 64 + 64)
                S.op("dve", lambda e, ps_=ps_, win=win: e.scalar_tensor_tensor(
                    out=dt[ps_], in0=sl(win, 15, W)[ps_], scalar=INVW[ps_, pc:pc + 1], in1=sl(Uv, 15, W)[ps_], op0=ALU.mult, op1=ALU.subtract),
                    reads=[wk, ukey, "invw"], writes=[kdt])
            if b == 0:
                yield
                for (hf, win, wk) in ((0, wlo, klo), (1, whi, khi)):
                    ps_ = slice(hf * 64, hf * 64 + 64)
                    S.op("dve", lambda e, ps_=ps_, win=win: e.tensor_tensor(out=TMP15[pc][ps_, 0:15], in0=win[ps_, 15:30], in1=INVTAB[ps_, pc, :], op=ALU.mult),
                         reads=[wk, "invtab"], writes=[ktm])
                yield
                S.op("dve", lambda e: e.tensor_tensor(out=DTB[pc][:, 0:15], in0=TMP15[pc][:, 0:15], in1=U[:, pc, 15:30], op=ALU.subtract),
                     reads=[ktm, ("u", pc)], writes=[kdt])
            yield
            po = hold_ps()
            S.op("pe", lambda e: e.matmul(bank(po)[:, 0:L], lhsT=PWBD[:, pc, :], rhs=DTB[pc][:, 0:L], start=True, stop=True),
                 reads=["pwbd", kdt], writes=[("ps", po)])
            yield
            S.op("act", lambda e: e.activation(out=MIXT[:, pc, blk_cols(b)], in_=bank(po)[:, 0:L], func=AF.Copy, scale=PSCALE[:, pc:pc + 1]),
                 reads=[("ps", po)] + K_PS, writes=[("mix", pc, b)])
            rel_ps(po)

        def conv_item(b, cc):
            L = blk_len(b)
            smp = (b == 4)
            pgc = proj(256 + 384 + 128 * cc, b)
            pgh = proj(256 + 768 + 128 * cc, b)
            yield
            zkey = "zs" if smp else ("z", cc)
            kg, kc = ("gcs", cc), ("cacc", cc)
            if smp:
                Zv = ZS[:, cc]
                Wz = 10

                def v3(ap):
                    return ap[:, 0:32].rearrange("p (n r) -> p n r", n=4)

                def sz(a, b_, Zv=Zv):
                    return Zv[:, :, a:b_]
            else:
                Zv = Z[:, cc]
                Wz = 514
                if b > 0:
                    S.op("pool", lambda e: e.tensor_copy(out=Z[:, cc, 0:2], in_=Z[:, cc, 512:514]), reads=[("z", cc)], writes=[("z", cc)])

                def v3(ap):
                    return ap[:, 0:512]

                def sz(a, b_, Zv=Zv):
                    return Zv[:, a:b_]
            Lq = Wz - 2
            S.op("act", lambda e: e.copy(out=v3(GCS[cc]), in_=v3(bank(pgc))), reads=[("ps", pgc)], writes=[kg])
            rel_ps(pgc)
            yield
            S.op("dve", lambda e: e.tensor_tensor(out=sz(2, Wz), in0=v3(bank(pgh)), in1=v3(GCS[cc]), op=ALU.mult), reads=[("ps", pgh), kg], writes=[zkey])
            rel_ps(pgh)
            yield
            S.op("pool", lambda e: e.tensor_scalar(out=v3(CACC[cc]), in0=sz(0, Lq), scalar1=CONVW[:, cc, 0:1], scalar2=None, op0=ALU.mult),
                 reads=[zkey] + K_CW, writes=[kc])
            pgb = proj(256 + 128 * cc, b)
            yield
            for k in (1, 2):
                S.op("dve", lambda e, k=k: e.scalar_tensor_tensor(out=v3(CACC[cc]), in0=sz(k, k + Lq), scalar=CONVW[:, cc, k:k + 1], in1=v3(CACC[cc]),
                                                                  op0=ALU.mult, op1=ALU.add), reads=[zkey, kc] + K_CW, writes=[kc])
                yield
            mo = MIXT[:, 3 + cc, blk_cols(b)]
            if smp:
                mo = mo.rearrange("p (n r) -> p n r", n=4)
            S.op("dve", lambda e: e.tensor_tensor(out=mo, in0=v3(bank(pgb)), in1=v3(CACC[cc]), op=ALU.mult), reads=[("ps", pgb), kc], writes=[("mix", 3 + cc, b)])
            rel_ps(pgb)

        items = []
        for b in range(5):
            for pc in range(2):
                items.append((("p", pc), lambda b=b, pc=pc: pool_item(b, pc)))
            for cc in range(3):
                items.append((("c", cc), lambda b=b, cc=cc: conv_item(b, cc)))
        pipeline(items, depth=5)

        for smp in (False, True):
            o1, k1 = (OUTS_S, "outs_s") if smp else (OUTS, "outs")
            o2_, k2 = (OUTS2_S, "outs2_s") if smp else (OUTS2, "outs2")
            for pc in range(2):
                if smp:
                    S.op("pool", lambda e, pc=pc: e.tensor_copy(out=TS32B[:, pc].rearrange("p (n r) -> p n r", n=4), in_=US[:, pc, :, 15:23]), reads=["us"], writes=[("ts32b", pc)])
                    src, sk = TS32B[:, pc], ("ts32b", pc)
                else:
                    src, sk = U[:, pc, 495:527], ("u", pc)
                pi = next_ps()
                S.op("pe", lambda e, src=src, pi=pi: e.transpose(out=bank(pi)[0:32, 0:128], in_=src, identity=IDF), reads=[sk, "idf"], writes=[("ps", pi)])
                S.op("act", lambda e, pc=pc, pi=pi, o1=o1: e.copy(out=o1[0:32, pc * 128:(pc + 1) * 128], in_=bank(pi)[0:32, 0:128]), reads=[("ps", pi)], writes=[k1])
            if smp:
                for n in range(4):
                    S.dma(lambda e, n=n: e.dma_start(out=o_pool_s[l, n, 7:15, :], in_=OUTS_S[8 * n:8 * n + 8, 0:256]), reads=[k1], slot="so1s")
            else:
                S.dma(lambda e: e.dma_start(out=o_pool_p[l], in_=OUTS[17:32, 0:256]), reads=[k1], slot="so1")
            for cc in range(3):
                if smp:
                    S.op("pool", lambda e, cc=cc: e.tensor_copy(out=TS32B[:, 2 + cc].rearrange("p (n r) -> p n r", n=4), in_=ZS[:, cc, :, 2:10]), reads=["zs"], writes=[("ts32b", 2 + cc)])
                    src, sk = TS32B[:, 2 + cc], ("ts32b", 2 + cc)
                else:
                    src, sk = Z[:, cc, 482:514], ("z", cc)
                pi = next_ps()
                S.op("pe", lambda e, src=src, pi=pi: e.transpose(out=bank(pi)[0:32, 0:128], in_=src, identity=IDF), reads=[sk, "idf"], writes=[("ps", pi)])
                S.op("act", lambda e, cc=cc, pi=pi, o2_=o2_: e.copy(out=o2_[0:32, cc * 128:(cc + 1) * 128], in_=bank(pi)[0:32, 0:128]), reads=[("ps", pi)], writes=[k2])
            if smp:
                for n in range(4):
                    S.dma(lambda e, n=n: e.dma_start(out=o_conv_s[l, n], in_=OUTS2_S[8 * n + 6:8 * n + 8, 0:384]), reads=[k2], slot="so2s")
            else:
                S.dma(lambda e: e.dma_start(out=o_conv_p[l], in_=OUTS2[30:32, 0:384]), reads=[k2], slot="so2")

    class ResPool:
        def __init__(self, n):
            self.free = list(range(n))

        def acquire(self):
            while not self.free:
                yield
            return self.free.pop(0)

        def release(self, i):
            self.free.append(i)

    def acq_bank(pref=(0, 1, 2, 3, 4, 5)):
        while True:
            for i in pref:
                if i not in ps_held:
                    ps_held.add(i)
                    return i
            yield

    def acq_pair():
        while True:
            for p0 in (0, 2, 4):
                if p0 not in ps_held and p0 + 1 not in ps_held:
                    ps_held.add(p0)
                    ps_held.add(p0 + 1)
                    return p0
            yield

    def phase_a2b(l):
        for i in range(3):
            S.op("dve", lambda e, i=i: e.memset(QBDG[i], 0.0), writes=[("qbd", i)])
        for i in range(2):
            S.op("dve", lambda e, i=i: e.memset(VS[i], 1.0), writes=[("vs", i)])
        for i in range(2):
            S.op("dve", lambda e, i=i: e.memset(QKB[i], 0.0), writes=[("qkb", i)])
        PSA = (7, 6)
        psa = [bank(PSA[h])[0:32, 0:65] for h in range(2)]
        first_pv = [True, True]
        P_Q, P_SQ, P_S4, P_ROT, P_QKB = ResPool(NQ), ResPool(1), ResPool(4), ResPool(1), ResPool(2)
        P_PB, P_CST, P_KI = ResPool(4), ResPool(2), ResPool(2)

        def qkv_items(g):
            W = WINS[g]
            sg = g % 2
            QKT, VAUG, ROPE = QKT_S[sg], VAUG_S[sg], ROPE_S[sg]
            kq = lambda a, ti: ("qkt", sg, a, ti)
            kv = lambda ti: ("vaug", sg, ti)
            krope = ("rope", sg)
            S.dma(lambda e: e.dma_start(out=ROPE, in_=c_rope[g]), writes=[krope], slot=("rope", sg))
            S.op("dve", lambda e: e.memset(VAUG, 1.0), writes=[kv(ti) for ti in range(NTILE)])

            def qkv_tile(tis):
                J = len(tis)
                R = tile_rows(tis[0])
                t0_ = tis[0]
                pb0 = (yield from acq_pair()) if J == 2 else (yield from acq_bank())
                pqv = psum[:, pb0 * 512:(pb0 + J) * 512].rearrange("p (j c) -> p j c", j=J)
                pkeys = [("ps", pb0 + j) for j in range(J)]
                for j, ti in enumerate(tis):
                    if ti == 16:
                        csel = slice(T, T + 32)
                        htk = [("ht", 16)]
                    elif g == 0:
                        csel = slice(128 * ti, 128 * ti + 128)
                        htk = [("ht", ti)]
                    elif g == 1:
                        r, kb = ti // 4, ti % 4
                        csel = slice(512 * kb + r, 512 * kb + 512, 4)
                        htk = [("ht", 4 * kb + jj) for jj in range(4)]
                    else:
                        csel = slice(ti, T, 16)
                        htk = [("ht", jj) for jj in range(16)]
                    for d in range(8):
                        S.op("pe", lambda e, d=d, j=j, csel=csel: e.matmul(pqv[0:R, j, 0:384], lhsT=HT[:, d, csel], rhs=WQ[g][:, d, :], start=(d == 0), stop=(d == 7)),
                             reads=[("wq", g)] + htk, writes=[pkeys[j]])
                yield
                qi = yield from P_Q.acquire()
                si = yield from P_SQ.acquire()
                qf = QKVF[qi][:, 0:J, :]
                qk = ("qkvf", qi)
                S.op("act", lambda e: e.copy(out=qf[0:R], in_=pqv[0:R, :, 0:384]), reads=pkeys, writes=[qk])
                sq = SQ[si][:, 0:J, :]
                S.op("act", lambda e: e.activation(out=sq[0:R], in_=pqv[0:R, :, 0:256], func=AF.Square), reads=pkeys, writes=[("sq", si)])
                for j in range(J):
                    rel_ps(pb0 + j)
                yield
                s4 = yield from P_S4.acquire()
                ss = SS4[:, s4, 0:4 * J]
                rs = RS4[:, s4, 0:4 * J]
                S.op("dve", lambda e: e.tensor_reduce(out=ss[0:R], in_=sq[0:R].rearrange("p j (a b) -> p (j a) b", a=4), axis=AX.X, op=ALU.add),
                     reads=[("sq", si)], writes=[("ss4", s4)])
                P_SQ.release(si)
                yield
                S.op("act", lambda e: e.activation(out=rs[0:R], in_=ss[0:R], func=AF.Ln, scale=1.0 / 64, bias=EPSC[0:R, :]),
                     reads=[("ss4", s4), "epsc"], writes=[("rs4", s4)])
                yield
                S.op("act", lambda e: e.activation(out=rs[0:R], in_=rs[0:R], func=AF.Exp, scale=-0.5), reads=[("rs4", s4)], writes=[("rs4", s4)])
                yield
                qk4 = qf[:, :, 0:256].rearrange("p j (a b) -> p j a b", a=4)
                rs4b = rs.rearrange("p (j a) -> p j a", j=J).unsqueeze(3)
                S.op("dve", lambda e: e.tensor_tensor(out=qk4[0:R], in0=qk4[0:R], in1=rs4b[0:R].to_broadcast([R, J, 4, 64]), op=ALU.mult),
                     reads=[qk, ("rs4", s4)], writes=[qk])
                P_S4.release(s4)
                yield
                S.op("dve", lambda e: e.tensor_tensor(out=qf[0:R, :, 0:256], in0=qf[0:R, :, 0:256], in1=GQK[0:R].unsqueeze(1).to_broadcast([R, J, 256]), op=ALU.mult),
                     reads=[qk] + K_GQ, writes=[qk])
                yield
                ri = yield from P_ROT.acquire()
                rot = ROT[ri][:, :, 0:J]
                x1 = qk4[0:R, :, :, 0:8]
                x2 = qk4[0:R, :, :, 8:16]
                cs = ROPE[0:R, t0_:t0_ + J, 0]
                sn = ROPE[0:R, t0_:t0_ + J, 1]
                for (k_, a_, b_) in ((0, x1, cs), (1, x2, sn), (2, x2, cs), (3, x1, sn)):
                    S.op("dve", lambda e, k_=k_, a_=a_, b_=b_: e.tensor_tensor(out=rot[0:R, k_], in0=a_, in1=b_, op=ALU.mult), reads=[qk, krope], writes=[("rot", ri)])
                yield
                S.op("dve", lambda e: e.tensor_tensor(out=x1, in0=rot[0:R, 0], in1=rot[0:R, 1], op=ALU.subtract), reads=[("rot", ri)], writes=[qk])
                S.op("dve", lambda e: e.tensor_tensor(out=x2, in0=rot[0:R, 2], in1=rot[0:R, 3], op=ALU.add), reads=[("rot", ri)], writes=[qk])
                P_ROT.release(ri)
                yield
                bi = yield from P_QKB.acquire()
                qb = QKB[bi][:, 0:J]
                qbq = qb.rearrange("p j a c -> p j (a c)").rearrange("p j (a x) -> p j a x", x=192)[:, :, :, 0:64]
                S.op("pool", lambda e: e.tensor_copy(out=qbq[0:R], in_=qf[0:R, :, 0:128].rearrange("p j (a c) -> p j a c", a=2)), reads=[qk], writes=[("qkb", bi)])
                S.op("act", lambda e: e.copy(out=qb[0:R, :, 2, :], in_=qf[0:R, :, 128:256]), reads=[qk], writes=[("qkb", bi)])
                S.op("act", lambda e: e.copy(out=VAUG[0:R, t0_:t0_ + J, :, 0:64], in_=qf[0:R, :, 256:384].rearrange("p j (h c) -> p j h c", h=2)),
                     reads=[qk], writes=[kv(ti) for ti in tis])
                for j, ti in enumerate(tis):
                    need = (ti == 16) or (g == 2) or (g == 1 and ti % 4 == 3) or (g == 0 and ti == 15)
                    if not need:
                        continue
                    if ti == 16:
                        for n in range(4):
                            S.dma(lambda e, n=n, j=j: e.dma_start(out=o_kv_s[g][l, n, W - 8:W, :], in_=qf[8 * n:8 * n + 8, j, 128:384]),
                                  reads=[qk], slot=("kvo", qi))
                    else:
                        if g == 0:
                            dst = o_kv_p[0][l]
                        elif g == 1:
                            dst = o_kv_p[1][l].rearrange("(i r) c -> r i c", r=4)[ti // 4]
                        else:
                            dst = o_kv_p[2][l].rearrange("(i r) c -> r i c", r=16)[ti]
                        S.dma(lambda e, j=j, dst=dst: e.dma_start(out=dst, in_=qf[:, j, 128:384]), reads=[qk], slot=("kvo", qi))
                P_Q.release(qi)
                yield
                pt = yield from acq_bank((5, 4, 3, 2, 1, 0))
                pst = bank(pt)[:, 0:192 * J].bitcast(BF16).rearrange("p (j a b) -> p j a b", j=J, a=3)
                for j in range(J):
                    for a in range(3):
                        S.op("pe", lambda e, a=a, j=j: e.transpose(out=pst[:, j, a, 0:R], in_=qb[0:R, j, a, :], identity=IDB[0:R, 0:R]),
                             reads=[("qkb", bi), "idb"], writes=[("ps", pt)])
                P_QKB.release(bi)
                yield
                if J == 2:
                    qdst = QKT[:, :, 128 * t0_:128 * t0_ + 256].rearrange("p a (j r) -> p a j r", j=2)
                else:
                    qdst = QKT[:, :, 128 * t0_:128 * t0_ + R].unsqueeze(2)
                S.op("act", lambda e: e.copy(out=qdst[:, :, :, 0:R], in_=pst[:, :, :, 0:R].rearrange("p j a r -> p a j r")), reads=[("ps", pt)],
                     writes=[kq(a, ti) for a in range(3) for ti in tis])
                rel_ps(pt)

            return [(lambda tis=tis: qkv_tile(tis)) for tis in ([[2 * i, 2 * i + 1] for i in range(8)] + [[16]])]

        def att_items(g):
            sg = g % 2
            QKT, VAUG = QKT_S[sg], VAUG_S[sg]
            kq = lambda a, ti: ("qkt", sg, a, ti)
            kv = lambda ti: ("vaug", sg, ti)
            ncls = (1, 4, 16)[g]
            nb = 16 // ncls

            def att_unit(r, qb):
                tq = r * nb + qb
                kbs = [(1, qb)] if qb == 0 else [(0, qb - 1), (1, qb)]
                c0 = 256 if qb == 0 else 0
                bi = yield from P_PB.acquire()
                pi = yield from acq_bank()
                pS = bank(pi)
                for (ki, kb) in kbs:
                    tk = r * nb + kb
                    S.op("pe", lambda e, ki=ki, tk=tk: e.matmul(pS[:, ki * 256:ki * 256 + 256], lhsT=QKT[:, 2, 128 * tk:128 * tk + 128],
                                                              rhs=QKT[:, 0:2, 128 * tq:128 * tq + 128], start=True, stop=False),
                         reads=[kq(2, tk), kq(0, tq), kq(1, tq)], writes=[("ps", pi)])
                    S.op("pe", lambda e, ki=ki: e.matmul(pS[:, ki * 256:ki * 256 + 256], lhsT=IDB, rhs=MASKP[:, ki * 256:ki * 256 + 256], start=False, stop=True),
                         reads=["idb", "maskp"], writes=[("ps", pi)])
                yield
                P = PB[bi]
                S.op("act", lambda e: e.activation(out=P[:, c0:512], in_=pS[:, c0:512], func=AF.Exp, scale=0.125), reads=[("ps", pi)], writes=[("pb", bi)])
                rel_ps(pi)
                yield
                po = yield from acq_bank()
                pO = bank(po)
                for h in range(2):
                    hs = slice(64 * h, 64 * h + 64)
                    for idx, (ki, kb) in enumerate(kbs):
                        tk = r * nb + kb
                        S.op("pe", lambda e, hs=hs, ki=ki, h=h, idx=idx, tk=tk: e.matmul(
                            pO[hs, 0:128], lhsT=VAUG[:, tk, h, 0:64], rhs=P[:, (ki * 2 + h) * 128:(ki * 2 + h) * 128 + 128],
                            start=(idx == 0), stop=(idx == len(kbs) - 1)), reads=[kv(tk), ("pb", bi)], writes=[("ps", po)])
                nd = 2 * len(kbs)
                for idx, (ki, kb) in enumerate(kbs):
                    for h in range(2):
                        S.op("pe", lambda e, ki=ki, idx=idx, h=h: e.matmul(pO[:, 128:256], lhsT=HSEL[:, h, :], rhs=P[:, (ki * 2 + h) * 128:(ki * 2 + h) * 128 + 128],
                                                                        start=(idx == 0 and h == 0), stop=(2 * idx + h == nd - 1)), reads=["hsel", ("pb", bi)], writes=[("ps", po)])
                P_PB.release(bi)
                yield
                if g == 0:
                    qsel = slice(128 * qb, 128 * qb + 128)
                elif g == 1:
                    qsel = slice(512 * qb + r, 512 * qb + 512, 4)
                else:
                    qsel = slice(r, T, 16)
                pov = pO[:, 0:256].rearrange("p (a b) -> p a b", a=2)
                if g == 0:
                    S.op("act", lambda e: e.copy(out=ACC[:, :, qsel], in_=pov), reads=[("ps", po)], writes=["acc"])
                else:
                    S.op("dve", lambda e: e.tensor_tensor(out=ACC[:, :, qsel], in0=pov, in1=ACC[:, :, qsel], op=ALU.add), reads=[("ps", po), "acc"], writes=["acc"])
                rel_ps(po)

            return [(lambda r=r, qb=qb: att_unit(r, qb)) for r in range(ncls) for qb in range(nb)]

        def smp_items(g):
            W = WINS[g]
            sg = g % 2
            QKT, VAUG = QKT_S[sg], VAUG_S[sg]
            kq = lambda a, ti: ("qkt", sg, a, ti)
            for h in range(2):
                hs = slice(64 * h, 64 * h + 64)
                S.op("act", lambda e, h=h, hs=hs: e.copy(out=QBDG[g][hs, 32 * h:32 * h + 32], in_=QKT[hs, h, 2048:2080]), reads=[kq(h, 16)], writes=[("qbd", g)])
            pi = next_free_bank()
            S.op("pe", lambda e: e.matmul(bank(pi)[0:32, 0:64], lhsT=QKT[:, 2, 2048:2080], rhs=QBDG[g], start=True, stop=True), reads=[kq(2, 16), ("qbd", g)], writes=[("ps", pi)])
            S.op("act", lambda e: e.activation(out=PN[0:32], in_=bank(pi)[0:32, 0:64], func=AF.Exp, scale=0.125), reads=[("ps", pi)], writes=["pn"])
            S.op("dve", lambda e: e.tensor_tensor(out=PN[0:32], in0=PN[0:32], in1=MASKN[0:32, g, :], op=ALU.mult), reads=["pn", "maskn"], writes=["pn"])
            for h in range(2):
                S.op("pe", lambda e, h=h, st=first_pv[h]: e.matmul(psa[h], lhsT=PN[0:32, 32 * h:32 * h + 32], rhs=VAUG[0:32, 16, h, :], start=st, stop=False),
                     reads=["pn", ("vaug", sg, 16)], writes=[("ps", PSA[h])])
                first_pv[h] = False
            ntile = 8 if g == 2 else W // 128

            def smp_chunk(n, c0):
                nt = min(4, ntile - c0)
                ci = yield from P_CST.acquire()
                cst = CST[ci]
                if g == 2:
                    src = caches[2][l, n].rearrange("(i j) c -> i j c", j=16)[:, c0:c0 + nt, :]
                else:
                    src = caches[g][l, n, 128 * c0:128 * (c0 + nt), :].rearrange("(i p) c -> p i c", p=128)
                S.dma(lambda e: e.dma_start(out=cst[:, 0:nt, :], in_=src), writes=[("cst", ci)], slot=("cst", ci))
                yield
                ki = yield from P_KI.acquire()
                S.op("dve", lambda e: e.tensor_copy(out=KB_[ki][:, 0:nt, :], in_=cst[:, 0:nt, 0:128]), reads=[("cst", ci)], writes=[("kb", ki)])
                S.op("act", lambda e: e.copy(out=VS[ki][:, 0:nt, :, 0:64], in_=cst[:, 0:nt, 128:256].rearrange("p i (h c) -> p i h c", h=2)),
                     reads=[("cst", ci)], writes=[("vs", ki)])
                P_CST.release(ci)
                yield
                pt = yield from acq_bank()
                pst = bank(pt)[:, 0:256].bitcast(BF16).rearrange("p (a b) -> p a b", a=4)
                for i in range(nt):
                    S.op("pe", lambda e, i=i: e.transpose(out=pst[:, i, :], in_=KB_[ki][:, i, :], identity=IDB), reads=[("kb", ki), "idb"], writes=[("ps", pt)])
                yield
                S.op("act", lambda e: e.copy(out=KTS[ki][:, 0:128 * nt], in_=bank(pt)[:, 0:64 * nt].bitcast(BF16)), reads=[("ps", pt)], writes=[("kts", ki)])
                rel_ps(pt)
                yield
                pq_ = yield from acq_bank()
                for i in range(nt):
                    S.op("pe", lambda e, i=i: e.matmul(bank(pq_)[:, 64 * i:64 * i + 64], lhsT=KTS[ki][:, 128 * i:128 * i + 128], rhs=QBDG[g], start=True, stop=True),
                         reads=[("kts", ki), ("qbd", g)], writes=[("ps", pq_)])
                yield
                psb = PSB[ki]
                S.op("act", lambda e: e.activation(out=psb[:, 0:nt, :], in_=bank(pq_)[:, 0:64 * nt].rearrange("p (i c) -> p i c", i=nt), func=AF.Exp, scale=0.125),
                     reads=[("ps", pq_)], writes=[("psb", ki)])
                rel_ps(pq_)
                yield
                if g == 2:
                    S.op("dve", lambda e: e.tensor_tensor(out=psb[:, 0:nt, :], in0=psb[:, 0:nt, :], in1=MASK2[:, n, c0:c0 + nt, :], op=ALU.mult),
                         reads=[("psb", ki), "mask2"], writes=[("psb", ki)])
                else:
                    i0 = 0
                    if c0 == 0:
                        S.op("dve", lambda e: e.tensor_tensor(out=psb[:, 0, :], in0=psb[:, 0, :], in1=MASKS[:, g, n, 0, :], op=ALU.mult), reads=[("psb", ki), "masks"], writes=[("psb", ki)])
                        i0 = 1
                    if nt > i0:
                        S.op("dve", lambda e, i0=i0: e.tensor_tensor(
                            out=psb[:, i0:nt, :], in0=psb[:, i0:nt, :], in1=MASKS[:, g, n, 1:2, :].to_broadcast([128, nt - i0, 64]), op=ALU.mult),
                            reads=[("psb", ki), "masks"], writes=[("psb", ki)])
                yield
                last_chunk = (g == 2 and n == 3 and c0 + nt == ntile)
                for i in range(nt):
                    for h in range(2):
                        S.op("pe", lambda e, i=i, h=h, sp_=(last_chunk and i == nt - 1): e.matmul(
                            psa[h], lhsT=psb[:, i, 32 * h:32 * h + 32], rhs=VS[ki][:, i, h, :], start=False, stop=sp_),
                            reads=[("psb", ki), ("vs", ki)], writes=[("ps", PSA[h])])
                P_KI.release(ki)

            return [(lambda n=n, c0=c0: smp_chunk(n, c0)) for n in range(4) for c0 in range(0, ntile, 4)]

        def next_free_bank():
            for i in (5, 4, 3, 2, 1, 0):
                if i not in ps_held:
                    return i
            raise AssertionError("no free PSUM bank")

        def merge(*lists, rate=None, delay=None):
            out = []
            tot = max(len(x) for x in lists)
            pos = [0] * len(lists)
            rate = rate or [1] * len(lists)
            delay = delay or [0] * len(lists)
            for step in range(tot):
                for li, x in enumerate(lists):
                    d0 = delay[li]
                    want = min(len(x), max(0, step + 1 - d0) * len(x) * rate[li] // max(1, tot - d0))
                    while pos[li] < want:
                        out.append(x[pos[li]])
                        pos[li] += 1
            return out

        pipeline(qkv_items(0), depth=8)
        if stop == "a2b_qkv0":
            return True
        for g in range(3):
            att = att_items(g)
            smp = smp_items(g)
            nxtq = qkv_items(g + 1) if g < 2 else []
            n_att = len(att)
            pipeline(merge(att, smp, nxtq, rate=[1, 1, 2], delay=[0, n_att // 3, 0]) if nxtq else merge(att, smp), depth=10)
            assert not ps_held, ps_held
            if stop == "a2b_s%d" % g:
                return True

        for b in range(4):
            cs_ = slice(512 * b, 512 * b + 512)
            S.op("dve", lambda e, cs_=cs_: e.reciprocal(out=ACC[:, 1, cs_], in_=ACC[:, 1, cs_]), reads=["acc"], writes=["acc"])
            S.op("dve", lambda e, cs_=cs_: e.tensor_tensor(out=MIXT[:, 2, cs_], in0=ACC[:, 0, cs_], in1=ACC[:, 1, cs_], op=ALU.mult), reads=["acc"], writes=[("mix", 2, b)])
        for h in range(2):
            S.op("dve", lambda e, h=h: e.tensor_copy(out=RD[0:32, h:h + 1], in_=psa[h][:, 64:65]), reads=[("ps", PSA[h])], writes=["rd"])
        S.op("dve", lambda e: e.reciprocal(out=RD[0:32], in_=RD[0:32]), reads=["rd"], writes=["rd"])
        for h in range(2):
            S.op("dve", lambda e, h=h: e.tensor_scalar(out=YSB[0:32, 64 * h:64 * h + 64], in0=psa[h][:, 0:64], scalar1=RD[0:32, h:h + 1], scalar2=None, op0=ALU.mult),
                 reads=[("ps", PSA[h]), "rd"], writes=["ysb"])
        pt = next_ps()
        pst = bank(pt)[:, 0:16].bitcast(BF16)
        S.op("pe", lambda e, pst=pst: e.transpose(out=pst, in_=YSB[0:32, :], identity=IDB[0:32, 0:32]), reads=["ysb", "idb"], writes=[("ps", pt)])
        S.op("act", lambda e, pst=pst: e.copy(out=MIXT[:, 2, 2048:2080], in_=pst), reads=[("ps", pt)], writes=[("mix", 2, 4)])

    def phase_c(l):
        def pre(t):
            R = tile_rows(t)
            if l == 0:
                src = xp[128 * t:128 * t + 128, :] if t < 16 else xs
            else:
                src = xpark[t, 0:R, :]
            S.dma(lambda e, t=t, R=R, src=src: e.dma_start(out=X[0:R, t, :], in_=src), reads=([("park", t)] if l > 0 else []), writes=[("x", t)], slot=("xl", t))
            b = t // 4
            for hf in range(2):
                pi = next_ps()
                for c in range(6):
                    S.op("pe", lambda e, c=c, pi=pi, R=R, t=t, hf=hf: e.matmul(bank(pi)[0:R, :], lhsT=MIXT[:, c, tile_cols(t)], rhs=WO[:, c, 512 * hf:512 * hf + 512],
                                                                               start=(c == 0), stop=(c == 5)), reads=["wo", ("mix", c, b)], writes=[("ps", pi)])
                S.op("dve", lambda e, pi=pi, R=R, t=t, hf=hf: e.tensor_tensor(out=X[0:R, t, 512 * hf:512 * hf + 512], in0=bank(pi)[0:R, :], in1=X[0:R, t, 512 * hf:512 * hf + 512], op=ALU.add),
                     reads=[("ps", pi), ("x", t)], writes=[("x", t)])
        norm_and_transpose(norm2_g[l], pre=pre)

    def phase_d(l, prefetch_next):
        last = (l == DEPTH - 1)
        if not last:
            S.op("dve", lambda e: e.memset(SS, 0.0), writes=["ss"])
        for fb in range(4):
            i = fb % 2

            def up(b, fb=fb, i=i):
                L = blk_len(b)
                at = AT[b % 2]
                for fc in range(8):
                    pi = next_ps()
                    for d in range(8):
                        S.op("pe", lambda e, d=d, fc=fc, pi=pi, L=L: e.matmul(bank(pi)[:, 0:L], lhsT=WU[i][:, d, 128 * fc:128 * fc + 128], rhs=HT[:, d, blk_cols(b)],
                                                                          start=(d == 0), stop=(d == 7)),
                             reads=[("wu", i)] + [("ht", t) for t in (range(4 * b, 4 * b + 4) if b < 4 else [16])], writes=[("ps", pi)])
                    ri = fc % 2
                    S.op("act", lambda e, pi=pi, L=L, ri=ri: e.activation(out=RL[ri][:, 0:L], in_=bank(pi)[:, 0:L], func=AF.Relu), reads=[("ps", pi)], writes=[("rl", ri)])
                    S.op("dve", lambda e, pi=pi, L=L, ri=ri, at=at, fc=fc: e.tensor_tensor(out=at[:, fc, 0:L], in0=bank(pi)[:, 0:L], in1=RL[ri][:, 0:L], op=ALU.mult),
                         reads=[("ps", pi), ("rl", ri)], writes=[("at", b % 2)])

            def down(b, fb=fb, i=i):
                at = AT[b % 2]
                tiles = range(4 * b, 4 * b + 4) if b < 4 else [16]
                for t in tiles:
                    R = tile_rows(t)
                    lo = 128 * (t % 4) if b < 4 else 0
                    for hf in range(2):
                        pi = next_ps()
                        for fc in range(8):
                            S.op("pe", lambda e, fc=fc, pi=pi, R=R, lo=lo, hf=hf, at=at: e.matmul(bank(pi)[0:R, :], lhsT=at[:, fc, lo:lo + R], rhs=WD[i][:, fc, 512 * hf:512 * hf + 512],
                                                                                          start=(fc == 0), stop=(fc == 7)), reads=[("wd", i), ("at", b % 2)], writes=[("ps", pi)])
                        S.op("dve", lambda e, pi=pi, R=R, t=t, hf=hf: e.tensor_tensor(out=X[0:R, t, 512 * hf:512 * hf + 512], in0=bank(pi)[0:R, :], in1=X[0:R, t, 512 * hf:512 * hf + 512], op=ALU.add),
                             reads=[("ps", pi), ("x", t)], writes=[("x", t)])
                    if fb == 3:
                        if last:
                            dst = yp[128 * t:128 * t + 128, :] if t < 16 else ys
                            S.dma(lambda e, t=t, R=R, dst=dst: e.dma_start(out=dst, in_=X[0:R, t, :]), reads=[("x", t)], slot=("xo", t))
                        else:
                            S.dma(lambda e, t=t, R=R: e.dma_start(out=xpark[t, 0:R, :], in_=X[0:R, t, :]), reads=[("x", t)], writes=[("park", t)], slot=("xo", t))
                            S.op("act", lambda e, t=t, R=R: e.activation(out=HB[t % 2][0:R, :], in_=X[0:R, t, :], func=AF.Square, accum_out=SS[0:R, t:t + 1]),
                                 reads=[("x", t)], writes=[("hb", t % 2), "ss"])

            up(0)
            for b in range(5):
                if b + 1 < 5:
                    up(b + 1)
                down(b)
            if fb == 1 and not last:
                load_gbc(norm1_g[l + 1])
                load_smalls(l + 1)
            if fb + 2 < 4:
                load_ffn(l, fb + 2)
            elif prefetch_next is not None:
                prefetch_next(fb)

    def fin():
        S.emit(es)
        es.close()
        return nc

    if stop == "setup":
        return fin()
    load_wa(0)
    load_wq(0)
    load_wo(0)
    for l in range(DEPTH):
        if l == 0:
            load_smalls(l)
        norm_and_transpose(norm1_g[l], have_ss=(l > 0))
        if stop == "a1":
            return fin()
        phase_a2a(l)
        if stop == "a2a":
            return fin()
        if phase_a2b(l) or stop == "a2b":
            return fin()
        load_ffn(l, 0)
        phase_c(l)
        if stop == "c":
            return fin()
        load_ffn(l, 1)

        def prefetch_next(fb, l=l):
            if l + 1 < DEPTH:
                if fb == 2:
                    load_wa(l + 1)
                if fb == 3:
                    load_wq(l + 1)
                    load_wo(l + 1)
        phase_d(l, prefetch_next)
        if stop == "d":
            return fin()
    S.emit(es)
    es.close()
    return nc


_NC_CACHE = {}


def kernel(x_prompt, x_sample, state_pool, state_conv, cache_kv_w128, cache_kv_w512, cache_kv_w2048,
           norm1_g, w_in, q_norm_g, k_norm_g, pool_w, pool_scale, conv_w, w_out, norm2_g, w_up, w_down):
    f = lambda a: np.ascontiguousarray(np.asarray(a, dtype=np.float32))
    consts = make_consts()
    shared = {
        "norm1_g": f(norm1_g), "w_in": f(w_in), "q_norm_g": f(q_norm_g), "k_norm_g": f(k_norm_g),
        "pool_w": f(pool_w), "pool_scale": f(pool_scale), "conv_w": f(conv_w), "w_out": f(w_out),
        "norm2_g": f(norm2_g), "w_up": f(w_up), "w_down": f(w_down),
    }
    shared.update(consts)
    x_prompt = f(x_prompt); x_sample = f(x_sample)
    state_pool = f(state_pool); state_conv = f(state_conv)
    c128 = f(cache_kv_w128); c512 = f(cache_kv_w512); c2048 = f(cache_kv_w2048)
    in_maps = []
    for c in range(NCORES):
        s = slice(4 * c, 4 * c + 4)
        m = dict(shared)
        m["xp"] = x_prompt[c]
        m["xs"] = np.ascontiguousarray(x_sample[s].reshape(32, D))
        m["spool"] = np.ascontiguousarray(state_pool[:, s])
        m["sconv"] = np.ascontiguousarray(state_conv[:, s])
        m["c128"] = np.ascontiguousarray(c128[:, s].reshape(DEPTH, 4, 128, 256))
        m["c512"] = np.ascontiguousarray(c512[:, s].reshape(DEPTH, 4, 512, 256))
        m["c2048"] = np.ascontiguousarray(c2048[:, s].reshape(DEPTH, 4, 2048, 256))
        in_maps.append(m)
    if "nc" not in _NC_CACHE:
        _NC_CACHE["nc"] = build_program()
    nc = _NC_CACHE["nc"]
    res = run_bass_kernel_spmd(nc, in_maps, core_ids=list(range(NCORES)))
    R = res.results
    cat = lambda k, ax: np.concatenate([np.asarray(r[k]) for r in R], axis=ax)
    y_prompt = np.stack([np.asarray(r["yp"]) for r in R], 0)
    y_sample = np.concatenate([np.asarray(r["ys"]).reshape(4, 8, D) for r in R], 0)
    pool_p = np.stack([np.asarray(r["pool_p"]) for r in R], 1)
    conv_p = np.stack([np.asarray(r["conv_p"]) for r in R], 1)
    kvp = [np.stack([np.asarray(r[k]) for r in R], 1).reshape(DEPTH, NCORES, w, 2, 2, 64)
           for k, w in (("kv128_p", 128), ("kv512_p", 512), ("kv2048_p", 2048))]
    pool_s = cat("pool_s", 1)
    conv_s = cat("conv_s", 1)
    kvs = [cat(k, 1).reshape(DEPTH, 32, w, 2, 2, 64) for k, w in (("kv128_s", 128), ("kv512_s", 512), ("kv2048_s", 2048))]
    outs = (y_prompt, y_sample, pool_p, conv_p, kvp[0], kvp[1], kvp[2], pool_s, conv_s, kvs[0], kvs[1], kvs[2])
    return tuple(np.ascontiguousarray(o, dtype=np.float32) for o in outs)
```

```python
import types
from contextlib import ExitStack

import ml_dtypes
import numpy as np

import concourse.bass as bass
import concourse.mybir as mybir
from concourse.bass_utils import run_bass_kernel_spmd

F32 = mybir.dt.float32
BF16 = mybir.dt.bfloat16
U8 = mybir.dt.uint8
ALU = mybir.AluOpType
AF = mybir.ActivationFunctionType
AX = mybir.AxisListType

NCORES = 8
D = 1024
T = 2048
TS = 32
NTOK = T + TS
NTILE = 17
DEPTH = 2
DFF = 4096
EPS = 1e-6
WINS = (128, 512, 2048)
DILS = (1, 4, 16)
COMPUTE = ("pe", "act", "dve", "pool")


def _snap(fn):
    if fn.__closure__ is None:
        return fn
    cells = tuple(types.CellType(c.cell_contents) for c in fn.__closure__)
    return types.FunctionType(fn.__code__, fn.__globals__, fn.__name__, fn.__defaults__, cells)


class Op:
    __slots__ = ("eng", "fn", "waits", "signal", "sigval", "idx", "dma_slot", "dma_val", "queue", "wait_vals")

    def __init__(self, eng, fn):
        self.eng = eng
        self.fn = _snap(fn)
        self.waits = []
        self.wait_vals = {}
        self.signal = False
        self.sigval = None
        self.idx = None
        self.dma_slot = None
        self.dma_val = None
        self.queue = None


class Region:
    __slots__ = ("name", "lo", "hi", "writer", "readers", "overl")

    def __init__(self, name, lo, hi):
        self.name, self.lo, self.hi = name, lo, hi
        self.writer = None
        self.readers = {}
        self.overl = None


class Sched:
    def __init__(self, nc):
        self.nc = nc
        self.ops = {e: [] for e in COMPUTE + ("sp",)}
        self.regions = {}
        self.phys = []
        self.dma_slots = {}
        self.waited = {}

    def region(self, name, lo=None, hi=None):
        r = self.regions.get(name)
        if r is None:
            r = Region(name, lo, hi)
            self.regions[name] = r
            if lo is not None:
                for o in self.phys:
                    if o.lo < hi and lo < o.hi:
                        if o.overl is None:
                            o.overl = []
                        o.overl.append(r)
                        if r.overl is None:
                            r.overl = []
                        r.overl.append(o)
                self.phys.append(r)
        return r

    def _regs(self, keys):
        out = []
        for k in keys:
            r = self.regions[k]
            out.append(r)
            if r.overl:
                out.extend(r.overl)
        return out

    def _deps(self, op, reads, writes):
        deps = []
        for r in self._regs(reads):
            if r.writer is not None:
                deps.append(r.writer)
        for r in self._regs(writes):
            if r.writer is not None:
                deps.append(r.writer)
            deps.extend(r.readers.values())
        best = {}
        for d in deps:
            if d is op:
                continue
            if d.dma_slot is not None:
                key = ("dma", d.dma_slot)
                v = self.dma_slots[d.dma_slot] - (16 if (op.dma_slot == d.dma_slot) else 0)
            else:
                if d.eng == op.eng and op.dma_slot is None and d.eng == "pe":
                    continue
                key = ("eng", d.eng)
                v = d.idx
            if key not in best or v > best[key][0]:
                best[key] = (v, d)
        wq = op.eng
        for key, (v, d) in best.items():
            wk = (wq, key)
            if self.waited.get(wk, -1) >= v:
                continue
            self.waited[wk] = v
            if d.dma_slot is None:
                d.signal = True
            else:
                op.wait_vals[id(d)] = v
            op.waits.append(d)

    def _commit(self, op, reads, writes):
        tag = ("dma", op.dma_slot) if op.dma_slot is not None else op.eng
        for k in reads:
            self.regions[k].readers[tag] = op
        for k in writes:
            r = self.regions[k]
            r.writer = op
            r.readers = {}

    def op(self, eng, fn, reads=(), writes=()):
        o = Op(eng, fn)
        o.idx = len(self.ops[eng])
        self._deps(o, reads, writes)
        self.ops[eng].append(o)
        self._commit(o, reads, writes)
        return o

    def dma(self, fn, reads=(), writes=(), slot=None, queue="sp"):
        o = Op(queue, fn)
        o.dma_slot = slot
        self.dma_slots[slot] = self.dma_slots.get(slot, 0) + 16
        o.dma_val = self.dma_slots[slot]
        o.idx = len(self.ops[queue])
        self._deps(o, reads, writes)
        self.ops[queue].append(o)
        self._commit(o, reads, writes)
        return o

    def emit(self, es):
        nc = self.nc
        sems = {}
        for e in COMPUTE:
            sems[("eng", e)] = es.enter_context(nc.semaphore("s_" + e))
        for i, s in enumerate(self.dma_slots):
            sems[("dma", s)] = es.enter_context(nc.semaphore("d%d" % i))
        for e in COMPUTE:
            c = 0
            for o in self.ops[e]:
                if o.signal:
                    c += 1
                    o.sigval = c
        final_waits = dict(self.dma_slots)

        def run(engname, eng):
            for o in self.ops[engname]:
                for d in o.waits:
                    if d.dma_slot is not None:
                        eng.wait_ge(sems[("dma", d.dma_slot)], o.wait_vals[id(d)])
                    else:
                        eng.wait_ge(sems[("eng", d.eng)], d.sigval)
                ins = o.fn(eng)
                if o.dma_slot is not None:
                    ins.then_inc(sems[("dma", o.dma_slot)], 16)
                elif o.signal:
                    ins.then_inc(sems[("eng", o.eng)], 1)
            if engname == "sp":
                for s, v in final_waits.items():
                    eng.wait_ge(sems[("dma", s)], v)

        block = es.enter_context(nc.Block())

        @block.tensor
        def _(e):
            run("pe", e)

        @block.scalar
        def _(e):
            run("act", e)

        @block.vector
        def _(e):
            run("dve", e)

        @block.gpsimd
        def _(e):
            run("pool", e)

        @block.sync
        def _(e):
            run("sp", e)


def _tile_positions(g, ti):
    if ti == 16:
        return None
    if g == 0:
        return 128 * ti + np.arange(128)
    if g == 1:
        r, kb = ti // 4, ti % 4
        return 512 * kb + 4 * np.arange(128) + r
    return 16 * np.arange(128) + ti


def make_consts():
    bf = ml_dtypes.bfloat16
    c = {}
    c["ident"] = np.eye(128, dtype=np.float32)
    k = np.arange(128)[:, None]
    q = np.arange(128)[None, :]
    prev = np.where(k >= q, 0.0, -1000.0).astype(np.float32)
    cur = np.where(k <= q, 0.0, -1000.0).astype(np.float32)
    c["maskp"] = np.concatenate([prev, prev, cur, cur], axis=1).astype(bf)
    ms = np.zeros((3, 128, 4, 2, 2, 4, 8), np.float32)
    mn = np.zeros((3, 32, 2, 4, 8), np.float32)
    for g in range(3):
        dil = DILS[g]
        p = np.arange(128)[:, None]
        t = np.arange(8)[None, :]
        base = ((t - p) % dil == 0).astype(np.float32)
        v0 = base * (p >= t)
        for n in range(4):
            for h in range(2):
                ms[g, :, n, 0, h, n, :] = v0
                ms[g, :, n, 1, h, n, :] = base
        for n in range(4):
            for tp in range(8):
                for tq in range(8):
                    if tp <= tq and (tq - tp) % dil == 0:
                        mn[g, n * 8 + tp, :, n, tq] = 1.0
    c["masks"] = ms.reshape(3, 128, 4, 2, 64)[0:2].astype(bf)
    m2 = np.zeros((128, 4, 8, 2, 4, 8), np.float32)
    for n in range(4):
        for j in range(8):
            m2[:, n, j, :, n, j] = 1.0
    c["mask2"] = m2.reshape(128, 4, 8, 64).astype(bf)
    c["maskn"] = mn.reshape(3, 32, 64).astype(bf)
    half = 8
    inv = np.power(np.float32(500000.0), -np.arange(half, dtype=np.float32) / half).astype(np.float32)
    rope = np.zeros((3, 128, 17, 2, 4, 8), np.float32)
    for g in range(3):
        for ti in range(17):
            if ti < 16:
                pos = _tile_positions(g, ti).astype(np.float32)
            else:
                pos = np.zeros(128, np.float32)
                pos[:32] = (8192 + (np.arange(32) % 8)).astype(np.float32)
            ang = (pos[:, None] * inv[None, :]).astype(np.float32)
            rope[g, :, ti, 0] = np.cos(ang)[:, None, :]
            rope[g, :, ti, 1] = np.sin(ang)[:, None, :]
    c["rope"] = rope
    wins = np.array([[2, 4], [8, 16]], np.float32)
    invw = np.zeros((128, 2), np.float32)
    invtab = np.zeros((128, 2, 15), np.float32)
    for pc in range(2):
        for hf in range(2):
            w = wins[pc, hf]
            invw[hf * 64:(hf + 1) * 64, pc] = 1.0 / w
            invtab[hf * 64:(hf + 1) * 64, pc, :] = 1.0 / np.minimum(np.arange(15) + 1, w)
    c["invw"] = invw
    c["invtab"] = invtab
    return c


def build_program(stop=None):
    nc = bass.Bass("TRN2", target_bir_lowering=False)

    def din(name, shape, dt=F32):
        return nc.dram_tensor(name, list(shape), dt, kind="ExternalInput").ap()

    def dout(name, shape):
        return nc.dram_tensor(name, list(shape), F32, kind="ExternalOutput").ap()

    xp = din("xp", [T, D])
    xs = din("xs", [TS, D])
    spool = din("spool", [DEPTH, 4, 15, 256])
    sconv = din("sconv", [DEPTH, 4, 2, 384])
    caches = [din("c128", [DEPTH, 4, 128, 256]), din("c512", [DEPTH, 4, 512, 256]), din("c2048", [DEPTH, 4, 2048, 256])]
    norm1_g = din("norm1_g", [DEPTH, D])
    w_in = din("w_in", [DEPTH, D, 2560])
    q_norm_g = din("q_norm_g", [DEPTH, 64])
    k_norm_g = din("k_norm_g", [DEPTH, 64])
    pool_w = din("pool_w", [DEPTH, 4, 64, 64])
    pool_scale = din("pool_scale", [DEPTH, 256])
    conv_w = din("conv_w", [DEPTH, 3, 384])
    w_out = din("w_out", [DEPTH, 768, D])
    norm2_g = din("norm2_g", [DEPTH, D])
    w_up = din("w_up", [DEPTH, D, DFF])
    w_down = din("w_down", [DEPTH, DFF, D])
    c_ident = din("ident", [128, 128])
    c_maskp = din("maskp", [128, 512], BF16)
    c_masks = din("masks", [2, 128, 4, 2, 64], BF16)
    c_mask2 = din("mask2", [128, 4, 8, 64], BF16)
    c_maskn = din("maskn", [3, 32, 64], BF16)
    c_rope = din("rope", [3, 128, 17, 2, 4, 8])
    c_invw = din("invw", [128, 2])
    c_invtab = din("invtab", [128, 2, 15])

    yp = dout("yp", [T, D])
    ys = dout("ys", [TS, D])
    o_pool_p = dout("pool_p", [DEPTH, 15, 256])
    o_conv_p = dout("conv_p", [DEPTH, 2, 384])
    o_kv_p = [dout("kv128_p", [DEPTH, 128, 256]), dout("kv512_p", [DEPTH, 512, 256]), dout("kv2048_p", [DEPTH, 2048, 256])]
    o_pool_s = dout("pool_s", [DEPTH, 4, 15, 256])
    o_conv_s = dout("conv_s", [DEPTH, 4, 2, 384])
    o_kv_s = [dout("kv128_s", [DEPTH, 4, 128, 256]), dout("kv512_s", [DEPTH, 4, 512, 256]), dout("kv2048_s", [DEPTH, 4, 2048, 256])]
    xpark = nc.dram_tensor("xpark", [NTILE, 128, D], F32, kind="Internal").ap()

    es = ExitStack()
    S = Sched(nc)
    NB = 212800
    big = es.enter_context(nc.sbuf_tensor("big", [128, NB], U8))
    psum = es.enter_context(nc.psum_tensor("psum", [128, 8 * 512], F32))

    def bank(i):
        return psum[:, i * 512:(i + 1) * 512]

    for i in range(8):
        S.region(("ps", i))
    ps_rr = [0]

    ps_held = set()

    def next_ps():
        for _ in range(8):
            i = ps_rr[0]
            ps_rr[0] = (i + 1) % 6
            if i not in ps_held:
                return i
        raise AssertionError("no free PSUM bank")

    def hold_ps():
        i = next_ps()
        ps_held.add(i)
        return i

    def rel_ps(i):
        ps_held.discard(i)

    def hold_ps_pref(order):
        for i in order:
            if i not in ps_held:
                ps_held.add(i)
                return i
        raise AssertionError("no free PSUM bank")

    def hold_ps_pair():
        for p0 in (0, 2, 4):
            if p0 not in ps_held and p0 + 1 not in ps_held:
                ps_held.add(p0)
                ps_held.add(p0 + 1)
                return p0
        raise AssertionError("no free PSUM bank pair")

    def carve(name, off, shape, dt):
        esz = 4 if dt == F32 else 2
        n = int(np.prod(shape[1:])) * esz
        assert off + n <= NB, (name, off, n)
        a = big[:, off:off + n].bitcast(dt)
        if len(shape) == 3:
            a = a.rearrange("p (a b) -> p a b", a=shape[1])
        elif len(shape) == 4:
            a = a.rearrange("p (a b c) -> p a b c", a=shape[1], b=shape[2])
        elif len(shape) == 5:
            a = a.rearrange("p (a b c d) -> p a b c d", a=shape[1], b=shape[2], c=shape[3])
        return a, off + n

    def reg(name, off, nbytes):
        S.region(name, off, off + nbytes)

    off = 0
    IDB, off = carve("idb", off, [128, 128], BF16); reg("idb", off - 256, 256)
    IDF, off = carve("idf", off, [128, 128], F32); reg("idf", off - 512, 512)
    ONESB, off = carve("onesb", off, [128, 128], BF16); reg("onesb", off - 256, 256)
    EPSC, off = carve("epsc", off, [128, 1], F32); reg("epsc", off - 4, 4)
    off = (off + 63) // 64 * 64
    HSEL, off = carve("hsel", off, [128, 2, 128], BF16); reg("hsel", off - 512, 512)
    off = (off + 63) // 64 * 64
    MASKP, off = carve("maskp", off, [128, 512], BF16); reg("maskp", off - 1024, 1024)
    MASKS, off = carve("masks", off, [128, 2, 4, 2, 64], BF16); reg("masks", off - 2048, 2048)
    MASK2, off = carve("mask2", off, [128, 4, 8, 64], BF16); reg("mask2", off - 4096, 4096)
    MASKN, off = carve("maskn", off, [128, 3, 64], BF16); reg("maskn", off - 384, 384)
    INVW, off = carve("invw", off, [128, 2], F32); reg("invw", off - 8, 8)
    INVTAB, off = carve("invtab", off, [128, 2, 15], F32); reg("invtab", off - 120, 120)
    PWBD, off = carve("pwbd", off, [128, 2, 128], BF16); reg("pwbd", off - 512, 512)
    PSCALE, off = carve("pscale", off, [128, 2], F32)
    CONVW, off = carve("convw", off, [128, 3, 3], F32)
    GQK, off = carve("gqk", off, [128, 256], F32)
    K_PS = [("pscale", i) for i in range(2)]
    K_CW = [("convw", i) for i in range(9)]
    K_GQ = [("gqk", i) for i in range(4)]
    for k_ in K_PS + K_CW + K_GQ:
        S.region(k_)
    GBC, off = carve("gbc", off, [128, 1024], F32); reg("gbc", off - 4096, 4096)
    SS, off = carve("ss", off, [128, 17], F32); reg("ss", off - 68, 68)
    RSTD, off = carve("rstd", off, [128, 17], F32); reg("rstd", off - 68, 68)
    off = (off + 63) // 64 * 64
    HB_OFF = off
    HB = []
    for i in range(2):
        a, off = carve("hb", off, [128, 1024], BF16); reg(("hb", i), off - 2048, 2048)
        HB.append(a)
    ARENA_OFF = off
    ARENA, off = carve("arena", off, [128, 32768], BF16)
    HT, off = carve("ht", off, [128, 8, NTOK], BF16)
    for t in range(NTILE):
        S.region(("ht", t))
    X_OFF = off
    X, off = carve("x", off, [128, NTILE, D], F32)
    for t in range(NTILE):
        reg(("x", t), X_OFF + t * 4096, 4096)
    MIX_OFF = off
    MIXT, off = carve("mixt", off, [128, 6, NTOK], BF16)
    for c in range(6):
        for b in range(5):
            lo = MIX_OFF + (c * NTOK + 512 * b) * 2
            reg(("mix", c, b), lo, lo + (1024 if b < 4 else 64) - lo + lo - lo)
    for c in range(6):
        for b in range(5):
            r = S.regions[("mix", c, b)]
            r.lo = MIX_OFF + (c * NTOK + 512 * b) * 2
            r.hi = r.lo + (1024 if b < 4 else 64)
    assert off <= NB, off
    TAIL_OFF = off

    o2 = X_OFF
    U, o2 = carve("u", o2, [128, 2, 527], F32)
    for pc in range(2):
        reg(("u", pc), X_OFF + pc * 527 * 4, 527 * 4)
    SB = []
    for pc in range(2):
        lst = []
        for nm in (("s2", "s4") if pc == 0 else ("s2", "s4", "s8", "s16")):
            a_, o2 = carve(nm, o2, [128, 527], F32); reg((nm, pc), o2 - 2108, 2108)
            lst.append(a_)
        SB.append(lst)
    TMP15, DTB = [], []
    for pc in range(2):
        a_, o2 = carve("tmp15", o2, [128, 16], F32); reg(("tmp15", pc), o2 - 64, 64)
        TMP15.append(a_)
    o2 = (o2 + 63) // 64 * 64
    for pc in range(2):
        a_, o2 = carve("dt", o2, [128, 512], BF16); reg(("dt", pc), o2 - 1024, 1024)
        DTB.append(a_)
    ZOFF = o2
    Z, o2 = carve("z", o2, [128, 3, 514], F32)
    for cc in range(3):
        reg(("z", cc), ZOFF + cc * 514 * 4, 514 * 4)
    GCS, CACC = [], []
    for cc in range(3):
        a_, o2 = carve("gcs", o2, [128, 512], F32); reg(("gcs", cc), o2 - 2048, 2048)
        GCS.append(a_)
        a_, o2 = carve("cacc", o2, [128, 512], F32); reg(("cacc", cc), o2 - 2048, 2048)
        CACC.append(a_)
    US, o2 = carve("us", o2, [128, 2, 4, 23], F32); reg("us", o2 - 736, 736)
    ZS, o2 = carve("zs", o2, [128, 3, 4, 10], F32); reg("zs", o2 - 480, 480)
    TS32, o2 = carve("ts32", o2, [128, 32], F32); reg("ts32", o2 - 128, 128)
    STF, o2 = carve("stf", o2, [128, 384], F32); reg("stf", o2 - 1536, 1536)
    OUTS, o2 = carve("outs", o2, [128, 384], F32); reg("outs", o2 - 1536, 1536)
    OUTS2, o2 = carve("outs2", o2, [128, 384], F32); reg("outs2", o2 - 1536, 1536)
    OUTS_S, o2 = carve("outs_s", o2, [128, 384], F32); reg("outs_s", o2 - 1536, 1536)
    OUTS2_S, o2 = carve("outs2_s", o2, [128, 384], F32); reg("outs2_s", o2 - 1536, 1536)
    TS32B, o2 = carve("ts32b", o2, [128, 5, 32], F32)
    for i in range(5):
        reg(("ts32b", i), o2 - 640 + 128 * i, 128)
    assert o2 <= X_OFF + NTILE * 4096

    o3 = X_OFF
    QKT_S, VAUG_S, ROPE_S = [None, None], [None, None], [None, None]
    QKT_S[0], o3 = carve("qkt", o3, [128, 3, NTOK], BF16)
    VOFF = o3
    VAUG_S[0], o3 = carve("vaug", o3, [128, NTILE, 2, 65], BF16)
    o3 = (o3 + 63) // 64 * 64
    ROFF = o3
    ROPE_S[0], o3 = carve("rope", o3, [128, 17, 2, 4, 8], F32)
    oa = ARENA_OFF
    Q1OFF = oa
    QKT_S[1], oa = carve("qkt1", oa, [128, 3, NTOK], BF16)
    V1OFF = oa
    VAUG_S[1], oa = carve("vaug1", oa, [128, NTILE, 2, 65], BF16)
    oa = (oa + 63) // 64 * 64
    R1OFF = oa
    ROPE_S[1], oa = carve("rope1", oa, [128, 17, 2, 4, 8], F32)
    assert oa <= ARENA_OFF + 11264 * 2, oa
    for sg, (qo, vo, ro) in enumerate(((X_OFF, VOFF, ROFF), (Q1OFF, V1OFF, R1OFF))):
        for a in range(3):
            for ti in range(NTILE):
                reg(("qkt", sg, a, ti), qo + (a * NTOK + ti * 128) * 2, 256 if ti < 16 else 64)
        for ti in range(NTILE):
            reg(("vaug", sg, ti), vo + ti * 260, 260)
        reg(("rope", sg), ro, 4352)
    AOFF = o3
    ACC, o3 = carve("acc", o3, [128, 2, T], F32); reg("acc", AOFF, o3 - AOFF)
    QKVF = []
    for i in range(3):
        a_, o3 = carve("qkvf", o3, [128, 2, 384], F32); reg(("qkvf", i), o3 - 3072, 3072)
        QKVF.append(a_)
    oq = ARENA_OFF + 28672 * 2
    for i in range(2):
        a_, oq = carve("qkvf", oq, [128, 2, 384], F32); reg(("qkvf", 3 + i), oq - 3072, 3072)
        QKVF.append(a_)
    NQ = len(QKVF)
    SQ = []
    for i in range(1):
        a_, o3 = carve("sq", o3, [128, 2, 256], F32); reg(("sq", i), o3 - 2048, 2048)
        SQ.append(a_)
    S4OFF = o3
    SS4, o3 = carve("ss4", o3, [128, 4, 8], F32)
    R4OFF = o3
    RS4, o3 = carve("rs4", o3, [128, 4, 8], F32)
    for i in range(4):
        reg(("ss4", i), S4OFF + 32 * i, 32)
        reg(("rs4", i), R4OFF + 32 * i, 32)
    ROT = []
    for i in range(1):
        a_, o3 = carve("rot", o3, [128, 4, 2, 4, 8], F32); reg(("rot", i), o3 - 1024, 1024)
        ROT.append(a_)
    QKB = []
    for i in range(2):
        a_, o3 = carve("qkb", o3, [128, 2, 3, 128], BF16); reg(("qkb", i), o3 - 1536, 1536)
        QKB.append(a_)
    CST = []
    for i in range(2):
        a_, o3 = carve("cst", o3, [128, 4, 256], F32); reg(("cst", i), o3 - 4096, 4096)
        CST.append(a_)
    KB_, VS, KTS, PSB = [], [], [], []
    for i in range(2):
        a_, o3 = carve("kb", o3, [128, 4, 128], BF16); reg(("kb", i), o3 - 1024, 1024)
        KB_.append(a_)
        a_, o3 = carve("vs", o3, [128, 4, 2, 65], BF16); reg(("vs", i), o3 - 1040, 1040)
        VS.append(a_)
        o3 = (o3 + 63) // 64 * 64
        a_, o3 = carve("kts", o3, [128, 512], BF16); reg(("kts", i), o3 - 1024, 1024)
        KTS.append(a_)
        a_, o3 = carve("psb", o3, [128, 4, 64], BF16); reg(("psb", i), o3 - 512, 512)
        PSB.append(a_)
    QBDG = []
    for i in range(3):
        a_, o3 = carve("qbd", o3, [128, 64], BF16); reg(("qbd", i), o3 - 128, 128)
        QBDG.append(a_)
    PN, o3 = carve("pn", o3, [128, 64], BF16); reg("pn", o3 - 128, 128)
    RD, o3 = carve("rd", o3, [128, 2], F32); reg("rd", o3 - 8, 8)
    o3 = (o3 + 63) // 64 * 64
    YSB, o3 = carve("ysb", o3, [128, 128], BF16); reg("ysb", o3 - 256, 256)
    assert o3 <= X_OFF + NTILE * 4096, (o3 - X_OFF - NTILE * 4096)
    PB = []
    for i in range(4):
        a_, _ = carve("pb", HB_OFF + 1024 * i, [128, 512], BF16); reg(("pb", i), HB_OFF + 1024 * i, 1024)
        PB.append(a_)

    AT = []
    o4 = MIX_OFF
    for i in range(2):
        a, o4 = carve("at", o4, [128, 8, 512], BF16); reg(("at", i), o4 - 8192, 8192)
        AT.append(a)
    RL = []
    for i in range(2):
        a, o4 = carve("rl", o4, [128, 512], F32); reg(("rl", i), o4 - 2048, 2048)
        RL.append(a)
    assert o4 <= MIX_OFF + 6 * NTOK * 2

    def arena_piece(name, col, shape):
        n = int(np.prod(shape[1:]))
        a = ARENA[:, col:col + n]
        if len(shape) == 3:
            a = a.rearrange("p (a b) -> p a b", a=shape[1])
        reg(name, ARENA_OFF + col * 2, n * 2)
        return a

    WA = arena_piece("wa", 0, [128, 8, 1408])
    WQ = [arena_piece(("wq", g), 12288 + 3072 * g, [128, 8, 384]) for g in range(3)]
    WO = arena_piece("wo", 22528, [128, 6, 1024])
    WU = [arena_piece(("wu", i), 16384 * i, [128, 8, 1024]) for i in range(2)]
    WD = [arena_piece(("wd", i), 16384 * i + 8192, [128, 8, 1024]) for i in range(2)]

    for t in range(NTILE):
        S.region(("park", t))

    def tile_rows(t):
        return 128 if t < 16 else 32

    def tile_cols(t):
        return slice(128 * t, 128 * t + tile_rows(t))

    def blk_cols(b):
        return slice(512 * b, 512 * b + (512 if b < 4 else 32))

    def blk_len(b):
        return 512 if b < 4 else 32

    S.dma(lambda e: e.dma_start(out=IDF, in_=c_ident), writes=["idf"], slot="c0")
    S.dma(lambda e: e.dma_start(out=MASKP, in_=c_maskp), writes=["maskp"], slot="c1")
    S.dma(lambda e: e.dma_start(out=MASKS, in_=c_masks.rearrange("g p n v c -> p g n v c")), writes=["masks"], slot="c2")
    S.dma(lambda e: e.dma_start(out=MASKN[0:32], in_=c_maskn.rearrange("g p c -> p g c")), writes=["maskn"], slot="c3")
    S.dma(lambda e: e.dma_start(out=INVW, in_=c_invw), writes=["invw"], slot="c4")
    S.dma(lambda e: e.dma_start(out=INVTAB, in_=c_invtab), writes=["invtab"], slot="c5")
    S.op("dve", lambda e: e.tensor_copy(out=IDB, in_=IDF), reads=["idf"], writes=["idb"])
    S.op("dve", lambda e: e.memset(ONESB, 1.0), writes=["onesb"])
    S.op("dve", lambda e: e.memset(EPSC, EPS), writes=["epsc"])
    S.op("dve", lambda e: e.memset(HSEL, 0.0), writes=["hsel"])
    for h in range(2):
        S.op("dve", lambda e, h=h: e.memset(HSEL[:, h, 64 * h:64 * h + 64], 1.0), writes=["hsel"])
    S.dma(lambda e: e.dma_start(out=MASK2, in_=c_mask2), writes=["mask2"], slot="c6")
    S.op("dve", lambda e: e.memset(PWBD, 0.0), writes=["pwbd"])
    for g in range(3):
        W = WINS[g]
        S.dma(lambda e, g=g, W=W: e.dma_start(out=o_kv_s[g][:, :, 0:W - 8, :], in_=caches[g][:, :, 8:W, :]),
              slot=("c2c", g))
    S.dma(lambda e: e.dma_start(out=o_pool_s[:, :, 0:7, :], in_=spool[:, :, 8:15, :]), slot="c2cp")
    for t in range(NTILE):
        src = xp[128 * t:128 * t + 128, :] if t < 16 else xs
        R = tile_rows(t)
        S.dma(lambda e, t=t, src=src, R=R: e.dma_start(out=X[0:R, t, :], in_=src), writes=[("x", t)], slot=("xl", t))

    wq_state = {"n": 0}

    def wdma(fn, writes, slot):
        S.dma(fn, writes=writes, slot=slot, queue="pool")

    def load_wa(l):
        wv = w_in[l].rearrange("(c p) n -> p c n", p=128)
        wdma(lambda e: e.dma_start(out=WA[:, :, 0:256], in_=wv[:, :, 0:256]), ["wa"], "wa")
        for h in range(4):
            wdma(lambda e, h=h: e.dma_start(out=WA[:, 2 * h:2 * h + 2, 256:1408], in_=wv[:, 2 * h:2 * h + 2, 1408:2560]), ["wa"], "wa")

    def load_wq(l):
        wv = w_in[l].rearrange("(c p) n -> p c n", p=128)
        for g in range(3):
            for j in range(3):
                c0 = 256 + 384 * j + 128 * g
                wdma(lambda e, g=g, j=j, c0=c0: e.dma_start(out=WQ[g][:, :, 128 * j:128 * j + 128], in_=wv[:, :, c0:c0 + 128]),
                     [("wq", g)], ("wq", g))

    def load_wo(l):
        wv = w_out[l].rearrange("(c p) n -> p c n", p=128)
        for h in range(3):
            wdma(lambda e, h=h: e.dma_start(out=WO[:, 2 * h:2 * h + 2, :], in_=wv[:, 2 * h:2 * h + 2, :]), ["wo"], "wo")

    def load_ffn(l, fb):
        i = fb % 2
        wu = w_up[l].rearrange("(c p) f -> p c f", p=128)
        wd = w_down[l][fb * 1024:(fb + 1) * 1024, :].rearrange("(c p) n -> p c n", p=128)
        for h in range(2):
            wdma(lambda e, h=h: e.dma_start(out=WU[i][:, 4 * h:4 * h + 4, :], in_=wu[:, 4 * h:4 * h + 4, fb * 1024:(fb + 1) * 1024]),
                 [("wu", i)], ("wu", i))
        for h in range(2):
            wdma(lambda e, h=h: e.dma_start(out=WD[i][:, 4 * h:4 * h + 4, :], in_=wd[:, 4 * h:4 * h + 4, :]),
                 [("wd", i)], ("wd", i))

    def load_smalls(l):
        for g in range(4):
            pc, hf = g // 2, g % 2
            wdma(lambda e, g=g, pc=pc, hf=hf: e.dma_start(out=PWBD[hf * 64:(hf + 1) * 64, pc, hf * 64:(hf + 1) * 64], in_=pool_w[l, g]),
                 ["pwbd"], "pwbd")
        for pc in range(2):
            S.dma(lambda e, pc=pc: e.dma_start(out=PSCALE[:, pc:pc + 1], in_=pool_scale[l, pc * 128:(pc + 1) * 128].rearrange("(p o) -> p o", o=1)),
                  writes=[("pscale", pc)], slot=("sm_ps", pc))
        for cc in range(3):
            for k in range(3):
                S.dma(lambda e, cc=cc, k=k: e.dma_start(out=CONVW[:, cc, k:k + 1], in_=conv_w[l, k, cc * 128:(cc + 1) * 128].rearrange("(p o) -> p o", o=1)),
                      writes=[("convw", 3 * cc + k)], slot=("sm_cw", 3 * cc + k))
        for j in range(4):
            src = q_norm_g if j < 2 else k_norm_g
            S.dma(lambda e, j=j, src=src: e.dma_start(out=GQK[:, 64 * j:64 * j + 64], in_=src[l].partition_broadcast(128)),
                  writes=[("gqk", j)], slot=("sm_gq", j))

    def load_gbc(gvec):
        S.dma(lambda e: e.dma_start(out=GBC, in_=gvec.partition_broadcast(128)), writes=["gbc"], slot="gbc")

    def norm_and_transpose(gvec, pre=None, have_ss=False):
        if not have_ss:
            load_gbc(gvec)
        if not have_ss:
            S.op("dve", lambda e: e.memset(SS, 0.0), writes=["ss"])
        for t in range(NTILE):
            if have_ss:
                break
            R = tile_rows(t)
            if pre is not None:
                pre(t)
            hb = HB[t % 2]
            S.op("act", lambda e, t=t, R=R, hb=hb: e.activation(out=hb[0:R, :], in_=X[0:R, t, :], func=AF.Square, accum_out=SS[0:R, t:t + 1]),
                 reads=[("x", t)], writes=[("hb", t % 2), "ss"])
        S.op("dve", lambda e: e.tensor_scalar(out=RSTD, in0=SS, scalar1=1.0 / D, scalar2=EPS, op0=ALU.mult, op1=ALU.add),
             reads=["ss"], writes=["rstd"])
        S.op("act", lambda e: e.activation(out=RSTD, in_=RSTD, func=AF.Ln), reads=["rstd"], writes=["rstd"])
        S.op("act", lambda e: e.activation(out=RSTD, in_=RSTD, func=AF.Exp, scale=-0.5), reads=["rstd"], writes=["rstd"])
        for t in range(NTILE):
            R = tile_rows(t)
            hb = HB[t % 2]
            S.op("dve", lambda e, t=t, R=R, hb=hb: e.scalar_tensor_tensor(out=hb[0:R, :], in0=X[0:R, t, :], scalar=RSTD[0:R, t:t + 1],
                                                                       in1=GBC[0:R, :], op0=ALU.mult, op1=ALU.mult),
                 reads=[("x", t), "rstd", "gbc"], writes=[("hb", t % 2)])
            pi = next_ps()
            pst = bank(pi)[:, 0:512].bitcast(BF16).rearrange("p (a b) -> p a b", a=8)
            for c in range(8):
                S.op("pe", lambda e, c=c, R=R, hb=hb, pst=pst: e.transpose(out=pst[:, c, 0:R], in_=hb[0:R, c * 128:(c + 1) * 128], identity=IDB[0:R, 0:R]),
                     reads=[("hb", t % 2), "idb"], writes=[("ps", pi)])
            S.op("act", lambda e, t=t, R=R, pst=pst: e.copy(out=HT[:, :, 128 * t:128 * t + R], in_=pst[:, :, 0:R]),
                 reads=[("ps", pi)], writes=[("ht", t)])

    def pipeline(items, lag=1, depth=4):
        active = []
        pending = list(items)
        tick = 0
        while pending or active:
            assert tick < 200000, "pipeline deadlock"

            started = None
            if pending and len(active) < depth:
                nxt_item = pending[0]
                key_ = nxt_item[0] if isinstance(nxt_item, tuple) else None
                if key_ is None or all(a_[2] != key_ for a_ in active):
                    pending.pop(0)
                    g_ = (nxt_item[1] if isinstance(nxt_item, tuple) else nxt_item)()
                    try:
                        next(g_)
                        started = [g_, tick + lag, key_]
                    except StopIteration:
                        pass
            for a_ in list(active):
                if a_[1] <= tick:
                    try:
                        next(a_[0])
                        a_[1] = tick + lag
                    except StopIteration:
                        active.remove(a_)
            if started:
                active.append(started)
            tick += 1

    def phase_a2a(l):
        S.dma(lambda e: e.dma_start(out=STF[0:60, 0:256], in_=spool[l].rearrange("n r c -> (n r) c")), writes=["stf"], slot="st1")
        for pc in range(2):
            pi = next_ps()
            S.op("pe", lambda e, pc=pc, pi=pi: e.transpose(out=bank(pi)[:, 0:60], in_=STF[0:60, pc * 128:(pc + 1) * 128], identity=IDF[0:60, 0:60]),
                 reads=["stf", "idf"], writes=[("ps", pi)])
            S.op("act", lambda e, pc=pc, pi=pi: e.copy(out=US[:, pc, :, 0:15], in_=bank(pi)[:, 0:60].rearrange("p (n r) -> p n r", n=4)),
                 reads=[("ps", pi)], writes=["us"])
        S.dma(lambda e: e.dma_start(out=OUTS2[0:8, 0:384], in_=sconv[l].rearrange("n r c -> (n r) c")), writes=["outs2"], slot="st2")
        for cc in range(3):
            pi = next_ps()
            S.op("pe", lambda e, cc=cc, pi=pi: e.transpose(out=bank(pi)[:, 0:8], in_=OUTS2[0:8, cc * 128:(cc + 1) * 128], identity=IDF[0:8, 0:8]),
                 reads=["outs2", "idf"], writes=[("ps", pi)])
            S.op("act", lambda e, cc=cc, pi=pi: e.copy(out=ZS[:, cc, :, 0:2], in_=bank(pi)[:, 0:8].rearrange("p (n r) -> p n r", n=4)),
                 reads=[("ps", pi)], writes=["zs"])
        S.op("dve", lambda e: e.memset(U[:, :, 0:15], 0.0), writes=[("u", 0), ("u", 1)])
        S.op("dve", lambda e: e.memset(Z[:, :, 0:2], 0.0), writes=[("z", 0), ("z", 1), ("z", 2)])

        def proj(col0, b):
            pi = hold_ps()
            L = blk_len(b)
            for d in range(8):
                S.op("pe", lambda e, d=d, pi=pi, L=L: e.matmul(bank(pi)[:, 0:L], lhsT=WA[:, d, col0:col0 + 128], rhs=HT[:, d, blk_cols(b)],
                                                               start=(d == 0), stop=(d == 7)),
                     reads=["wa"] + [("ht", t) for t in (range(4 * b, 4 * b + 4) if b < 4 else [16])], writes=[("ps", pi)])
            return pi

        def pool_item(b, pc):
            L = blk_len(b)
            smp = (b == 4)
            pi = proj(128 * pc, b)
            yield
            ukey = "us" if smp else ("u", pc)
            k2, k4, k8, k16, kdt, ktm = ("s2", pc), ("s4", pc), ("s8", pc), ("s16", pc), ("dt", pc), ("tmp15", pc)
            if smp:
                Uv = US[:, pc]
                W = 23
                S.op("act", lambda e: e.copy(out=US[:, pc, :, 15:23], in_=bank(pi)[:, 0:32].rearrange("p (n r) -> p n r", n=4)),
                     reads=[("ps", pi)], writes=["us"])

                def v3(ap):
                    return ap[:, 0:92].rearrange("p (n r) -> p n r", n=4)
                dt = DTB[pc][:, 0:32].rearrange("p (n r) -> p n r", n=4)

                def sl(ap, a, b_):
                    return ap[:, :, a:b_]
            else:
                Uv = U[:, pc]
                W = 527
                if b > 0:
                    S.op("pool", lambda e: e.tensor_copy(out=U[:, pc, 0:15], in_=U[:, pc, 512:527]), reads=[("u", pc)], writes=[("u", pc)])
                S.op("act", lambda e: e.copy(out=U[:, pc, 15:527], in_=bank(pi)[:, 0:512]), reads=[("ps", pi)], writes=[("u", pc)])

                def v3(ap):
                    return ap
                dt = DTB[pc]

                def sl(ap, a, b_):
                    return ap[:, a:b_]
            rel_ps(pi)
            s2, s4 = v3(SB[pc][0]), v3(SB[pc][1])
            yield
            S.op("pool", lambda e: e.tensor_tensor(out=sl(s2, 1, W), in0=sl(Uv, 1, W), in1=sl(Uv, 0, W - 1), op=ALU.add), reads=[ukey], writes=[k2])
            S.op("pool", lambda e: e.tensor_tensor(out=sl(s4, 3, W), in0=sl(s2, 3, W), in1=sl(s2, 1, W - 2), op=ALU.add), reads=[k2], writes=[k4])
            yield
            if pc == 0:
                wlo, whi, klo, khi = s2, s4, k2, k4
            else:
                s8, s16 = v3(SB[pc][2]), v3(SB[pc][3])
                S.op("pool", lambda e: e.tensor_tensor(out=sl(s8, 7, W), in0=sl(s4, 7, W), in1=sl(s4, 3, W - 4), op=ALU.add), reads=[k4], writes=[k8])
                S.op("pool", lambda e: e.tensor_tensor(out=sl(s16, 15, W), in0=sl(s8, 15, W), in1=sl(s8, 7, W - 8), op=ALU.add), reads=[k8], writes=[k16])
                yield
                wlo, whi, klo, khi = s8, s16, k8, k16
            for (hf, win, wk) in ((0, wlo, klo), (1, whi, khi)):
                ps_ = slice(hf * 64, hf * 64 + 64)
                S.op("dve", lambda e, ps_=ps_, win=win: e.scalar_tensor_tensor(
                    out=dt[ps_], in0=sl(win, 15, W)[ps_], scalar=INVW[ps_, pc:pc + 1], in1=sl(Uv, 15, W)[ps_], op0=ALU.mult, op1=ALU.subtract),
                    reads=[wk, ukey, "invw"], writes=[kdt])
            if b == 0:
                yield
                for (hf, win, wk) in ((0, wlo, klo), (1, whi, khi)):
                    ps_ = slice(hf * 64, hf * 64 + 64)
                    S.op("dve", lambda e, ps_=ps_, win=win: e.tensor_tensor(out=TMP15[pc][ps_, 0:15], in0=win[ps_, 15:30], in1=INVTAB[ps_, pc, :], op=ALU.mult),
                         reads=[wk, "invtab"], writes=[ktm])
                yield
                S.op("dve", lambda e: e.tensor_tensor(out=DTB[pc][:, 0:15], in0=TMP15[pc][:, 0:15], in1=U[:, pc, 15:30], op=ALU.subtract),
                     reads=[ktm, ("u", pc)], writes=[kdt])
            yield
            po = hold_ps()
            S.op("pe", lambda e: e.matmul(bank(po)[:, 0:L], lhsT=PWBD[:, pc, :], rhs=DTB[pc][:, 0:L], start=True, stop=True),
                 reads=["pwbd", kdt], writes=[("ps", po)])
            yield
            S.op("act", lambda e: e.activation(out=MIXT[:, pc, blk_cols(b)], in_=bank(po)[:, 0:L], func=AF.Copy, scale=PSCALE[:, pc:pc + 1]),
                 reads=[("ps", po)] + K_PS, writes=[("mix", pc, b)])
            rel_ps(po)

        def conv_item(b, cc):
            L = blk_len(b)
            smp = (b == 4)
            pgc = proj(256 + 384 + 128 * cc, b)
            pgh = proj(256 + 768 + 128 * cc, b)
            yield
            zkey = "zs" if smp else ("z", cc)
            kg, kc = ("gcs", cc), ("cacc", cc)
            if smp:
                Zv = ZS[:, cc]
                Wz = 10

                def v3(ap):
                    return ap[:, 0:32].rearrange("p (n r) -> p n r", n=4)

                def sz(a, b_, Zv=Zv):
                    return Zv[:, :, a:b_]
            else:
                Zv = Z[:, cc]
                Wz = 514
                if b > 0:
                    S.op("pool", lambda e: e.tensor_copy(out=Z[:, cc, 0:2], in_=Z[:, cc, 512:514]), reads=[("z", cc)], writes=[("z", cc)])

                def v3(ap):
                    return ap[:, 0:512]

                def sz(a, b_, Zv=Zv):
                    return Zv[:, a:b_]
            Lq = Wz - 2
            S.op("act", lambda e: e.copy(out=v3(GCS[cc]), in_=v3(bank(pgc))), reads=[("ps", pgc)], writes=[kg])
            rel_ps(pgc)
            yield
            S.op("dve", lambda e: e.tensor_tensor(out=sz(2, Wz), in0=v3(bank(pgh)), in1=v3(GCS[cc]), op=ALU.mult), reads=[("ps", pgh), kg], writes=[zkey])
            rel_ps(pgh)
            yield
            S.op("pool", lambda e: e.tensor_scalar(out=v3(CACC[cc]), in0=sz(0, Lq), scalar1=CONVW[:, cc, 0:1], scalar2=None, op0=ALU.mult),
                 reads=[zkey] + K_CW, writes=[kc])
            pgb = proj(256 + 128 * cc, b)
            yield
            for k in (1, 2):
                S.op("dve", lambda e, k=k: e.scalar_tensor_tensor(out=v3(CACC[cc]), in0=sz(k, k + Lq), scalar=CONVW[:, cc, k:k + 1], in1=v3(CACC[cc]),
                                                                  op0=ALU.mult, op1=ALU.add), reads=[zkey, kc] + K_CW, writes=[kc])
                yield
            mo = MIXT[:, 3 + cc, blk_cols(b)]
            if smp:
                mo = mo.rearrange("p (n r) -> p n r", n=4)
            S.op("dve", lambda e: e.tensor_tensor(out=mo, in0=v3(bank(pgb)), in1=v3(CACC[cc]), op=ALU.mult), reads=[("ps", pgb), kc], writes=[("mix", 3 + cc, b)])
            rel_ps(pgb)

        items = []
        for b in range(5):
            for pc in range(2):
                items.append((("p", pc), lambda b=b, pc=pc: pool_item(b, pc)))
            for cc in range(3):
                items.append((("c", cc), lambda b=b, cc=cc: conv_item(b, cc)))
        pipeline(items, depth=5)

        for smp in (False, True):
            o1, k1 = (OUTS_S, "outs_s") if smp else (OUTS, "outs")
            o2_, k2 = (OUTS2_S, "outs2_s") if smp else (OUTS2, "outs2")
            for pc in range(2):
                if smp:
                    S.op("pool", lambda e, pc=pc: e.tensor_copy(out=TS32B[:, pc].rearrange("p (n r) -> p n r", n=4), in_=US[:, pc, :, 15:23]), reads=["us"], writes=[("ts32b", pc)])
                    src, sk = TS32B[:, pc], ("ts32b", pc)
                else:
                    src, sk = U[:, pc, 495:527], ("u", pc)
                pi = next_ps()
                S.op("pe", lambda e, src=src, pi=pi: e.transpose(out=bank(pi)[0:32, 0:128], in_=src, identity=IDF), reads=[sk, "idf"], writes=[("ps", pi)])
                S.op("act", lambda e, pc=pc, pi=pi, o1=o1: e.copy(out=o1[0:32, pc * 128:(pc + 1) * 128], in_=bank(pi)[0:32, 0:128]), reads=[("ps", pi)], writes=[k1])
            if smp:
                for n in range(4):
                    S.dma(lambda e, n=n: e.dma_start(out=o_pool_s[l, n, 7:15, :], in_=OUTS_S[8 * n:8 * n + 8, 0:256]), reads=[k1], slot="so1s")
            else:
                S.dma(lambda e: e.dma_start(out=o_pool_p[l], in_=OUTS[17:32, 0:256]), reads=[k1], slot="so1")
            for cc in range(3):
                if smp:
                    S.op("pool", lambda e, cc=cc: e.tensor_copy(out=TS32B[:, 2 + cc].rearrange("p (n r) -> p n r", n=4), in_=ZS[:, cc, :, 2:10]), reads=["zs"], writes=[("ts32b", 2 + cc)])
                    src, sk = TS32B[:, 2 + cc], ("ts32b", 2 + cc)
                else:
                    src, sk = Z[:, cc, 482:514], ("z", cc)
                pi = next_ps()
                S.op("pe", lambda e, src=src, pi=pi: e.transpose(out=bank(pi)[0:32, 0:128], in_=src, identity=IDF), reads=[sk, "idf"], writes=[("ps", pi)])
                S.op("act", lambda e, cc=cc, pi=pi, o2_=o2_: e.copy(out=o2_[0:32, cc * 128:(cc + 1) * 128], in_=bank(pi)[0:32, 0:128]), reads=[("ps", pi)], writes=[k2])
            if smp:
                for n in range(4):
                    S.dma(lambda e, n=n: e.dma_start(out=o_conv_s[l, n], in_=OUTS2_S[8 * n + 6:8 * n + 8, 0:384]), reads=[k2], slot="so2s")
            else:
                S.dma(lambda e: e.dma_start(out=o_conv_p[l], in_=OUTS2[30:32, 0:384]), reads=[k2], slot="so2")

    class ResPool:
        def __init__(self, n):
            self.free = list(range(n))

        def acquire(self):
            while not self.free:
                yield
            return self.free.pop(0)

        def release(self, i):
            self.free.append(i)

    def acq_bank(pref=(0, 1, 2, 3, 4, 5)):
        while True:
            for i in pref:
                if i not in ps_held:
                    ps_held.add(i)
                    return i
            yield

    def acq_pair():
        while True:
            for p0 in (0, 2, 4):
                if p0 not in ps_held and p0 + 1 not in ps_held:
                    ps_held.add(p0)
                    ps_held.add(p0 + 1)
                    return p0
            yield

    def phase_a2b(l):
        for i in range(3):
            S.op("dve", lambda e, i=i: e.memset(QBDG[i], 0.0), writes=[("qbd", i)])
        for i in range(2):
            S.op("dve", lambda e, i=i: e.memset(VS[i], 1.0), writes=[("vs", i)])
        for i in range(2):
            S.op("dve", lambda e, i=i: e.memset(QKB[i], 0.0), writes=[("qkb", i)])
        PSA = (7, 6)
        psa = [bank(PSA[h])[0:32, 0:65] for h in range(2)]
        first_pv = [True, True]
        P_Q, P_SQ, P_S4, P_ROT, P_QKB = ResPool(NQ), ResPool(1), ResPool(4), ResPool(1), ResPool(2)
        P_PB, P_CST, P_KI = ResPool(4), ResPool(2), ResPool(2)

        def qkv_items(g):
            W = WINS[g]
            sg = g % 2
            QKT, VAUG, ROPE = QKT_S[sg], VAUG_S[sg], ROPE_S[sg]
            kq = lambda a, ti: ("qkt", sg, a, ti)
            kv = lambda ti: ("vaug", sg, ti)
            krope = ("rope", sg)
            S.dma(lambda e: e.dma_start(out=ROPE, in_=c_rope[g]), writes=[krope], slot=("rope", sg))
            S.op("dve", lambda e: e.memset(VAUG, 1.0), writes=[kv(ti) for ti in range(NTILE)])

            def qkv_tile(tis):
                J = len(tis)
                R = tile_rows(tis[0])
                t0_ = tis[0]
                pb0 = (yield from acq_pair()) if J == 2 else (yield from acq_bank())
                pqv = psum[:, pb0 * 512:(pb0 + J) * 512].rearrange("p (j c) -> p j c", j=J)
                pkeys = [("ps", pb0 + j) for j in range(J)]
                for j, ti in enumerate(tis):
                    if ti == 16:
                        csel = slice(T, T + 32)
                        htk = [("ht", 16)]
                    elif g == 0:
                        csel = slice(128 * ti, 128 * ti + 128)
                        htk = [("ht", ti)]
                    elif g == 1:
                        r, kb = ti // 4, ti % 4
                        csel = slice(512 * kb + r, 512 * kb + 512, 4)
                        htk = [("ht", 4 * kb + jj) for jj in range(4)]
                    else:
                        csel = slice(ti, T, 16)
                        htk = [("ht", jj) for jj in range(16)]
                    for d in range(8):
                        S.op("pe", lambda e, d=d, j=j, csel=csel: e.matmul(pqv[0:R, j, 0:384], lhsT=HT[:, d, csel], rhs=WQ[g][:, d, :], start=(d == 0), stop=(d == 7)),
                             reads=[("wq", g)] + htk, writes=[pkeys[j]])
                yield
                qi = yield from P_Q.acquire()
                si = yield from P_SQ.acquire()
                qf = QKVF[qi][:, 0:J, :]
                qk = ("qkvf", qi)
                S.op("act", lambda e: e.copy(out=qf[0:R], in_=pqv[0:R, :, 0:384]), reads=pkeys, writes=[qk])
                sq = SQ[si][:, 0:J, :]
                S.op("act", lambda e: e.activation(out=sq[0:R], in_=pqv[0:R, :, 0:256], func=AF.Square), reads=pkeys, writes=[("sq", si)])
                for j in range(J):
                    rel_ps(pb0 + j)
                yield
                s4 = yield from P_S4.acquire()
                ss = SS4[:, s4, 0:4 * J]
                rs = RS4[:, s4, 0:4 * J]
                S.op("dve", lambda e: e.tensor_reduce(out=ss[0:R], in_=sq[0:R].rearrange("p j (a b) -> p (j a) b", a=4), axis=AX.X, op=ALU.add),
                     reads=[("sq", si)], writes=[("ss4", s4)])
                P_SQ.release(si)
                yield
                S.op("act", lambda e: e.activation(out=rs[0:R], in_=ss[0:R], func=AF.Ln, scale=1.0 / 64, bias=EPSC[0:R, :]),
                     reads=[("ss4", s4), "epsc"], writes=[("rs4", s4)])
                yield
                S.op("act", lambda e: e.activation(out=rs[0:R], in_=rs[0:R], func=AF.Exp, scale=-0.5), reads=[("rs4", s4)], writes=[("rs4", s4)])
                yield
                qk4 = qf[:, :, 0:256].rearrange("p j (a b) -> p j a b", a=4)
                rs4b = rs.rearrange("p (j a) -> p j a", j=J).unsqueeze(3)
                S.op("dve", lambda e: e.tensor_tensor(out=qk4[0:R], in0=qk4[0:R], in1=rs4b[0:R].to_broadcast([R, J, 4, 64]), op=ALU.mult),
                     reads=[qk, ("rs4", s4)], writes=[qk])
                P_S4.release(s4)
                yield
                S.op("dve", lambda e: e.tensor_tensor(out=qf[0:R, :, 0:256], in0=qf[0:R, :, 0:256], in1=GQK[0:R].unsqueeze(1).to_broadcast([R, J, 256]), op=ALU.mult),
                     reads=[qk] + K_GQ, writes=[qk])
                yield
                ri = yield from P_ROT.acquire()
                rot = ROT[ri][:, :, 0:J]
                x1 = qk4[0:R, :, :, 0:8]
                x2 = qk4[0:R, :, :, 8:16]
                cs = ROPE[0:R, t0_:t0_ + J, 0]
                sn = ROPE[0:R, t0_:t0_ + J, 1]
                for (k_, a_, b_) in ((0, x1, cs), (1, x2, sn), (2, x2, cs), (3, x1, sn)):
                    S.op("dve", lambda e, k_=k_, a_=a_, b_=b_: e.tensor_tensor(out=rot[0:R, k_], in0=a_, in1=b_, op=ALU.mult), reads=[qk, krope], writes=[("rot", ri)])
                yield
                S.op("dve", lambda e: e.tensor_tensor(out=x1, in0=rot[0:R, 0], in1=rot[0:R, 1], op=ALU.subtract), reads=[("rot", ri)], writes=[qk])
                S.op("dve", lambda e: e.tensor_tensor(out=x2, in0=rot[0:R, 2], in1=rot[0:R, 3], op=ALU.add), reads=[("rot", ri)], writes=[qk])
                P_ROT.release(ri)
                yield
                bi = yield from P_QKB.acquire()
                qb = QKB[bi][:, 0:J]
                qbq = qb.rearrange("p j a c -> p j (a c)").rearrange("p j (a x) -> p j a x", x=192)[:, :, :, 0:64]
                S.op("pool", lambda e: e.tensor_copy(out=qbq[0:R], in_=qf[0:R, :, 0:128].rearrange("p j (a c) -> p j a c", a=2)), reads=[qk], writes=[("qkb", bi)])
                S.op("act", lambda e: e.copy(out=qb[0:R, :, 2, :], in_=qf[0:R, :, 128:256]), reads=[qk], writes=[("qkb", bi)])
                S.op("act", lambda e: e.copy(out=VAUG[0:R, t0_:t0_ + J, :, 0:64], in_=qf[0:R, :, 256:384].rearrange("p j (h c) -> p j h c", h=2)),
                     reads=[qk], writes=[kv(ti) for ti in tis])
                for j, ti in enumerate(tis):
                    need = (ti == 16) or (g == 2) or (g == 1 and ti % 4 == 3) or (g == 0 and ti == 15)
                    if not need:
                        continue
                    if ti == 16:
                        for n in range(4):
                            S.dma(lambda e, n=n, j=j: e.dma_start(out=o_kv_s[g][l, n, W - 8:W, :], in_=qf[8 * n:8 * n + 8, j, 128:384]),
                                  reads=[qk], slot=("kvo", qi))
                    else:
                        if g == 0:
                            dst = o_kv_p[0][l]
                        elif g == 1:
                            dst = o_kv_p[1][l].rearrange("(i r) c -> r i c", r=4)[ti // 4]
                        else:
                            dst = o_kv_p[2][l].rearrange("(i r) c -> r i c", r=16)[ti]
                        S.dma(lambda e, j=j, dst=dst: e.dma_start(out=dst, in_=qf[:, j, 128:384]), reads=[qk], slot=("kvo", qi))
                P_Q.release(qi)
                yield
                pt = yield from acq_bank((5, 4, 3, 2, 1, 0))
                pst = bank(pt)[:, 0:192 * J].bitcast(BF16).rearrange("p (j a b) -> p j a b", j=J, a=3)
                for j in range(J):
                    for a in range(3):
                        S.op("pe", lambda e, a=a, j=j: e.transpose(out=pst[:, j, a, 0:R], in_=qb[0:R, j, a, :], identity=IDB[0:R, 0:R]),
                             reads=[("qkb", bi), "idb"], writes=[("ps", pt)])
                P_QKB.release(bi)
                yield
                if J == 2:
                    qdst = QKT[:, :, 128 * t0_:128 * t0_ + 256].rearrange("p a (j r) -> p a j r", j=2)
                else:
                    qdst = QKT[:, :, 128 * t0_:128 * t0_ + R].unsqueeze(2)
                S.op("act", lambda e: e.copy(out=qdst[:, :, :, 0:R], in_=pst[:, :, :, 0:R].rearrange("p j a r -> p a j r")), reads=[("ps", pt)],
                     writes=[kq(a, ti) for a in range(3) for ti in tis])
                rel_ps(pt)

            return [(lambda tis=tis: qkv_tile(tis)) for tis in ([[2 * i, 2 * i + 1] for i in range(8)] + [[16]])]

        def att_items(g):
            sg = g % 2
            QKT, VAUG = QKT_S[sg], VAUG_S[sg]
            kq = lambda a, ti: ("qkt", sg, a, ti)
            kv = lambda ti: ("vaug", sg, ti)
            ncls = (1, 4, 16)[g]
            nb = 16 // ncls

            def att_unit(r, qb):
                tq = r * nb + qb
                kbs = [(1, qb)] if qb == 0 else [(0, qb - 1), (1, qb)]
                c0 = 256 if qb == 0 else 0
                bi = yield from P_PB.acquire()
                pi = yield from acq_bank()
                pS = bank(pi)
                for (ki, kb) in kbs:
                    tk = r * nb + kb
                    S.op("pe", lambda e, ki=ki, tk=tk: e.matmul(pS[:, ki * 256:ki * 256 + 256], lhsT=QKT[:, 2, 128 * tk:128 * tk + 128],
                                                              rhs=QKT[:, 0:2, 128 * tq:128 * tq + 128], start=True, stop=False),
                         reads=[kq(2, tk), kq(0, tq), kq(1, tq)], writes=[("ps", pi)])
                    S.op("pe", lambda e, ki=ki: e.matmul(pS[:, ki * 256:ki * 256 + 256], lhsT=IDB, rhs=MASKP[:, ki * 256:ki * 256 + 256], start=False, stop=True),
                         reads=["idb", "maskp"], writes=[("ps", pi)])
                yield
                P = PB[bi]
                S.op("act", lambda e: e.activation(out=P[:, c0:512], in_=pS[:, c0:512], func=AF.Exp, scale=0.125), reads=[("ps", pi)], writes=[("pb", bi)])
                rel_ps(pi)
                yield
                po = yield from acq_bank()
                pO = bank(po)
                for h in range(2):
                    hs = slice(64 * h, 64 * h + 64)
                    for idx, (ki, kb) in enumerate(kbs):
                        tk = r * nb + kb
                        S.op("pe", lambda e, hs=hs, ki=ki, h=h, idx=idx, tk=tk: e.matmul(
                            pO[hs, 0:128], lhsT=VAUG[:, tk, h, 0:64], rhs=P[:, (ki * 2 + h) * 128:(ki * 2 + h) * 128 + 128],
                            start=(idx == 0), stop=(idx == len(kbs) - 1)), reads=[kv(tk), ("pb", bi)], writes=[("ps", po)])
                nd = 2 * len(kbs)
                for idx, (ki, kb) in enumerate(kbs):
                    for h in range(2):
                        S.op("pe", lambda e, ki=ki, idx=idx, h=h: e.matmul(pO[:, 128:256], lhsT=HSEL[:, h, :], rhs=P[:, (ki * 2 + h) * 128:(ki * 2 + h) * 128 + 128],
                                                                        start=(idx == 0 and h == 0), stop=(2 * idx + h == nd - 1)), reads=["hsel", ("pb", bi)], writes=[("ps", po)])
                P_PB.release(bi)
                yield
                if g == 0:
                    qsel = slice(128 * qb, 128 * qb + 128)
                elif g == 1:
                    qsel = slice(512 * qb + r, 512 * qb + 512, 4)
                else:
                    qsel = slice(r, T, 16)
                pov = pO[:, 0:256].rearrange("p (a b) -> p a b", a=2)
                if g == 0:
                    S.op("act", lambda e: e.copy(out=ACC[:, :, qsel], in_=pov), reads=[("ps", po)], writes=["acc"])
                else:
                    S.op("dve", lambda e: e.tensor_tensor(out=ACC[:, :, qsel], in0=pov, in1=ACC[:, :, qsel], op=ALU.add), reads=[("ps", po), "acc"], writes=["acc"])
                rel_ps(po)

            return [(lambda r=r, qb=qb: att_unit(r, qb)) for r in range(ncls) for qb in range(nb)]

        def smp_items(g):
            W = WINS[g]
            sg = g % 2
            QKT, VAUG = QKT_S[sg], VAUG_S[sg]
            kq = lambda a, ti: ("qkt", sg, a, ti)
            for h in range(2):
                hs = slice(64 * h, 64 * h + 64)
                S.op("act", lambda e, h=h, hs=hs: e.copy(out=QBDG[g][hs, 32 * h:32 * h + 32], in_=QKT[hs, h, 2048:2080]), reads=[kq(h, 16)], writes=[("qbd", g)])
            pi = next_free_bank()
            S.op("pe", lambda e: e.matmul(bank(pi)[0:32, 0:64], lhsT=QKT[:, 2, 2048:2080], rhs=QBDG[g], start=True, stop=True), reads=[kq(2, 16), ("qbd", g)], writes=[("ps", pi)])
            S.op("act", lambda e: e.activation(out=PN[0:32], in_=bank(pi)[0:32, 0:64], func=AF.Exp, scale=0.125), reads=[("ps", pi)], writes=["pn"])
            S.op("dve", lambda e: e.tensor_tensor(out=PN[0:32], in0=PN[0:32], in1=MASKN[0:32, g, :], op=ALU.mult), reads=["pn", "maskn"], writes=["pn"])
            for h in range(2):
                S.op("pe", lambda e, h=h, st=first_pv[h]: e.matmul(psa[h], lhsT=PN[0:32, 32 * h:32 * h + 32], rhs=VAUG[0:32, 16, h, :], start=st, stop=False),
                     reads=["pn", ("vaug", sg, 16)], writes=[("ps", PSA[h])])
                first_pv[h] = False
            ntile = 8 if g == 2 else W // 128

            def smp_chunk(n, c0):
                nt = min(4, ntile - c0)
                ci = yield from P_CST.acquire()
                cst = CST[ci]
                if g == 2:
                    src = caches[2][l, n].rearrange("(i j) c -> i j c", j=16)[:, c0:c0 + nt, :]
                else:
                    src = caches[g][l, n, 128 * c0:128 * (c0 + nt), :].rearrange("(i p) c -> p i c", p=128)
                S.dma(lambda e: e.dma_start(out=cst[:, 0:nt, :], in_=src), writes=[("cst", ci)], slot=("cst", ci))
                yield
                ki = yield from P_KI.acquire()
                S.op("dve", lambda e: e.tensor_copy(out=KB_[ki][:, 0:nt, :], in_=cst[:, 0:nt, 0:128]), reads=[("cst", ci)], writes=[("kb", ki)])
                S.op("act", lambda e: e.copy(out=VS[ki][:, 0:nt, :, 0:64], in_=cst[:, 0:nt, 128:256].rearrange("p i (h c) -> p i h c", h=2)),
                     reads=[("cst", ci)], writes=[("vs", ki)])
                P_CST.release(ci)
                yield
                pt = yield from acq_bank()
                pst = bank(pt)[:, 0:256].bitcast(BF16).rearrange("p (a b) -> p a b", a=4)
                for i in range(nt):
                    S.op("pe", lambda e, i=i: e.transpose(out=pst[:, i, :], in_=KB_[ki][:, i, :], identity=IDB), reads=[("kb", ki), "idb"], writes=[("ps", pt)])
                yield
                S.op("act", lambda e: e.copy(out=KTS[ki][:, 0:128 * nt], in_=bank(pt)[:, 0:64 * nt].bitcast(BF16)), reads=[("ps", pt)], writes=[("kts", ki)])
                rel_ps(pt)
                yield
                pq_ = yield from acq_bank()
                for i in range(nt):
                    S.op("pe", lambda e, i=i: e.matmul(bank(pq_)[:, 64 * i:64 * i + 64], lhsT=KTS[ki][:, 128 * i:128 * i + 128], rhs=QBDG[g], start=True, stop=True),
                         reads=[("kts", ki), ("qbd", g)], writes=[("ps", pq_)])
                yield
                psb = PSB[ki]
                S.op("act", lambda e: e.activation(out=psb[:, 0:nt, :], in_=bank(pq_)[:, 0:64 * nt].rearrange("p (i c) -> p i c", i=nt), func=AF.Exp, scale=0.125),
                     reads=[("ps", pq_)], writes=[("psb", ki)])
                rel_ps(pq_)
                yield
                if g == 2:
                    S.op("dve", lambda e: e.tensor_tensor(out=psb[:, 0:nt, :], in0=psb[:, 0:nt, :], in1=MASK2[:, n, c0:c0 + nt, :], op=ALU.mult),
                         reads=[("psb", ki), "mask2"], writes=[("psb", ki)])
                else:
                    i0 = 0
                    if c0 == 0:
                        S.op("dve", lambda e: e.tensor_tensor(out=psb[:, 0, :], in0=psb[:, 0, :], in1=MASKS[:, g, n, 0, :], op=ALU.mult), reads=[("psb", ki), "masks"], writes=[("psb", ki)])
                        i0 = 1
                    if nt > i0:
                        S.op("dve", lambda e, i0=i0: e.tensor_tensor(
                            out=psb[:, i0:nt, :], in0=psb[:, i0:nt, :], in1=MASKS[:, g, n, 1:2, :].to_broadcast([128, nt - i0, 64]), op=ALU.mult),
                            reads=[("psb", ki), "masks"], writes=[("psb", ki)])
                yield
                last_chunk = (g == 2 and n == 3 and c0 + nt == ntile)
                for i in range(nt):
                    for h in range(2):
                        S.op("pe", lambda e, i=i, h=h, sp_=(last_chunk and i == nt - 1): e.matmul(
                            psa[h], lhsT=psb[:, i, 32 * h:32 * h + 32], rhs=VS[ki][:, i, h, :], start=False, stop=sp_),
                            reads=[("psb", ki), ("vs", ki)], writes=[("ps", PSA[h])])
                P_KI.release(ki)

            return [(lambda n=n, c0=c0: smp_chunk(n, c0)) for n in range(4) for c0 in range(0, ntile, 4)]

        def next_free_bank():
            for i in (5, 4, 3, 2, 1, 0):
                if i not in ps_held:
                    return i
            raise AssertionError("no free PSUM bank")

        def merge(*lists, rate=None):
            out = []
            tot = max(len(x) for x in lists)
            pos = [0] * len(lists)
            rate = rate or [1] * len(lists)
            for step in range(tot):
                for li, x in enumerate(lists):
                    want = min(len(x), (step + 1) * len(x) * rate[li] // tot)
                    while pos[li] < want:
                        out.append(x[pos[li]])
                        pos[li] += 1
            return out

        pipeline(qkv_items(0), depth=8)
        if stop == "a2b_qkv0":
            return True
        for g in range(3):
            att = att_items(g)
            smp = smp_items(g)
            nxtq = qkv_items(g + 1) if g < 2 else []
            pipeline(merge(att, smp, nxtq, rate=[1, 1, 2]) if nxtq else merge(att, smp), depth=10)
            assert not ps_held, ps_held
            if stop == "a2b_s%d" % g:
                return True

        for b in range(4):
            cs_ = slice(512 * b, 512 * b + 512)
            S.op("dve", lambda e, cs_=cs_: e.reciprocal(out=ACC[:, 1, cs_], in_=ACC[:, 1, cs_]), reads=["acc"], writes=["acc"])
            S.op("dve", lambda e, cs_=cs_: e.tensor_tensor(out=MIXT[:, 2, cs_], in0=ACC[:, 0, cs_], in1=ACC[:, 1, cs_], op=ALU.mult), reads=["acc"], writes=[("mix", 2, b)])
        for h in range(2):
            S.op("dve", lambda e, h=h: e.tensor_copy(out=RD[0:32, h:h + 1], in_=psa[h][:, 64:65]), reads=[("ps", PSA[h])], writes=["rd"])
        S.op("dve", lambda e: e.reciprocal(out=RD[0:32], in_=RD[0:32]), reads=["rd"], writes=["rd"])
        for h in range(2):
            S.op("dve", lambda e, h=h: e.tensor_scalar(out=YSB[0:32, 64 * h:64 * h + 64], in0=psa[h][:, 0:64], scalar1=RD[0:32, h:h + 1], scalar2=None, op0=ALU.mult),
                 reads=[("ps", PSA[h]), "rd"], writes=["ysb"])
        pt = next_ps()
        pst = bank(pt)[:, 0:16].bitcast(BF16)
        S.op("pe", lambda e, pst=pst: e.transpose(out=pst, in_=YSB[0:32, :], identity=IDB[0:32, 0:32]), reads=["ysb", "idb"], writes=[("ps", pt)])
        S.op("act", lambda e, pst=pst: e.copy(out=MIXT[:, 2, 2048:2080], in_=pst), reads=[("ps", pt)], writes=[("mix", 2, 4)])

    def phase_c(l):
        def pre(t):
            R = tile_rows(t)
            if l == 0:
                src = xp[128 * t:128 * t + 128, :] if t < 16 else xs
            else:
                src = xpark[t, 0:R, :]
            S.dma(lambda e, t=t, R=R, src=src: e.dma_start(out=X[0:R, t, :], in_=src), reads=([("park", t)] if l > 0 else []), writes=[("x", t)], slot=("xl", t))
            b = t // 4
            for hf in range(2):
                pi = next_ps()
                for c in range(6):
                    S.op("pe", lambda e, c=c, pi=pi, R=R, t=t, hf=hf: e.matmul(bank(pi)[0:R, :], lhsT=MIXT[:, c, tile_cols(t)], rhs=WO[:, c, 512 * hf:512 * hf + 512],
                                                                               start=(c == 0), stop=(c == 5)), reads=["wo", ("mix", c, b)], writes=[("ps", pi)])
                S.op("dve", lambda e, pi=pi, R=R, t=t, hf=hf: e.tensor_tensor(out=X[0:R, t, 512 * hf:512 * hf + 512], in0=bank(pi)[0:R, :], in1=X[0:R, t, 512 * hf:512 * hf + 512], op=ALU.add),
                     reads=[("ps", pi), ("x", t)], writes=[("x", t)])
        norm_and_transpose(norm2_g[l], pre=pre)

    def phase_d(l, prefetch_next):
        last = (l == DEPTH - 1)
        if not last:
            S.op("dve", lambda e: e.memset(SS, 0.0), writes=["ss"])
        for fb in range(4):
            i = fb % 2

            def up(b, fb=fb, i=i):
                L = blk_len(b)
                at = AT[b % 2]
                for fc in range(8):
                    pi = next_ps()
                    for d in range(8):
                        S.op("pe", lambda e, d=d, fc=fc, pi=pi, L=L: e.matmul(bank(pi)[:, 0:L], lhsT=WU[i][:, d, 128 * fc:128 * fc + 128], rhs=HT[:, d, blk_cols(b)],
                                                                          start=(d == 0), stop=(d == 7)),
                             reads=[("wu", i)] + [("ht", t) for t in (range(4 * b, 4 * b + 4) if b < 4 else [16])], writes=[("ps", pi)])
                    ri = fc % 2
                    S.op("act", lambda e, pi=pi, L=L, ri=ri: e.activation(out=RL[ri][:, 0:L], in_=bank(pi)[:, 0:L], func=AF.Relu), reads=[("ps", pi)], writes=[("rl", ri)])
                    S.op("dve", lambda e, pi=pi, L=L, ri=ri, at=at, fc=fc: e.tensor_tensor(out=at[:, fc, 0:L], in0=bank(pi)[:, 0:L], in1=RL[ri][:, 0:L], op=ALU.mult),
                         reads=[("ps", pi), ("rl", ri)], writes=[("at", b % 2)])

            def down(b, fb=fb, i=i):
                at = AT[b % 2]
                tiles = range(4 * b, 4 * b + 4) if b < 4 else [16]
                for t in tiles:
                    R = tile_rows(t)
                    lo = 128 * (t % 4) if b < 4 else 0
                    for hf in range(2):
                        pi = next_ps()
                        for fc in range(8):
                            S.op("pe", lambda e, fc=fc, pi=pi, R=R, lo=lo, hf=hf, at=at: e.matmul(bank(pi)[0:R, :], lhsT=at[:, fc, lo:lo + R], rhs=WD[i][:, fc, 512 * hf:512 * hf + 512],
                                                                                          start=(fc == 0), stop=(fc == 7)), reads=[("wd", i), ("at", b % 2)], writes=[("ps", pi)])
                        S.op("dve", lambda e, pi=pi, R=R, t=t, hf=hf: e.tensor_tensor(out=X[0:R, t, 512 * hf:512 * hf + 512], in0=bank(pi)[0:R, :], in1=X[0:R, t, 512 * hf:512 * hf + 512], op=ALU.add),
                             reads=[("ps", pi), ("x", t)], writes=[("x", t)])
                    if fb == 3:
                        if last:
                            dst = yp[128 * t:128 * t + 128, :] if t < 16 else ys
                            S.dma(lambda e, t=t, R=R, dst=dst: e.dma_start(out=dst, in_=X[0:R, t, :]), reads=[("x", t)], slot=("xo", t))
                        else:
                            S.dma(lambda e, t=t, R=R: e.dma_start(out=xpark[t, 0:R, :], in_=X[0:R, t, :]), reads=[("x", t)], writes=[("park", t)], slot=("xo", t))
                            S.op("act", lambda e, t=t, R=R: e.activation(out=HB[t % 2][0:R, :], in_=X[0:R, t, :], func=AF.Square, accum_out=SS[0:R, t:t + 1]),
                                 reads=[("x", t)], writes=[("hb", t % 2), "ss"])

            up(0)
            for b in range(5):
                if b + 1 < 5:
                    up(b + 1)
                down(b)
            if fb == 1 and not last:
                load_gbc(norm1_g[l + 1])
                load_smalls(l + 1)
            if fb + 2 < 4:
                load_ffn(l, fb + 2)
            elif prefetch_next is not None:
                prefetch_next(fb)

    def fin():
        S.emit(es)
        es.close()
        return nc

    if stop == "setup":
        return fin()
    load_wa(0)
    load_wq(0)
    load_wo(0)
    for l in range(DEPTH):
        if l == 0:
            load_smalls(l)
        norm_and_transpose(norm1_g[l], have_ss=(l > 0))
        if stop == "a1":
            return fin()
        phase_a2a(l)
        if stop == "a2a":
            return fin()
        if phase_a2b(l) or stop == "a2b":
            return fin()
        load_ffn(l, 0)
        phase_c(l)
        if stop == "c":
            return fin()
        load_ffn(l, 1)

        def prefetch_next(fb, l=l):
            if l + 1 < DEPTH:
                if fb == 2:
                    load_wa(l + 1)
                if fb == 3:
                    load_wq(l + 1)
                    load_wo(l + 1)
        phase_d(l, prefetch_next)
        if stop == "d":
            return fin()
    S.emit(es)
    es.close()
    return nc


_NC_CACHE = {}


def kernel(x_prompt, x_sample, state_pool, state_conv, cache_kv_w128, cache_kv_w512, cache_kv_w2048,
           norm1_g, w_in, q_norm_g, k_norm_g, pool_w, pool_scale, conv_w, w_out, norm2_g, w_up, w_down):
    f = lambda a: np.ascontiguousarray(np.asarray(a, dtype=np.float32))
    consts = make_consts()
    shared = {
        "norm1_g": f(norm1_g), "w_in": f(w_in), "q_norm_g": f(q_norm_g), "k_norm_g": f(k_norm_g),
        "pool_w": f(pool_w), "pool_scale": f(pool_scale), "conv_w": f(conv_w), "w_out": f(w_out),
        "norm2_g": f(norm2_g), "w_up": f(w_up), "w_down": f(w_down),
    }
    shared.update(consts)
    x_prompt = f(x_prompt); x_sample = f(x_sample)
    state_pool = f(state_pool); state_conv = f(state_conv)
    c128 = f(cache_kv_w128); c512 = f(cache_kv_w512); c2048 = f(cache_kv_w2048)
    in_maps = []
    for c in range(NCORES):
        s = slice(4 * c, 4 * c + 4)
        m = dict(shared)
        m["xp"] = x_prompt[c]
        m["xs"] = np.ascontiguousarray(x_sample[s].reshape(32, D))
        m["spool"] = np.ascontiguousarray(state_pool[:, s])
        m["sconv"] = np.ascontiguousarray(state_conv[:, s])
        m["c128"] = np.ascontiguousarray(c128[:, s].reshape(DEPTH, 4, 128, 256))
        m["c512"] = np.ascontiguousarray(c512[:, s].reshape(DEPTH, 4, 512, 256))
        m["c2048"] = np.ascontiguousarray(c2048[:, s].reshape(DEPTH, 4, 2048, 256))
        in_maps.append(m)
    if "nc" not in _NC_CACHE:
        _NC_CACHE["nc"] = build_program()
    nc = _NC_CACHE["nc"]
    res = run_bass_kernel_spmd(nc, in_maps, core_ids=list(range(NCORES)))
    R = res.results
    cat = lambda k, ax: np.concatenate([np.asarray(r[k]) for r in R], axis=ax)
    y_prompt = np.stack([np.asarray(r["yp"]) for r in R], 0)
    y_sample = np.concatenate([np.asarray(r["ys"]).reshape(4, 8, D) for r in R], 0)
    pool_p = np.stack([np.asarray(r["pool_p"]) for r in R], 1)
    conv_p = np.stack([np.asarray(r["conv_p"]) for r in R], 1)
    kvp = [np.stack([np.asarray(r[k]) for r in R], 1).reshape(DEPTH, NCORES, w, 2, 2, 64)
           for k, w in (("kv128_p", 128), ("kv512_p", 512), ("kv2048_p", 2048))]
    pool_s = cat("pool_s", 1)
    conv_s = cat("conv_s", 1)
    kvs = [cat(k, 1).reshape(DEPTH, 32, w, 2, 2, 64) for k, w in (("kv128_s", 128), ("kv512_s", 512), ("kv2048_s", 2048))]
    outs = (y_prompt, y_sample, pool_p, conv_p, kvp[0], kvp[1], kvp[2], pool_s, conv_s, kvs[0], kvs[1], kvs[2])
    return tuple(np.ascontiguousarray(o, dtype=np.float32) for o in outs)
```
